# Optimizing a Trainium2 kernel written in Bass

```python
import jax, jax.numpy as jnp
from jax import lax
import numpy as np

D_MODEL = 2048
BATCH = 8
SEQ = 4096
DEPTH = 4

N_MIXERS = 4
N_SB = len(range(0, DEPTH, N_MIXERS))
N_GDN = len(range(1, DEPTH, N_MIXERS))
N_DSW = len(range(2, DEPTH, N_MIXERS))
N_LRU = len(range(3, DEPTH, N_MIXERS))

HEAD_DIM = 128
NORM_EPS = 1e-6
Q_BLOCK = 128

SB_HEADS = D_MODEL // HEAD_DIM

GDN_K_HEADS = D_MODEL // HEAD_DIM
GDN_V_HEADS = 2 * GDN_K_HEADS
GDN_KEY_DIM = GDN_K_HEADS * HEAD_DIM
GDN_VAL_DIM = GDN_V_HEADS * HEAD_DIM
GDN_CONV = 4
GDN_CHUNK = 64
GDN_IN = 2 * GDN_KEY_DIM + 2 * GDN_VAL_DIM + 2 * GDN_V_HEADS

DSW_GROUPS = ((128, 1), (512, 4), (2048, 16))
DSW_HEADS_PER_GROUP = 6
DSW_HEADS = len(DSW_GROUPS) * DSW_HEADS_PER_GROUP
DSW_BLOCK = 128
ROPE_DIM = HEAD_DIM // 4
ROPE_THETA = 500000.0

LRU_WIDTH = D_MODEL
LRU_BLOCK_DIM = 256
LRU_BLOCKS = LRU_WIDTH // LRU_BLOCK_DIM
LRU_CONV = 4
LRU_C = 8.0

FFN_HIDDEN = ((8 * D_MODEL + 3 * 256 - 1) // (3 * 256)) * 256

kernel_name = 'hybrid_sb_gdn_dilated_rglru_trunk'


def rms_norm(x, g):
    xf = x.astype(jnp.float32)
    y = xf * lax.rsqrt(jnp.mean(xf * xf, axis=-1, keepdims=True) + NORM_EPS)
    return (y * g.astype(jnp.float32)).astype(x.dtype)


def l2_norm(x):
    xf = x.astype(jnp.float32)
    return xf * lax.rsqrt(jnp.sum(xf * xf, axis=-1, keepdims=True) + NORM_EPS)


def causal_dwconv(x, w):
    K, C = w.shape
    return lax.conv_general_dilated(
        x, w[:, None, :], window_strides=(1,), padding=((K - 1, 0),),
        dimension_numbers=('NWC', 'WIO', 'NWC'), feature_group_count=C)


def partial_rope(x, positions):
    half = ROPE_DIM // 2
    inv_freq = ROPE_THETA ** (-jnp.arange(half, dtype=jnp.float32) / half)
    ang = positions.astype(jnp.float32)[..., None] * inv_freq
    cos = jnp.cos(ang)[:, :, None, :]
    sin = jnp.sin(ang)[:, :, None, :]
    xr = x[..., :ROPE_DIM].astype(jnp.float32)
    x1, x2 = xr[..., :half], xr[..., half:]
    rot = jnp.concatenate([x1 * cos - x2 * sin, x2 * cos + x1 * sin], axis=-1)
    return jnp.concatenate([rot.astype(x.dtype), x[..., ROPE_DIM:]], axis=-1)


def swiglu(h, w_gu, w_down):
    gate, up = jnp.split(h @ w_gu, 2, axis=-1)
    return (jax.nn.silu(gate) * up) @ w_down


def stick_breaking_mixer(h, w_in, q_norm, k_norm, w_out):
    B, T, _ = h.shape
    H, dh = SB_HEADS, HEAD_DIM
    q, k, v = jnp.split(h @ w_in, 3, axis=-1)
    q = rms_norm(q.reshape(B, T, H, dh), q_norm)
    k = rms_norm(k.reshape(B, T, H, dh), k_norm)
    v = v.reshape(B, T, H, dh)
    nb = T // Q_BLOCK
    qb = q.reshape(B, nb, Q_BLOCK, H, dh).transpose(1, 0, 3, 2, 4)
    key_pos = jnp.arange(T)
    scale = HEAD_DIM ** -0.5

    def block(args):
        q_blk, n = args
        z = jnp.einsum('bhqd,bshd->bhqs', q_blk, k, preferred_element_type=jnp.float32) * scale
        q_pos = n * Q_BLOCK + jnp.arange(Q_BLOCK)
        past = key_pos[None, :] < q_pos[:, None]
        neg_log_1m_beta = jnp.where(past, jax.nn.softplus(z), 0.0)
        between = lax.cumsum(neg_log_1m_beta, axis=3, reverse=True) - neg_log_1m_beta
        log_w = jax.nn.log_sigmoid(z) - between
        w = jnp.where(past, jnp.exp(log_w), 0.0)
        return jnp.einsum('bhqs,bshd->bqhd', w.astype(v.dtype), v)

    o = lax.map(block, (qb, jnp.arange(nb)))
    o = o.transpose(1, 0, 2, 3, 4).reshape(B, T, H * dh)
    return o @ w_out


def chunk_gated_delta_rule(q, k, v, g, beta):
    B, T, H, dk = k.shape
    dv = v.shape[-1]
    C = GDN_CHUNK
    N = T // C
    f32 = jnp.float32

    def chunks(x):
        return x.astype(f32).reshape(B, N, C, H, -1).transpose(1, 0, 3, 2, 4)

    q, k, v = chunks(q), chunks(k), chunks(v)
    beta = chunks(beta[..., None])[..., 0]
    g = jnp.cumsum(chunks(g[..., None])[..., 0], axis=-1)
    causal = jnp.tril(jnp.ones((C, C), bool))
    strict = jnp.tril(jnp.ones((C, C), bool), -1)
    diff = g[..., :, None] - g[..., None, :]
    decay = jnp.where(causal, jnp.exp(jnp.where(causal, diff, 0.0)), 0.0)
    kb = k * beta[..., None]
    vb = v * beta[..., None]
    a_mat = jnp.where(strict, jnp.einsum('nbhcd,nbhsd->nbhcs', kb, k) * decay, 0.0)
    eye = jnp.eye(C, dtype=f32)
    t_mat = lax.linalg.triangular_solve(a_mat + eye, jnp.broadcast_to(eye, a_mat.shape),
                                        left_side=True, lower=True, unit_diagonal=True)
    u = t_mat @ vb
    w = t_mat @ (kb * jnp.exp(g)[..., None])

    def step(state, inp):
        q_c, k_c, u_c, w_c, g_c, decay_c = inp
        v_new = u_c - w_c @ state
        attn = jnp.einsum('bhcd,bhsd->bhcs', q_c, k_c) * decay_c
        out = (q_c * jnp.exp(g_c)[..., None]) @ state + attn @ v_new
        g_last = g_c[..., -1:]
        k_dec = k_c * jnp.exp(g_last - g_c)[..., None]
        state = state * jnp.exp(g_last)[..., None] + jnp.einsum('bhcd,bhce->bhde', k_dec, v_new)
        return state, out

    state0 = jnp.zeros((B, H, dk, dv), f32)
    _, out = lax.scan(step, state0, (q, k, u, w, g, decay))
    return out.transpose(1, 0, 3, 2, 4).reshape(B, T, H, dv)


def gated_deltanet_mixer(h, w_in, conv_w, a_log, dt_bias, o_norm, w_out):
    B, T, _ = h.shape
    Kd, Vd, Hk, Hv, dh = GDN_KEY_DIM, GDN_VAL_DIM, GDN_K_HEADS, GDN_V_HEADS, HEAD_DIM
    proj = h @ w_in
    qkv, z, b, a = jnp.split(proj, [2 * Kd + Vd, 2 * Kd + 2 * Vd, 2 * Kd + 2 * Vd + Hv], axis=-1)
    qkv = jax.nn.silu(causal_dwconv(qkv, conv_w))
    q, k, v = jnp.split(qkv, [Kd, 2 * Kd], axis=-1)
    rep = Hv // Hk
    q = jnp.repeat(q.reshape(B, T, Hk, dh), rep, axis=2)
    k = jnp.repeat(k.reshape(B, T, Hk, dh), rep, axis=2)
    v = v.reshape(B, T, Hv, dh)
    q = l2_norm(q) * dh ** -0.5
    k = l2_norm(k)
    beta = jax.nn.sigmoid(b.astype(jnp.float32))
    g = -jnp.exp(a_log.astype(jnp.float32)) * jax.nn.softplus(
        a.astype(jnp.float32) + dt_bias.astype(jnp.float32))
    o = chunk_gated_delta_rule(q, k, v, g, beta)
    o = rms_norm(o, o_norm) * jax.nn.silu(z.reshape(B, T, Hv, dh).astype(jnp.float32))
    return o.astype(h.dtype).reshape(B, T, Vd) @ w_out


def dilated_band_attention(q, k, v, window, dilation):
    B, T, H, dh = q.shape
    blk = DSW_BLOCK
    span = window // dilation
    unit = dilation * blk
    t_pad = -(-T // unit) * unit
    sub_len = t_pad // dilation
    nb = sub_len // blk

    def gather(x):
        x = jnp.pad(x, ((0, 0), (0, t_pad - T), (0, 0), (0, 0)))
        x = x.reshape(B, sub_len, dilation, H, dh).transpose(0, 2, 1, 3, 4)
        return x.reshape(B, dilation, nb, blk, H, dh)

    def with_previous_block(x):
        prev = jnp.pad(x[:, :, :-1], ((0, 0), (0, 0), (1, 0), (0, 0), (0, 0), (0, 0)))
        return jnp.concatenate([prev, x], axis=3)

    qg = gather(q)
    kw = with_previous_block(gather(k))
    vw = with_previous_block(gather(v))
    s = jnp.einsum('brnqhd,brnkhd->brnhqk', qg, kw, preferred_element_type=jnp.float32) * dh ** -0.5
    qi = jnp.arange(blk)[:, None]
    kj = jnp.arange(2 * blk)[None, :]
    steps = blk + qi - kj
    band = (steps >= 0) & (steps <= span)
    key_sub = jnp.arange(nb)[:, None, None] * blk + (kj - blk)[None]
    mask = band[None] & (key_sub >= 0)
    s = jnp.where(mask[:, None], s, -jnp.inf)
    m = jnp.max(s, axis=-1, keepdims=True)
    p = jnp.exp(s - m)
    den = jnp.sum(p, axis=-1, keepdims=True)
    o = jnp.einsum('brnhqk,brnkhd->brnqhd', (p / den).astype(v.dtype), vw)
    lse = (m + jnp.log(den))[..., 0]
    o = o.reshape(B, dilation, sub_len, H, dh).transpose(0, 2, 1, 3, 4).reshape(B, t_pad, H, dh)[:, :T]
    lse = lse.transpose(0, 1, 2, 4, 3).reshape(B, dilation, sub_len, H)
    lse = lse.transpose(0, 2, 1, 3).reshape(B, t_pad, H)[:, :T]
    return o, lse


def dilated_window_mixer(h, positions, w_in, q_norm, k_norm, w_out):
    B, T, _ = h.shape
    G, Hg, dh = len(DSW_GROUPS), DSW_HEADS_PER_GROUP, HEAD_DIM
    q, k, v = jnp.split(h @ w_in, 3, axis=-1)
    q = partial_rope(rms_norm(q.reshape(B, T, G * Hg, dh), q_norm), positions)
    k = partial_rope(rms_norm(k.reshape(B, T, G * Hg, dh), k_norm), positions)
    v = v.reshape(B, T, G * Hg, dh)
    outs, lses = [], []
    for gi, (window, dilation) in enumerate(DSW_GROUPS):
        sl = slice(gi * Hg, (gi + 1) * Hg)
        o_g, lse_g = dilated_band_attention(q[:, :, sl], k[:, :, sl], v[:, :, sl], window, dilation)
        outs.append(o_g)
        lses.append(lse_g)
    o = jnp.stack(outs, axis=2)
    alpha = jax.nn.softmax(jnp.stack(lses, axis=2), axis=2)
    o = (o * alpha[..., None].astype(o.dtype)).reshape(B, T, G * Hg * dh)
    return o @ w_out


def rglru_mixer(h, w_in, conv_w, conv_b, w_a, b_a, w_x, b_x, lam, w_out):
    B, T, _ = h.shape
    gate, xr = jnp.split(h @ w_in, 2, axis=-1)
    gate = jax.nn.gelu(gate, approximate=True)
    xr = causal_dwconv(xr, conv_w) + conv_b
    xb = xr.reshape(B, T, LRU_BLOCKS, LRU_BLOCK_DIM)
    r = jax.nn.sigmoid(jnp.einsum('btni,nij->btnj', xb, w_a) + b_a).reshape(B, T, LRU_WIDTH)
    i = jax.nn.sigmoid(jnp.einsum('btni,nij->btnj', xb, w_x) + b_x).reshape(B, T, LRU_WIDTH)
    log_a = -LRU_C * r.astype(jnp.float32) * jax.nn.softplus(-lam.astype(jnp.float32))
    a = jnp.exp(log_a)
    u = jnp.sqrt(-jnp.expm1(2.0 * log_a)) * (i * xr).astype(jnp.float32)

    def combine(left, right):
        a_l, b_l = left
        a_r, b_r = right
        return a_l * a_r, a_r * b_l + b_r

    _, hs = lax.associative_scan(combine, (a, u), axis=1)
    return (hs.astype(h.dtype) * gate) @ w_out


def setup_inputs(seed: int = 0) -> dict:
    key = jax.random.key(seed)
    ks = iter(jax.random.split(key, 48))
    f32 = jnp.float32

    def dense(shape, fan_in):
        return jax.random.normal(next(ks), shape, f32) * fan_in ** -0.5

    def gain(shape):
        return 1.0 + 0.02 * jax.random.normal(next(ks), shape, f32)

    def small(shape):
        return 0.01 * jax.random.normal(next(ks), shape, f32)

    x = jax.random.normal(next(ks), (BATCH, SEQ, D_MODEL), f32)
    offsets = jax.random.randint(next(ks), (BATCH, 1), 0, SEQ, jnp.int32)
    positions = offsets + jnp.arange(SEQ, dtype=jnp.int32)[None, :]

    dt = jnp.exp(jax.random.uniform(next(ks), (N_GDN, GDN_V_HEADS), f32, np.log(1e-3), np.log(1e-1)))
    gdn_dt_bias = dt + jnp.log(-jnp.expm1(-dt))
    gdn_a_log = jnp.log(jax.random.uniform(next(ks), (N_GDN, GDN_V_HEADS), f32, 1.0, 16.0))
    a_c = jax.random.uniform(next(ks), (N_LRU, LRU_WIDTH), f32, 0.9, 0.999)
    s_lam = a_c ** (1.0 / LRU_C)
    lru_lambda = jnp.log(s_lam) - jnp.log1p(-s_lam)

    return {
        'x': x,
        'positions': positions,
        'mix_norm': gain((DEPTH, D_MODEL)),
        'ffn_norm': gain((DEPTH, D_MODEL)),
        'ffn_w_gu': dense((DEPTH, D_MODEL, 2 * FFN_HIDDEN), D_MODEL),
        'ffn_w_down': dense((DEPTH, FFN_HIDDEN, D_MODEL), FFN_HIDDEN),
        'sb_w_in': dense((N_SB, D_MODEL, 3 * SB_HEADS * HEAD_DIM), D_MODEL),
        'sb_q_norm': gain((N_SB, HEAD_DIM)),
        'sb_k_norm': gain((N_SB, HEAD_DIM)),
        'sb_w_out': dense((N_SB, SB_HEADS * HEAD_DIM, D_MODEL), SB_HEADS * HEAD_DIM),
        'gdn_w_in': dense((N_GDN, D_MODEL, GDN_IN), D_MODEL),
        'gdn_conv_w': dense((N_GDN, GDN_CONV, 2 * GDN_KEY_DIM + GDN_VAL_DIM), GDN_CONV),
        'gdn_a_log': gdn_a_log,
        'gdn_dt_bias': gdn_dt_bias,
        'gdn_o_norm': gain((N_GDN, HEAD_DIM)),
        'gdn_w_out': dense((N_GDN, GDN_VAL_DIM, D_MODEL), GDN_VAL_DIM),
        'dsw_w_in': dense((N_DSW, D_MODEL, 3 * DSW_HEADS * HEAD_DIM), D_MODEL),
        'dsw_q_norm': gain((N_DSW, HEAD_DIM)),
        'dsw_k_norm': gain((N_DSW, HEAD_DIM)),
        'dsw_w_out': dense((N_DSW, DSW_HEADS * HEAD_DIM, D_MODEL), DSW_HEADS * HEAD_DIM),
        'lru_w_in': dense((N_LRU, D_MODEL, 2 * LRU_WIDTH), D_MODEL),
        'lru_conv_w': dense((N_LRU, LRU_CONV, LRU_WIDTH), LRU_CONV),
        'lru_conv_b': small((N_LRU, LRU_WIDTH)),
        'lru_w_a': dense((N_LRU, LRU_BLOCKS, LRU_BLOCK_DIM, LRU_BLOCK_DIM), LRU_BLOCK_DIM),
        'lru_b_a': small((N_LRU, LRU_BLOCKS, LRU_BLOCK_DIM)),
        'lru_w_x': dense((N_LRU, LRU_BLOCKS, LRU_BLOCK_DIM, LRU_BLOCK_DIM), LRU_BLOCK_DIM),
        'lru_b_x': small((N_LRU, LRU_BLOCKS, LRU_BLOCK_DIM)),
        'lru_lambda': lru_lambda,
        'lru_w_out': dense((N_LRU, LRU_WIDTH, D_MODEL), LRU_WIDTH),
    }


def reference(x, positions, mix_norm, ffn_norm, ffn_w_gu, ffn_w_down,
              sb_w_in, sb_q_norm, sb_k_norm, sb_w_out,
              gdn_w_in, gdn_conv_w, gdn_a_log, gdn_dt_bias, gdn_o_norm, gdn_w_out,
              dsw_w_in, dsw_q_norm, dsw_k_norm, dsw_w_out,
              lru_w_in, lru_conv_w, lru_conv_b, lru_w_a, lru_b_a, lru_w_x, lru_b_x,
              lru_lambda, lru_w_out):
    for i in range(DEPTH):
        kind, j = i % N_MIXERS, i // N_MIXERS
        h = rms_norm(x, mix_norm[i])
        if kind == 0:
            y = stick_breaking_mixer(h, sb_w_in[j], sb_q_norm[j], sb_k_norm[j], sb_w_out[j])
        elif kind == 1:
            y = gated_deltanet_mixer(h, gdn_w_in[j], gdn_conv_w[j], gdn_a_log[j], gdn_dt_bias[j],
                                     gdn_o_norm[j], gdn_w_out[j])
        elif kind == 2:
            y = dilated_window_mixer(h, positions, dsw_w_in[j], dsw_q_norm[j], dsw_k_norm[j], dsw_w_out[j])
        else:
            y = rglru_mixer(h, lru_w_in[j], lru_conv_w[j], lru_conv_b[j], lru_w_a[j], lru_b_a[j],
                            lru_w_x[j], lru_b_x[j], lru_lambda[j], lru_w_out[j])
        x = x + y.astype(x.dtype)
        x = x + swiglu(rms_norm(x, ffn_norm[i]), ffn_w_gu[i], ffn_w_down[i]).astype(x.dtype)
    return x
```

```python
import numpy as np
from contextlib import ExitStack
import concourse.bass as bass
import concourse.mybir as mybir
from concourse.bass_utils import run_bass_kernel_spmd

F32 = mybir.dt.float32
BF16 = mybir.dt.bfloat16
I32 = mybir.dt.int32
AF = mybir.ActivationFunctionType
ALU = mybir.AluOpType

DBG_STOP = 9
DBG_CORE = 9
DBG_STEP = 9
SAME_ENG_RAW = True
DBG_GE = "act"
DBG_SKIP_FFN = False
DBG_OUT = False
DBG_K = 9
DBG_K2 = 0
D = 2048
KC = D // 128
FFN_H = 5632
EPS = 1e-6


class View:
    __slots__ = ("buf", "ap")

    def __init__(self, buf, ap):
        self.buf = buf
        self.ap = ap

    def __getitem__(self, idx):
        return View(self.buf, self.ap[idx])

    def rr(self, pat, **kw):
        return View(self.buf, self.ap.rearrange(pat, **kw))

    def bc(self, dt):
        return View(self.buf, self.ap.bitcast(dt))


class Buf:
    def __init__(self, t, name):
        self.t = t
        self.name = name
        self.writers = []
        self.readers = []
        self.sem = None
        self.semname = None
        self.psum = False

    def __getitem__(self, idx):
        return View(self, self.t[idx])

    def rr(self, pat, **kw):
        return View(self, self.t[:].rearrange(pat, **kw))


class Op:
    __slots__ = ("eng", "fn", "waits", "sem", "val", "inc", "is_dma")


def _ap(v):
    return v.ap if isinstance(v, View) else v


class Prog:
    CE = ("pe", "act", "dve", "pool")
    ALLE = ("pe", "act", "dve", "pool", "sp")

    def __init__(self, nc, es):
        self.nc = nc
        self.es = es
        self.esem = {e: es.enter_context(nc.semaphore("cs_" + e)) for e in self.CE}
        self.ecnt = {e: 0 for e in self.CE}
        self.semcnt = {}
        self.semobj = {}
        self.free_dma_sems = []
        self.n_dma_sems = 0
        self.known = {e: {} for e in self.ALLE}
        self.ops = {e: [] for e in self.ALLE}
        self.phase_sems = set()
        self.pes = None
        self.nops = 0
        self.uid = 0

    def begin_phase(self):
        self.pes = ExitStack()
        self.pes.__enter__()
        self.phase_bufs = []

    def sb(self, shape, dt, name=None):
        self.uid += 1
        name = (name or "b") + "_%d" % self.uid
        t = self.pes.enter_context(self.nc.sbuf_tensor(name, list(shape), dt))
        b = Buf(t, name)
        self.phase_bufs.append(b)
        return b

    def ps(self, shape, dt=F32, name=None):
        self.uid += 1
        name = (name or "p") + "_%d" % self.uid
        t = self.pes.enter_context(self.nc.psum_tensor(name, list(shape), dt))
        b = Buf(t, name)
        b.psum = True
        self.phase_bufs.append(b)
        return b

    def _dma_sem(self, buf):
        if buf.sem is None:
            if self.free_dma_sems:
                nm = self.free_dma_sems.pop()
            else:
                nm = "ds_%d" % self.n_dma_sems
                self.n_dma_sems += 1
                self.semobj[nm] = self.es.enter_context(self.nc.semaphore(nm))
                self.semcnt[nm] = 0
            buf.sem = self.semobj[nm]
            buf.semname = nm
        return buf.semname

    def _add(self, eng, fn, ins, outs, dma_sig=None):
        op = Op()
        op.eng = eng
        op.fn = fn
        op.is_dma = dma_sig is not None
        deps = []
        rb = []
        wb = []
        for v in ins:
            if isinstance(v, View) and v.buf not in rb:
                rb.append(v.buf)
        for v in outs:
            if isinstance(v, View) and v.buf not in wb:
                wb.append(v.buf)
        for b in rb:
            for w in b.writers:
                if w.eng == eng and not w.is_dma and not op.is_dma and (eng == "pe" or not SAME_ENG_RAW):
                    continue
                deps.append(w)
            if b.psum:
                for r in b.readers:
                    if r.eng != eng:
                        deps.append(r)
        for b in wb:
            for r in b.readers:
                if r.eng == eng and not r.is_dma and not op.is_dma:
                    continue
                deps.append(r)
            for w in b.writers:
                if (w.is_dma and op.is_dma) or (w.eng == eng and not w.is_dma and not op.is_dma):
                    continue
                deps.append(w)
        waits = {}
        for d in deps:
            k = d.sem
            if waits.get(k, (None, 0))[1] < d.val:
                waits[k] = (d.sem, d.val)
        kn = self.known[eng]
        op.waits = []
        for k, (s, v) in waits.items():
            if kn.get(k, 0) >= v:
                continue
            kn[k] = v
            op.waits.append((s, v))
        if op.is_dma:
            nm = self._dma_sem(dma_sig)
            self.semcnt[nm] += 1
            op.sem = nm
            op.val = 16 * self.semcnt[nm]
            op.inc = 16
            self.phase_sems.add(nm)
        else:
            self.ecnt[eng] += 1
            op.sem = "cs_" + eng
            op.val = self.ecnt[eng]
            op.inc = 1
        for b in rb:
            b.readers.append(op)
        for b in wb:
            if b.readers:
                b.readers = []
                b.writers = [op]
            else:
                if not op.is_dma:
                    b.writers = [w for w in b.writers if w.is_dma or w.eng != eng]
                b.writers.append(op)
        self.ops[eng].append(op)
        self.nops += 1
        return op

    def _sem(self, name):
        if name.startswith("cs_"):
            return self.esem[name[3:]]
        return self.semobj[name]

    def end_phase(self):
        finals = []
        for e in self.CE:
            if self.ecnt[e] > 0:
                finals.append(("cs_" + e, self.ecnt[e]))
        for nm in sorted(self.phase_sems):
            finals.append((nm, 16 * self.semcnt[nm]))
        ops = self.ops
        known = self.known
        P = self

        def emit(eng_name):
            def f(e):
                for op in ops[eng_name]:
                    for (s, v) in op.waits:
                        e.wait_ge(P._sem(s), v)
                    ins = op.fn(e)
                    ins.then_inc(P._sem(op.sem), op.inc)
                for (s, v) in finals:
                    if known[eng_name].get(s, 0) < v:
                        known[eng_name][s] = v
                        e.wait_ge(P._sem(s), v)
            return f

        with self.nc.Block() as blk:
            blk.tensor(emit("pe"))
            blk.scalar(emit("act"))
            blk.vector(emit("dve"))
            blk.gpsimd(emit("pool"))
            blk.sync(emit("sp"))
        for e in self.ALLE:
            self.ops[e] = []
        for b in self.phase_bufs:
            if b.semname is not None:
                self.free_dma_sems.append(b.semname)
        self.phase_sems = set()
        self.pes.__exit__(None, None, None)
        self.pes = None

    def dma(self, out, in_, q="sp", **kw):
        sig = out.buf if isinstance(out, View) else in_.buf
        o, i = _ap(out), _ap(in_)
        return self._add(q, lambda e: e.dma_start(out=o, in_=i, **kw), [in_], [out], dma_sig=sig)

    def mm(self, out, lhsT, rhs, start=True, stop=True, **kw):
        o, l, r = _ap(out), _ap(lhsT), _ap(rhs)
        return self._add("pe", lambda e: e.matmul(o, l, r, start=start, stop=stop, **kw), [lhsT, rhs], [out])

    def transpose(self, out, in_, ident):
        o, i, d = _ap(out), _ap(in_), _ap(ident)
        return self._add("pe", lambda e: e.transpose(o, i, d), [in_, ident], [out])

    def act(self, out, in_, func, bias=None, scale=1.0, accum_out=None, eng="act"):
        o, i = _ap(out), _ap(in_)
        kw = {}
        ins = [in_]
        outs = [out]
        if bias is not None:
            kw["bias"] = _ap(bias)
            ins.append(bias)
        if isinstance(scale, View):
            kw["scale"] = _ap(scale)
            ins.append(scale)
        else:
            kw["scale"] = scale
        if accum_out is not None:
            kw["accum_out"] = _ap(accum_out)
            outs.append(accum_out)
        return self._add("act", lambda e: e.activation(out=o, in_=i, func=func, **kw), ins, outs)

    def tt(self, out, in0, in1, op, eng="dve"):
        o, a, b = _ap(out), _ap(in0), _ap(in1)
        return self._add(eng, lambda e: e.tensor_tensor(out=o, in0=a, in1=b, op=op), [in0, in1], [out])

    def ts(self, out, in0, s1, op0, s2=None, op1=None, eng="dve"):
        o, a = _ap(out), _ap(in0)
        ins = [in0]
        if isinstance(s1, View):
            ins.append(s1)
        if isinstance(s2, View):
            ins.append(s2)
        x1, x2 = _ap(s1), _ap(s2)
        if op1 is None:
            return self._add(eng, lambda e: e.tensor_scalar(out=o, in0=a, scalar1=x1, scalar2=None, op0=op0), ins, [out])
        return self._add(eng, lambda e: e.tensor_scalar(out=o, in0=a, scalar1=x1, scalar2=x2, op0=op0, op1=op1), ins, [out])

    def stt(self, out, in0, scalar, in1, op0, op1):
        o, a, b = _ap(out), _ap(in0), _ap(in1)
        ins = [in0, in1]
        if isinstance(scalar, View):
            ins.append(scalar)
        s = _ap(scalar)
        return self._add("dve", lambda e: e.scalar_tensor_tensor(out=o, in0=a, scalar=s, in1=b, op0=op0, op1=op1), ins, [out])

    def copy(self, out, in_, eng="dve"):
        o, i = _ap(out), _ap(in_)
        if eng == "act":
            return self._add("act", lambda e: e.activation(out=o, in_=i, func=AF.Copy), [in_], [out])
        return self._add(eng, lambda e: e.tensor_copy(out=o, in_=i), [in_], [out])

    def memset(self, out, val, eng="pool"):
        o = _ap(out)
        return self._add(eng, lambda e: e.memset(o, val), [], [out])

    def recip(self, out, in_):
        o, i = _ap(out), _ap(in_)
        return self._add("dve", lambda e: e.reciprocal(out=o, in_=i), [in_], [out])

    def scan(self, out, d0, d1, initial, op0, op1):
        o, a, b = _ap(out), _ap(d0), _ap(d1)
        ins = [d0, d1]
        if isinstance(initial, View):
            ins.append(initial)
        ini = _ap(initial)
        return self._add("dve", lambda e: e.tensor_tensor_scan(out=o, data0=a, data1=b, initial=ini, op0=op0, op1=op1), ins, [out])


class Ctx:
    pass


def setup_consts(P, nc, es, cdram):
    C = Ctx()

    def sbg(name, shape, dt):
        t = es.enter_context(nc.sbuf_tensor(name, list(shape), dt))
        return Buf(t, name)

    C.ones_f = sbg("ones_f", [128, 128], F32)
    C.ones_b = sbg("ones_b", [128, 128], BF16)
    C.ident_b = sbg("ident_b", [128, 128], BF16)
    C.ident_f = sbg("ident_f", [128, 128], F32)
    C.negtri_b = sbg("negtri_b", [128, 128], BF16)
    C.sbmask = sbg("sbmask", [128, 4, 512], BF16)
    C.perm_b = sbg("perm_b", [128, 128], BF16)
    C.dmask = sbg("dmask", [128, 2, 512], BF16)
    C.invf = sbg("invf", [128, 2], F32)
    C.triu_f = sbg("triu_f", [128, 128], F32)
    C.sellast_f = sbg("sellast_f", [128, 128], F32)
    C.maskSL_f = sbg("maskSL_f", [128, 128], F32)
    C.maskLI_f = sbg("maskLI_f", [128, 128], F32)
    C.lvmask = sbg("lvmask", [128, 7, 256], BF16)
    C.eps_col = sbg("eps_col", [128, 1], F32)
    C.one_col = sbg("one_col", [128, 1], F32)
    P.begin_phase()
    stage = P.sb([128, CF_W], F32, "cstage")
    P.dma(stage[:, :], cdram["cf"][:, :])
    P.copy(C.ones_f[:, :], stage[:, 0:128])
    P.copy(C.ones_b[:, :], stage[:, 0:128])
    P.copy(C.ident_f[:, :], stage[:, 128:256])
    P.copy(C.ident_b[:, :], stage[:, 128:256])
    P.copy(C.negtri_b[:, :], stage[:, 256:384])
    P.copy(C.sbmask[:, :, :], stage[:, 640:640 + 2048].rr("p (j c) -> p j c", j=4))
    P.copy(C.perm_b[:, :], stage[:, 384:512])
    P.copy(C.dmask[:, :, :], stage[:, 2688:2688 + 1024].rr("p (j c) -> p j c", j=2))
    P.copy(C.invf[:, :], stage[:, 512:514])
    P.copy(C.triu_f[:, :], stage[:, 3712:3840])
    P.copy(C.sellast_f[:, :], stage[:, 3840:3968])
    P.copy(C.maskSL_f[:, :], stage[:, 3968:4096])
    P.copy(C.maskLI_f[:, :], stage[:, 4096:4224])
    P.copy(C.lvmask[:, :, :], stage[:, 4224:4224 + 1792].rr("p (l c) -> p l c", l=7))
    P.memset(C.eps_col[:, :], EPS)
    P.memset(C.one_col[:, :], 1.0)
    P.end_phase()
    return C


CF_W = 640 + 2048 + 1024 + 512 + 1792


def host_consts():
    cf = np.zeros((128, CF_W), np.float32)
    cf[:, 0:128] = 1.0
    cf[:, 128:256] = np.eye(128, dtype=np.float32)
    s = np.arange(128)[:, None]
    sp = np.arange(128)[None, :]
    cf[:, 256:384] = -(s >= sp).astype(np.float32)
    c = np.arange(512)[None, :]
    for j in range(4):
        cf[:, 640 + j * 512: 640 + (j + 1) * 512] = (c > j * 128 + s).astype(np.float32)
    for dd in range(16):
        cf[dd + 16, 384 + dd] = 1.0
        cf[dd, 384 + dd + 16] = 1.0
    invf = (np.float32(500000.0) ** (-np.arange(16, dtype=np.float32) / np.float32(16))).astype(np.float32)
    cf[0:16, 512] = invf
    cf[16:32, 512] = invf
    cf[0:16, 513] = -1.0
    cf[16:32, 513] = 1.0
    cm = np.arange(512)[None, :] % 128
    cf[:, 2688:2688 + 512] = (s <= cm).astype(np.float32)
    cf[:, 2688 + 512:2688 + 1024] = (s >= cm).astype(np.float32)
    cf[:, 3712:3840] = (s <= sp).astype(np.float32)
    cf[127, 3840:3968] = 1.0
    cf[:, 3968:4096] = (s > sp).astype(np.float32)
    cf[:, 4096:4224] = (s >= sp).astype(np.float32)
    for l in range(7):
        b = 1 << l
        mk = ((s // (2 * b) == sp // (2 * b)) & (s % (2 * b) >= b) & (sp % (2 * b) < b)).astype(np.float32)
        cf[:, 4224 + l * 256: 4224 + l * 256 + 128] = mk
        cf[:, 4224 + l * 256 + 128: 4224 + (l + 1) * 256] = mk.T
    return {"cf": cf}


def load_wpanel(P, wbuf, w_dram, kc_n, c0, ncols):
    src = w_dram[:, c0:c0 + ncols].rearrange("(c p) n -> p c n", p=128)
    half = kc_n // 2 if kc_n >= 2 else kc_n
    P.dma(wbuf[:, 0:half, 0:ncols], src[:, 0:half, :], q="pool")
    if half < kc_n:
        P.dma(wbuf[:, half:kc_n, 0:ncols], src[:, half:kc_n, :], q="pool")


def linear_fm(P, src, kc_n, ntok, w_dram, c0, ncols, wbufs, psums, epilogue, panel=512, state=None):
    if state is None:
        state = {"w": 0, "p": 0}
    npan = (ncols + panel - 1) // panel
    ci = 0
    for pi in range(npan):
        pc = min(panel, ncols - pi * panel)
        wb = wbufs[state["w"] % len(wbufs)]
        state["w"] += 1
        load_wpanel(P, wb, w_dram, kc_n, c0 + pi * panel, pc)
        for j in range(pc // 128):
            pb = psums[state["p"] % len(psums)]
            state["p"] += 1
            for kc in range(kc_n):
                P.mm(pb[:, 0:ntok], wb[:, kc, j * 128:(j + 1) * 128], src[:, kc, 0:ntok],
                     start=(kc == 0), stop=(kc == kc_n - 1))
            epilogue(ci, pb)
            ci += 1
    return state


def phase_norm(P, C, xT, g_dram, hT, T):
    P.begin_phase()
    TT = min(512, T)
    gcol = P.sb([128, KC], F32, "gcol")
    P.dma(gcol[:, :], g_dram.rearrange("(c p) -> p c", p=128), allow_slow_non_contiguous=True)
    xb = [P.sb([128, KC, TT], F32, "nx") for _ in range(2)]
    sq = P.sb([128, KC, TT], F32, "nsq")
    hb = [P.sb([128, KC, TT], BF16, "nh") for _ in range(2)]
    ssq = [P.ps([128, 512], F32, "nssq") for _ in range(2)]
    lnv = P.sb([128, TT], F32, "nln")
    rstd = [P.sb([128, TT], F32, "nrstd") for _ in range(2)]
    xr = xT.rearrange("(c p) t -> p c t", p=128)
    hr = hT.rearrange("(c p) t -> p c t", p=128)
    for ti in range(T // TT):
        x = xb[ti % 2]
        h = hb[ti % 2]
        ps = ssq[ti % 2]
        rs = rstd[ti % 2]
        P.dma(x[:, :, :], xr[:, :, ti * TT:(ti + 1) * TT])
        P.act(sq[:, :, :], x[:, :, :], AF.Square)
        for c in range(KC):
            P.mm(ps[:, 0:TT], C.ones_f[:, :], sq[:, c, :], start=(c == 0), stop=(c == KC - 1))
        P.act(lnv[:, :], ps[:, 0:TT], AF.Ln, bias=C.eps_col[:, :], scale=1.0 / D)
        P.act(rs[:, :], lnv[:, :], AF.Exp, scale=-0.5)
        for c in range(KC):
            P.stt(h[:, c, :], x[:, c, :], gcol[:, c:c + 1], rs[:, :], ALU.mult, ALU.mult)
        P.dma(hr[:, :, ti * TT:(ti + 1) * TT], h[:, :, :])
    P.end_phase()


def phase_ffn(P, C, xT, hT, w_gu, w_down, T):
    P.begin_phase()
    TB = min(512, T)
    HC = FFN_H // 128
    hb = [P.sb([128, KC, TB], BF16, "fh") for _ in range(2)]
    actT = P.sb([128, HC, TB], BF16, "fact")
    wg = [P.sb([128, KC, 512], BF16, "fwg") for _ in range(2)]
    wu = [P.sb([128, KC, 512], BF16, "fwu") for _ in range(2)]
    wd = [P.sb([128, HC, 256], BF16, "fwd") for _ in range(2)]
    sg = [P.sb([128, TB], F32, "fsg") for _ in range(2)]
    xt = [P.sb([128, TB], F32, "fx") for _ in range(3)]
    pg = [P.ps([128, 512], F32, "fpg") for _ in range(2)]
    pu = [P.ps([128, 512], F32, "fpu") for _ in range(2)]
    po = [P.ps([128, 512], F32, "fpo") for _ in range(2)]
    hr = hT.rearrange("(c p) t -> p c t", p=128)
    xr = xT.rearrange("(c p) t -> p c t", p=128)
    cnt = 0
    for bi in range(T // TB):
        h = hb[bi % 2]
        P.dma(h[:, :, :], hr[:, :, bi * TB:(bi + 1) * TB])
        for pi in range(FFN_H // 512):
            g_w = wg[pi % 2]
            u_w = wu[pi % 2]
            load_wpanel(P, g_w, w_gu, KC, pi * 512, 512)
            load_wpanel(P, u_w, w_gu, KC, FFN_H + pi * 512, 512)
            for j in range(4):
                fc = pi * 4 + j
                g_p = pg[cnt % 2]
                u_p = pu[cnt % 2]
                s_b = sg[cnt % 2]
                cnt += 1
                for kc in range(KC):
                    P.mm(g_p[:, 0:TB], g_w[:, kc, j * 128:(j + 1) * 128], h[:, kc, :], start=(kc == 0), stop=(kc == KC - 1))
                for kc in range(KC):
                    P.mm(u_p[:, 0:TB], u_w[:, kc, j * 128:(j + 1) * 128], h[:, kc, :], start=(kc == 0), stop=(kc == KC - 1))
                P.act(s_b[:, :], g_p[:, 0:TB], AF.Silu)
                P.tt(actT[:, fc, :], s_b[:, :], u_p[:, 0:TB], ALU.mult)
        for dp in range(D // 256):
            d_w = wd[dp % 2]
            load_wpanel(P, d_w, w_down, HC, dp * 256, 256)
            for j in range(2):
                dc = dp * 2 + j
                o_p = po[dc % 2]
                x_b = xt[dc % 3]
                P.dma(x_b[:, :], xr[:, dc, bi * TB:(bi + 1) * TB])
                for kc in range(HC):
                    P.mm(o_p[:, 0:TB], d_w[:, kc, j * 128:(j + 1) * 128], actT[:, kc, :], start=(kc == 0), stop=(kc == HC - 1))
                P.tt(x_b[:, :], x_b[:, :], o_p[:, 0:TB], ALU.add)
                P.dma(xr[:, dc, bi * TB:(bi + 1) * TB], x_b[:, :])
    P.end_phase()


def phase_outproj(P, C, xT, srcT, kc_n, w_out, T):
    P.begin_phase()
    TB = min(512, T)
    sb_ = [P.sb([128, kc_n, TB], BF16, "os") for _ in range(2)]
    wb = [P.sb([128, kc_n, 512], BF16, "ow") for _ in range(2)]
    xt = [P.sb([128, TB], F32, "ox") for _ in range(3)]
    po = [P.ps([128, 512], F32, "opo") for _ in range(2)]
    sr = srcT[0:kc_n * 128, :].rearrange("(c p) t -> p c t", p=128)
    xr = xT.rearrange("(c p) t -> p c t", p=128)
    st = {"w": 0, "p": 0}
    for bi in range(T // TB):
        s = sb_[bi % 2]
        P.dma(s[:, :, :], sr[:, :, bi * TB:(bi + 1) * TB])

        def epi(ci, pb, bi=bi):
            x_b = xt[ci % 3]
            P.dma(x_b[:, :], xr[:, ci, bi * TB:(bi + 1) * TB])
            P.tt(x_b[:, :], x_b[:, :], pb[:, 0:TB], ALU.add)
            P.dma(xr[:, ci, bi * TB:(bi + 1) * TB], x_b[:, :])
        linear_fm(P, s, kc_n, TB, w_out, 0, D, wb, po, epi, state=st)
    P.end_phase()


def headnorm_epilogue(P, C, pb, ntok, gcol_view, sqb, pss, lnb, rsb, outv):
    P.act(sqb[:, 0:ntok], pb[:, 0:ntok], AF.Square)
    P.mm(pss[:, 0:ntok], C.ones_f[:, :], sqb[:, 0:ntok], start=True, stop=True)
    P.act(lnb[:, 0:ntok], pss[:, 0:ntok], AF.Ln, bias=C.eps_col[:, :], scale=1.0 / 128)
    P.act(rsb[:, 0:ntok], lnb[:, 0:ntok], AF.Exp, scale=-0.5)
    P.stt(outv, pb[:, 0:ntok], gcol_view, rsb[:, 0:ntok], ALU.mult, ALU.mult)


def phase_sb_proj(P, C, hT, w_in, qn_d, kn_d, qT, kT, v, T):
    P.begin_phase()
    TB = min(512, T)
    H = 16
    hb = [P.sb([128, KC, TB], BF16, "sh") for _ in range(2)]
    wb = [P.sb([128, KC, 512], BF16, "sw") for _ in range(2)]
    gq = P.sb([128, 1], F32, "gq")
    gk = P.sb([128, 1], F32, "gk")
    P.dma(gq[:, :], qn_d.rearrange("(p o) -> p o", o=1))
    P.dma(gk[:, :], kn_d.rearrange("(p o) -> p o", o=1))
    P.ts(gq[:, :], gq[:, :], 128.0 ** -0.5, ALU.mult)
    pq = [P.ps([128, 512], F32, "spq") for _ in range(2)]
    pss = [P.ps([128, 512], F32, "spss") for _ in range(2)]
    pv = [P.ps([128, 512], F32, "spv") for _ in range(2)]
    sqb = [P.sb([128, TB], F32, "ssq") for _ in range(2)]
    lnb = P.sb([128, TB], F32, "sln")
    rsb = [P.sb([128, TB], F32, "srs") for _ in range(2)]
    ob = [P.sb([128, TB], BF16, "sob") for _ in range(3)]
    vb = [P.sb([128, 512], BF16, "svb") for _ in range(3)]
    hr = hT.rearrange("(c p) t -> p c t", p=128)
    st = {"w": 0, "p": 0}
    n = [0]
    for bi in range(T // TB):
        h = hb[bi % 2]
        P.dma(h[:, :, :], hr[:, :, bi * TB:(bi + 1) * TB])
        for which, (c0, gcol, dst) in enumerate(((0, gq, qT), (H * 128, gk, kT))):
            def epi(ci, pb, gcol=gcol, dst=dst, bi=bi):
                i = n[0]
                n[0] += 1
                o = ob[i % 3]
                headnorm_epilogue(P, C, pb, TB, gcol[:, 0:1], sqb[i % 2], pss[i % 2], lnb, rsb[i % 2], o[:, 0:TB])
                P.dma(dst[ci * 128:(ci + 1) * 128, bi * TB:(bi + 1) * TB], o[:, 0:TB])
            linear_fm(P, h, KC, TB, w_in, c0, H * 128, wb, pq, epi, state=st)
        for pi in range(H * 128 // 512):
            w = wb[st["w"] % 2]
            st["w"] += 1
            load_wpanel(P, w, w_in, KC, 2 * H * 128 + pi * 512, 512)
            for tt in range(TB // 128):
                i = n[0]
                n[0] += 1
                p = pv[i % 2]
                for kc in range(KC):
                    P.mm(p[:, :], h[:, kc, tt * 128:(tt + 1) * 128], w[:, kc, :], start=(kc == 0), stop=(kc == KC - 1))
                o = vb[i % 3]
                P.copy(o[:, :], p[:, :], eng="act")
                P.dma(v[bi * TB + tt * 128: bi * TB + (tt + 1) * 128, pi * 512:(pi + 1) * 512], o[:, :])
    P.end_phase()


def phase_sb_attn(P, C, qT, kT, v, oT, T):
    P.begin_phase()
    H = 16
    NS = 2
    NB = T // 128
    QT = min(512, T)

    def mkset(i):
        B = {}
        B["q"] = P.sb([128, T], BF16, "aq")
        B["k"] = P.sb([128, T], BF16, "ak")
        B["v"] = P.sb([128, NB, 128], BF16, "av")
        B["pz"] = [P.ps([128, 512], F32, "apz") for _ in range(2)]
        B["pcs"] = P.ps([128, 512], F32, "apcs")
        B["po"] = P.ps([128, 512], F32, "apo")
        B["e"] = [P.sb([128, QT], F32, "ae") for _ in range(2)]
        B["sp"] = [P.sb([128, QT], BF16, "asp") for _ in range(2)]
        B["d"] = [P.sb([128, QT], F32, "ad") for _ in range(2)]
        B["w"] = [P.sb([128, QT], BF16, "aw") for _ in range(2)]
        B["cs"] = [P.sb([128, QT], F32, "acs") for _ in range(2)]
        B["ob"] = [P.sb([128, QT], BF16, "aob") for _ in range(2)]
        return B
    sets = [mkset(i) for i in range(NS)]

    def stream(h, B):
        q, k, vv = B["q"], B["k"], B["v"]
        P.dma(q[:, :], qT[h * 128:(h + 1) * 128, :])
        P.dma(k[:, :], kT[h * 128:(h + 1) * 128, :])
        P.dma(vv[:, :, :], v[:, h * 128:(h + 1) * 128].rearrange("(nb s) d -> s nb d", s=128))
        yield
        u = 0
        for qi in range(T // QT):
            o_p = B["po"]
            cs = B["cs"][qi % 2]
            c_p = B["pcs"]
            nsub = QT // 128
            mlast = qi * nsub + nsub - 1
            for m in range(mlast, -1, -1):
                z = B["pz"][u % 2]
                e_b, s_b, d_b, w_b = B["e"][u % 2], B["sp"][u % 2], B["d"][u % 2], B["w"][u % 2]
                u += 1
                first = (m == mlast)
                diag = m >= qi * nsub
                j = m - qi * nsub
                qv = q[:, qi * QT:(qi + 1) * QT]
                P.mm(z[:, 0:QT], k[:, m * 128:(m + 1) * 128], qv, start=True, stop=False)
                yield
                P.act(e_b[:, :], z[:, 0:QT], AF.Exp)
                yield
                P.act(s_b[:, :], e_b[:, :], AF.Ln, bias=C.one_col[:, :])
                yield
                if diag:
                    P.tt(s_b[:, :], s_b[:, :], C.sbmask[:, j, 0:QT], ALU.mult, eng="pool")
                    yield
                P.mm(z[:, 0:QT], C.negtri_b[:, :], s_b[:, :], start=False, stop=True)
                if m > 0:
                    P.mm(c_p[:, 0:QT], C.ones_b[:, :], s_b[:, :], start=True, stop=True)
                yield
                if first:
                    P.act(w_b[:, :], z[:, 0:QT], AF.Exp)
                    if m > 0:
                        P.copy(cs[:, :], c_p[:, 0:QT], eng="dve")
                else:
                    P.tt(d_b[:, :], z[:, 0:QT], cs[:, :], ALU.subtract)
                    yield
                    P.act(w_b[:, :], d_b[:, :], AF.Exp)
                    if m > 0:
                        P.tt(cs[:, :], cs[:, :], c_p[:, 0:QT], ALU.add)
                yield
                if diag:
                    P.tt(w_b[:, :], w_b[:, :], C.sbmask[:, j, 0:QT], ALU.mult, eng="pool")
                    yield
                P.mm(o_p[:, 0:QT], vv[:, m, :], w_b[:, :], start=first, stop=(m == 0))
                yield
            o_b = B["ob"][qi % 2]
            P.copy(o_b[:, :], o_p[:, 0:QT], eng="act")
            P.dma(oT[h * 128:(h + 1) * 128, qi * QT:(qi + 1) * QT], o_b[:, :])
            yield

    for h0 in range(0, H, NS):
        gens = [stream(h0 + i, sets[i]) for i in range(NS)]
        active = list(gens)
        while active:
            nxt = []
            for gnr in active:
                try:
                    next(gnr)
                    nxt.append(gnr)
                except StopIteration:
                    pass
            active = nxt
    P.end_phase()


def phase_lru_proj(P, C, hT, w_in, gT, xrT, T):
    P.begin_phase()
    TB = min(512, T)
    hb = [P.sb([128, KC, TB], BF16, "lh") for _ in range(2)]
    wb = [P.sb([128, KC, 512], BF16, "lw") for _ in range(2)]
    pq = [P.ps([128, 512], F32, "lpq") for _ in range(2)]
    gb = [P.sb([128, TB], BF16, "lgb") for _ in range(3)]
    xb = [P.sb([128, TB], F32, "lxb") for _ in range(3)]
    hr = hT.rearrange("(c p) t -> p c t", p=128)
    st = {"w": 0, "p": 0}
    for bi in range(T // TB):
        h = hb[bi % 2]
        P.dma(h[:, :, :], hr[:, :, bi * TB:(bi + 1) * TB])

        def epi(ci, pb, bi=bi):
            if ci < 16:
                o = gb[ci % 3]
                P.act(o[:, :], pb[:, 0:TB], AF.Gelu_apprx_tanh)
                P.dma(gT[ci * 128:(ci + 1) * 128, bi * TB:(bi + 1) * TB], o[:, :])
            else:
                o = xb[ci % 3]
                P.copy(o[:, :], pb[:, 0:TB], eng="dve")
                P.dma(xrT[(ci - 16) * 128:(ci - 15) * 128, bi * TB:(bi + 1) * TB], o[:, :])
        linear_fm(P, h, KC, TB, w_in, 0, 2 * D, wb, pq, epi, state=st)
    P.end_phase()


def phase_lru_rec(P, C, gT, xrT, yT, conv_w, conv_b, w_a, b_a, w_x, b_x, lam, T):
    P.begin_phase()
    TT = min(512, T)
    cw = P.sb([128, 4, KC], F32, "rcw")
    cb = P.sb([128, KC], F32, "rcb")
    ba = P.sb([128, KC], F32, "rba")
    bx = P.sb([128, KC], F32, "rbx")
    lm = P.sb([128, KC], F32, "rlm")
    cc = P.sb([128, KC], F32, "rcc")
    for k in range(4):
        P.dma(cw[:, k, :], conv_w[k].rearrange("(c p) -> p c", p=128), allow_slow_non_contiguous=True)
    P.dma(cb[:, :], conv_b.rearrange("(c p) -> p c", p=128), allow_slow_non_contiguous=True)
    P.dma(ba[:, :], b_a.rearrange("n (c p) -> p (n c)", p=128), allow_slow_non_contiguous=True)
    P.dma(bx[:, :], b_x.rearrange("n (c p) -> p (n c)", p=128), allow_slow_non_contiguous=True)
    P.dma(lm[:, :], lam.rearrange("(c p) -> p c", p=128), allow_slow_non_contiguous=True)
    P.act(cc[:, :], lm[:, :], AF.Exp, scale=-1.0)
    P.act(cc[:, :], cc[:, :], AF.Ln, bias=C.one_col[:, :])
    P.ts(cc[:, :], cc[:, :], -8.0, ALU.mult)
    xp = P.sb([128, 2, T + 3], F32, "rxp")
    xc = P.sb([128, 2, T], F32, "rxc")
    xcb = P.sb([128, 2, T], BF16, "rxcb")
    wab = [P.sb([128, 2, 256], BF16, "rwa") for _ in range(2)]
    wxb = [P.sb([128, 2, 256], BF16, "rwx") for _ in range(2)]
    a_b = P.sb([128, T], F32, "ra")
    u_b = P.sb([128, T], F32, "ru")
    hs = P.sb([128, T], F32, "rhs")
    g_b = [P.sb([128, T], BF16, "rg") for _ in range(2)]
    y_b = [P.sb([128, T], BF16, "ry") for _ in range(2)]
    pr = [P.ps([128, 512], F32, "rpr") for _ in range(2)]
    pi_ = [P.ps([128, 512], F32, "rpi") for _ in range(2)]
    r_t = [P.sb([128, TT], F32, "rrt") for _ in range(2)]
    i_t = [P.sb([128, TT], F32, "rit") for _ in range(2)]
    a2 = [P.sb([128, TT], F32, "ra2") for _ in range(2)]
    s_t = [P.sb([128, TT], F32, "rst") for _ in range(2)]
    P.memset(xp[:, :, 0:3], 0.0)
    n = 0
    for blk in range(8):
        wa_, wx_ = wab[blk % 2], wxb[blk % 2]
        P.dma(wa_[:, :, :], w_a[blk].rearrange("(c p) n -> p c n", p=128), q="pool")
        P.dma(wx_[:, :, :], w_x[blk].rearrange("(c p) n -> p c n", p=128), q="pool")
        for c in range(2):
            ch = blk * 2 + c
            P.dma(xp[:, c, 3:T + 3], xrT[ch * 128:(ch + 1) * 128, :])
            P.ts(xc[:, c, :], xp[:, c, 0:T], cw[:, 0, ch:ch + 1], ALU.mult, cb[:, ch:ch + 1], ALU.add)
            for k in range(1, 4):
                P.stt(xc[:, c, :], xp[:, c, k:T + k], cw[:, k, ch:ch + 1], xc[:, c, :], ALU.mult, ALU.add)
            P.copy(xcb[:, c, :], xc[:, c, :], eng="act")
        for jo in range(2):
            ch = blk * 2 + jo
            g = g_b[ch % 2]
            P.dma(g[:, :], gT[ch * 128:(ch + 1) * 128, :])
            for tt in range(T // TT):
                sl = slice(tt * TT, (tt + 1) * TT)
                p_r, p_i = pr[n % 2], pi_[n % 2]
                rt, it, a2t, stt_ = r_t[n % 2], i_t[n % 2], a2[n % 2], s_t[n % 2]
                n += 1
                for ki in range(2):
                    P.mm(p_r[:, 0:TT], wa_[:, ki, jo * 128:(jo + 1) * 128], xcb[:, ki, sl], start=(ki == 0), stop=(ki == 1))
                for ki in range(2):
                    P.mm(p_i[:, 0:TT], wx_[:, ki, jo * 128:(jo + 1) * 128], xcb[:, ki, sl], start=(ki == 0), stop=(ki == 1))
                P.act(rt[:, :], p_r[:, 0:TT], AF.Sigmoid, bias=ba[:, ch:ch + 1])
                P.act(it[:, :], p_i[:, 0:TT], AF.Sigmoid, bias=bx[:, ch:ch + 1])
                P.act(a_b[:, sl], rt[:, :], AF.Exp, scale=cc[:, ch:ch + 1])
                P.ts(a_b[:, sl], a_b[:, sl], 1.0, ALU.min)
                P.tt(a2t[:, :], a_b[:, sl], a_b[:, sl], ALU.mult)
                P.act(stt_[:, :], a2t[:, :], AF.Sqrt, bias=C.one_col[:, :], scale=-1.0)
                P.tt(it[:, :], it[:, :], xc[:, jo, sl], ALU.mult, eng="pool")
                P.tt(u_b[:, sl], it[:, :], stt_[:, :], ALU.mult)
            P.scan(hs[:, :], a_b[:, :], u_b[:, :], 0.0, ALU.mult, ALU.add)
            y = y_b[ch % 2]
            P.tt(y[:, :], hs[:, :], g[:, :], ALU.mult, eng="pool")
            P.dma(yT[ch * 128:(ch + 1) * 128, :], y[:, :])
    P.end_phase()


TWO_PI = 6.283185307179586
CW1 = 6.28125
CW2 = TWO_PI - CW1
DSW_DIL = (1, 4, 16)


def phase_rope_tables(P, C, pos, ctab, stab, T):
    P.begin_phase()
    TT = min(512, T)
    posi = [P.sb([1, TT], I32, "tpi") for _ in range(2)]
    posf = [P.sb([1, TT], F32, "tpf") for _ in range(2)]
    pb = [P.ps([128, 512], F32, "tpb") for _ in range(2)]
    ang = P.sb([32, TT], F32, "tang")
    ki = P.sb([32, TT], I32, "tki")
    kf = P.sb([32, TT], F32, "tkf")
    r = P.sb([32, TT], F32, "tr")
    y = P.sb([32, TT], F32, "ty")
    m = P.sb([32, TT], F32, "tm")
    sn = P.sb([32, TT], F32, "tsn")
    ct = [P.sb([128, TT], F32, "tct") for _ in range(2)]
    st_ = [P.sb([128, TT], F32, "tst") for _ in range(2)]
    for i in range(2):
        P.memset(ct[i][:, :], 1.0)
        P.memset(st_[i][:, :], 0.0)
    for ti in range(T // TT):
        sl = slice(ti * TT, (ti + 1) * TT)
        pi_, pf, p_b, c_t, s_t = posi[ti % 2], posf[ti % 2], pb[ti % 2], ct[ti % 2], st_[ti % 2]
        P.dma(pi_[:, :], pos[0:1, sl])
        P.copy(pf[:, :], pi_[:, :], eng="dve")
        P.mm(p_b[0:32, 0:TT], C.ones_f[0:1, 0:32], pf[:, :], start=True, stop=True)
        P.ts(ang[:, :], p_b[0:32, 0:TT], C.invf[0:32, 0:1], ALU.mult)
        P.ts(ki[:, :], ang[:, :], 1.0 / TWO_PI, ALU.mult)
        P.copy(kf[:, :], ki[:, :], eng="dve")
        P.stt(r[:, :], kf[:, :], -CW1, ang[:, :], ALU.mult, ALU.add)
        P.stt(r[:, :], kf[:, :], -CW2, r[:, :], ALU.mult, ALU.add)
        for shift, dst, sgn in ((0.0, s_t, True), (TWO_PI / 4, c_t, False)):
            P.ts(y[:, :], r[:, :], shift, ALU.add)
            P.ts(m[:, :], y[:, :], TWO_PI / 2, ALU.is_gt, -TWO_PI, ALU.mult)
            P.tt(y[:, :], y[:, :], m[:, :], ALU.add)
            P.ts(m[:, :], y[:, :], -TWO_PI / 2, ALU.is_lt, TWO_PI, ALU.mult)
            P.tt(y[:, :], y[:, :], m[:, :], ALU.add)
            if sgn:
                P.act(sn[:, :], y[:, :], AF.Sin)
                P.ts(dst[0:32, :], sn[:, :], C.invf[0:32, 1:2], ALU.mult)
            else:
                P.act(dst[0:32, :], y[:, :], AF.Sin)
        P.dma(ctab[:, sl], c_t[:, :])
        P.dma(stab[:, sl], s_t[:, :])
    P.end_phase()


def phase_dsw_proj(P, C, hT, w_in, qn_d, kn_d, ctab, stab, qT, kT, v, T):
    P.begin_phase()
    TB = min(512, T)
    H = 18
    hb = [P.sb([128, KC, TB], BF16, "dh") for _ in range(2)]
    wb = [P.sb([128, KC, 512], BF16, "dw") for _ in range(2)]
    gq = P.sb([128, 1], F32, "dgq")
    gk = P.sb([128, 1], F32, "dgk")
    P.dma(gq[:, :], qn_d.rearrange("(p o) -> p o", o=1))
    P.dma(gk[:, :], kn_d.rearrange("(p o) -> p o", o=1))
    P.ts(gq[:, :], gq[:, :], 128.0 ** -0.5, ALU.mult)
    pq = [P.ps([128, 512], F32, "dpq") for _ in range(2)]
    pss = [P.ps([128, 512], F32, "dpss") for _ in range(2)]
    pr = [P.ps([128, 512], F32, "dpr") for _ in range(2)]
    pv = [P.ps([128, 512], F32, "dpv") for _ in range(2)]
    sqb = [P.sb([128, TB], F32, "dsq") for _ in range(2)]
    lnb = P.sb([128, TB], F32, "dln")
    rsb = [P.sb([128, TB], F32, "drs") for _ in range(2)]
    qnb = [P.sb([128, TB], BF16, "dqn") for _ in range(2)]
    t1 = [P.sb([128, TB], F32, "dt1") for _ in range(2)]
    t2 = [P.sb([128, TB], F32, "dt2") for _ in range(2)]
    ob = [P.sb([128, TB], BF16, "dob") for _ in range(3)]
    vb = [P.sb([128, 512], BF16, "dvb") for _ in range(3)]
    cb = [P.sb([128, TB], F32, "dcb") for _ in range(2)]
    sb_ = [P.sb([128, TB], F32, "dsb") for _ in range(2)]
    hr = hT.rearrange("(c p) t -> p c t", p=128)
    st = {"w": 0, "p": 0}
    n = [0]
    for bi in range(T // TB):
        h = hb[bi % 2]
        c_t, s_t = cb[bi % 2], sb_[bi % 2]
        tsl = slice(bi * TB, (bi + 1) * TB)
        P.dma(h[:, :, :], hr[:, :, tsl])
        P.dma(c_t[:, :], ctab[:, tsl])
        P.dma(s_t[:, :], stab[:, tsl])
        for which, (c0, gcol, dst) in enumerate(((0, gq, qT), (H * 128, gk, kT))):
            def epi(ci, pb, gcol=gcol, dst=dst, tsl=tsl, c_t=c_t, s_t=s_t):
                i = n[0]
                n[0] += 1
                qn = qnb[i % 2]
                headnorm_epilogue(P, C, pb, TB, gcol[:, 0:1], sqb[i % 2], pss[i % 2], lnb, rsb[i % 2], qn[:, 0:TB])
                p_r = pr[i % 2]
                P.mm(p_r[:, 0:TB], C.perm_b[:, :], qn[:, 0:TB], start=True, stop=True)
                P.tt(t1[i % 2][:, :], p_r[:, 0:TB], s_t[:, :], ALU.mult)
                P.tt(t2[i % 2][:, :], qn[:, 0:TB], c_t[:, :], ALU.mult, eng="pool")
                o = ob[i % 3]
                P.tt(o[:, 0:TB], t1[i % 2][:, :], t2[i % 2][:, :], ALU.add)
                P.dma(dst[ci * 128:(ci + 1) * 128, tsl], o[:, 0:TB])
            linear_fm(P, h, KC, TB, w_in, c0, H * 128, wb, pq, epi, state=st)
        ncol = H * 128
        for pi in range((ncol + 511) // 512):
            pc = min(512, ncol - pi * 512)
            w = wb[st["w"] % 2]
            st["w"] += 1
            load_wpanel(P, w, w_in, KC, 2 * H * 128 + pi * 512, pc)
            for tt in range(TB // 128):
                i = n[0]
                n[0] += 1
                p = pv[i % 2]
                for kc in range(KC):
                    P.mm(p[:, 0:pc], h[:, kc, tt * 128:(tt + 1) * 128], w[:, kc, 0:pc], start=(kc == 0), stop=(kc == KC - 1))
                o = vb[i % 3]
                P.copy(o[:, 0:pc], p[:, 0:pc], eng="act")
                P.dma(v[bi * TB + tt * 128: bi * TB + (tt + 1) * 128, pi * 512:pi * 512 + pc], o[:, 0:pc])
    P.end_phase()


def phase_dsw_attn(P, C, qT, kT, v, oT, T):
    P.begin_phase()
    qnat = P.sb([128, T], BF16, "wqn")
    knat = P.sb([128, T], BF16, "wkn")
    qd = P.sb([128, T], BF16, "wqd")
    kd = P.sb([128, T], BF16, "wkd")
    vd = [P.sb([128, T // 128, 128], BF16, "wvd") for _ in range(2)]
    Og = [P.sb([128, T], F32, "wO") for _ in range(3)]
    Dg = [P.sb([128, T], F32, "wD") for _ in range(3)]
    outb = [P.sb([128, T], BF16, "wout") for _ in range(2)]
    psc = [P.ps([128, 512], F32, "wpsc") for _ in range(2)]
    psp = [P.ps([128, 512], F32, "wpsp") for _ in range(2)]
    po = [P.ps([128, 512], F32, "wpo") for _ in range(2)]
    pd = [P.ps([128, 512], F32, "wpd") for _ in range(2)]
    pcb = [P.sb([128, 512], BF16, "wpc") for _ in range(2)]
    ppb = [P.sb([128, 512], BF16, "wpp") for _ in range(2)]
    u = 0
    hcount = 0
    for hg in range(6):
        for g in range(3):
            d = DSW_DIL[g]
            head = g * 6 + hg
            L = T // d
            NBs = L // 128
            vv = vd[hcount % 2]
            hcount += 1
            P.dma(qnat[:, :], qT[head * 128:(head + 1) * 128, :])
            P.dma(knat[:, :], kT[head * 128:(head + 1) * 128, :])
            vsrc = v[:, head * 128:(head + 1) * 128].rearrange("(nb j r) c -> r j nb c", j=128, r=d)
            for r in range(d):
                P.dma(vv[:, r * NBs:(r + 1) * NBs, :], vsrc[r])
            if d == 1:
                qdv, kdv = qnat, knat
            else:
                P.copy(qd.rr("p (r n) -> p r n", r=d), qnat.rr("p (n r) -> p r n", r=d), eng="pool")
                P.copy(kd.rr("p (r n) -> p r n", r=d), knat.rr("p (n r) -> p r n", r=d), eng="pool")
                qdv, kdv = qd, kd
            Onat = Og[g].rr("p (n r) -> p r n", r=d)
            Dnat = Dg[g].rr("p (n r) -> p r n", r=d)
            for r in range(d):
                for q0 in range(0, NBs, 4):
                    nq = min(4, NBs - q0)
                    W = nq * 128
                    s_c, s_p, o_p, d_p = psc[u % 2], psp[u % 2], po[u % 2], pd[u % 2]
                    p_c, p_p = pcb[u % 2], ppb[u % 2]
                    u += 1
                    base = r * L
                    for b in range(nq):
                        qb = q0 + b
                        qs = slice(base + qb * 128, base + (qb + 1) * 128)
                        P.mm(s_c[:, b * 128:(b + 1) * 128], kdv[:, qs], qdv[:, qs], start=True, stop=True)
                        if qb > 0:
                            ks = slice(base + (qb - 1) * 128, base + qb * 128)
                            P.mm(s_p[:, b * 128:(b + 1) * 128], kdv[:, ks], qdv[:, qs], start=True, stop=True)
                    P.act(p_c[:, 0:W], s_c[:, 0:W], AF.Exp)
                    P.tt(p_c[:, 0:W], p_c[:, 0:W], C.dmask[:, 0, 0:W], ALU.mult, eng="pool")
                    lo = 128 if q0 == 0 else 0
                    if lo:
                        P.memset(p_p[:, 0:128], 0.0)
                    if W > lo:
                        P.act(p_p[:, lo:W], s_p[:, lo:W], AF.Exp)
                        P.tt(p_p[:, lo:W], p_p[:, lo:W], C.dmask[:, 1, lo:W], ALU.mult, eng="pool")
                    for b in range(nq):
                        qb = q0 + b
                        cs = slice(b * 128, (b + 1) * 128)
                        P.mm(o_p[:, cs], vv[:, r * NBs + qb, :], p_c[:, cs], start=True, stop=(qb == 0))
                        if qb > 0:
                            P.mm(o_p[:, cs], vv[:, r * NBs + qb - 1, :], p_p[:, cs], start=False, stop=True)
                    P.mm(d_p[:, 0:W], C.ones_b[:, :], p_c[:, 0:W], start=True, stop=False)
                    P.mm(d_p[:, 0:W], C.ones_b[:, :], p_p[:, 0:W], start=False, stop=True)
                    ns = slice(q0 * 128, q0 * 128 + W)
                    P.copy(Onat[:, r, ns], o_p[:, 0:W], eng="act")
                    P.copy(Dnat[:, r, ns], d_p[:, 0:W], eng="dve")
        P.tt(Dg[0][:, :], Dg[0][:, :], Dg[1][:, :], ALU.add)
        P.tt(Dg[0][:, :], Dg[0][:, :], Dg[2][:, :], ALU.add)
        P.recip(Dg[0][:, :], Dg[0][:, :])
        for g in range(3):
            head = g * 6 + hg
            ob_ = outb[g % 2]
            P.tt(ob_[:, :], Og[g][:, :], Dg[0][:, :], ALU.mult, eng=("pool" if g == 1 else "dve"))
            P.dma(oT[head * 128:(head + 1) * 128, :], ob_[:, :])
    P.end_phase()


GQ0, GK0, GV0, GZ0, GB0 = 0, 2048, 4096, 8192, 12288


def phase_gdn_proj(P, C, hT, w_in, a_log, dt_bias, pre, zs, beta_d, g_d, T):
    P.begin_phase()
    TB = min(512, T)
    hb = [P.sb([128, KC, TB], BF16, "gh") for _ in range(2)]
    wb = [P.sb([128, KC, 512], BF16, "gw") for _ in range(2)]
    wba = P.sb([128, KC, 64], BF16, "gwba")
    pq = [P.ps([128, 512], F32, "gpq") for _ in range(2)]
    pb_ = [P.ps([128, 512], F32, "gpb") for _ in range(2)]
    ob = [P.sb([128, TB], BF16, "gob") for _ in range(3)]
    row = P.sb([1, 64], F32, "grow")
    negea = P.sb([128, 32], F32, "gnegea")
    dtb = P.sb([128, 32], F32, "gdtb")
    bt = [P.sb([128, 32], F32, "gbt") for _ in range(2)]
    tt_ = [P.sb([128, 32], F32, "gtt") for _ in range(2)]
    gt = [P.sb([128, 32], F32, "ggt") for _ in range(2)]
    P.dma(row[:, 0:32], a_log.rearrange("(o n) -> o n", o=1))
    P.dma(row[:, 32:64], dt_bias.rearrange("(o n) -> o n", o=1))
    P.mm(pb_[0][:, 0:64], C.ones_f[0:1, :], row[:, :], start=True, stop=True)
    P.act(negea[:, :], pb_[0][:, 0:32], AF.Exp)
    P.ts(negea[:, :], negea[:, :], -1.0, ALU.mult)
    P.copy(dtb[:, :], pb_[0][:, 32:64], eng="dve")
    load_wpanel(P, wba, w_in, KC, GB0, 64)
    hr = hT.rearrange("(c p) t -> p c t", p=128)
    st = {"w": 0, "p": 0}
    n = 0
    for bi in range(T // TB):
        h = hb[bi % 2]
        tsl = slice(bi * TB, (bi + 1) * TB)
        P.dma(h[:, :, :], hr[:, :, tsl])

        def epi(ci, pb, tsl=tsl):
            o = ob[ci % 3]
            if ci < 64:
                P.copy(o[:, :], pb[:, 0:TB], eng=("act" if ci % 2 else "dve"))
                P.dma(pre[ci * 128:(ci + 1) * 128, tsl], o[:, :])
            else:
                P.act(o[:, :], pb[:, 0:TB], AF.Silu)
                P.dma(zs[(ci - 64) * 128:(ci - 63) * 128, tsl], o[:, :])
        linear_fm(P, h, KC, TB, w_in, 0, GB0, wb, pq, epi, state=st)
        for t4 in range(TB // 128):
            p = pb_[n % 2]
            b_t, t_t, g_t = bt[n % 2], tt_[n % 2], gt[n % 2]
            n += 1
            for kc in range(KC):
                P.mm(p[:, 0:64], h[:, kc, t4 * 128:(t4 + 1) * 128], wba[:, kc, :], start=(kc == 0), stop=(kc == KC - 1))
            rows = slice(bi * TB + t4 * 128, bi * TB + (t4 + 1) * 128)
            P.act(b_t[:, :], p[:, 0:32], AF.Sigmoid)
            P.dma(beta_d[rows, :], b_t[:, :])
            P.tt(t_t[:, :], p[:, 32:64], dtb[:, :], ALU.add)
            P.act(t_t[:, :], t_t[:, :], AF.Exp)
            P.act(t_t[:, :], t_t[:, :], AF.Ln, bias=C.one_col[:, :])
            P.tt(g_t[:, :], t_t[:, :], negea[:, :], ALU.mult)
            P.dma(g_d[rows, :], g_t[:, :])
    P.end_phase()


def phase_gdn_conv(P, C, pre, conv_w, post, T):
    P.begin_phase()
    TT = min(512, T)
    NCHN = 64
    cw = P.sb([128, 4, NCHN], F32, "ccw")
    for k in range(4):
        P.dma(cw[:, k, :], conv_w[k].rearrange("(c p) -> p c", p=128), allow_slow_non_contiguous=True)
    xp = [P.sb([128, T + 3], BF16, "cxp") for _ in range(2)]
    acc = [P.sb([128, T], F32, "cacc") for _ in range(2)]
    sl_ = [P.sb([128, T], F32, "csl") for _ in range(2)]
    ob = [P.sb([128, T], BF16, "cob") for _ in range(2)]
    sqb = [P.sb([128, TT], F32, "csq") for _ in range(2)]
    lnb = P.sb([128, TT], F32, "cln")
    rsb = [P.sb([128, TT], F32, "crs") for _ in range(2)]
    pss = [P.ps([128, 512], F32, "cpss") for _ in range(2)]
    for i in range(2):
        P.memset(xp[i][:, 0:3], 0.0)
    n = 0
    for ch in range(NCHN):
        x, a, s_, o = xp[ch % 2], acc[ch % 2], sl_[ch % 2], ob[ch % 2]
        P.dma(x[:, 3:T + 3], pre[ch * 128:(ch + 1) * 128, :])
        P.ts(a[:, :], x[:, 0:T], cw[:, 0, ch:ch + 1], ALU.mult)
        for k in range(1, 4):
            P.stt(a[:, :], x[:, k:T + k], cw[:, k, ch:ch + 1], a[:, :], ALU.mult, ALU.add)
        if ch >= 32:
            P.act(o[:, :], a[:, :], AF.Silu)
        else:
            P.act(s_[:, :], a[:, :], AF.Silu)
            sc = (128.0 ** -0.5) if ch < 16 else 1.0
            for ti in range(T // TT):
                tsl = slice(ti * TT, (ti + 1) * TT)
                sq, ps, rs = sqb[n % 2], pss[n % 2], rsb[n % 2]
                n += 1
                P.tt(sq[:, :], s_[:, tsl], s_[:, tsl], ALU.mult, eng="pool")
                P.mm(ps[:, 0:TT], C.ones_f[:, :], sq[:, :], start=True, stop=True)
                P.act(lnb[:, :], ps[:, 0:TT], AF.Ln, bias=C.eps_col[:, :])
                P.act(rs[:, :], lnb[:, :], AF.Exp, scale=-0.5)
                P.stt(o[:, tsl], s_[:, tsl], sc, rs[:, :], ALU.mult, ALU.mult)
        P.dma(post[ch * 128:(ch + 1) * 128, :], o[:, :])
    P.end_phase()


def phase_gdn_core(P, C, post, zs, beta_d, g_d, onorm_d, oT, T):
    P.begin_phase()
    GE = DBG_GE
    NCH = T // 128
    TT = min(256, T)
    W = NCH * 32
    G = 6
    f = lambda nm: P.sb([128, NCH, 32], F32, nm)
    beta, gg, gc, c2, nbeta, dcol, egl, glb = f("kb"), f("kg"), f("kgc"), f("kc2"), f("knb"), f("kdc"), f("keg"), f("kgl")
    gon = P.sb([128, 1], F32, "kgon")
    P.dma(gon[:, :], onorm_d.rearrange("(p o) -> p o", o=1))
    P.dma(beta[:, :, :], beta_d.rearrange("(c p) n -> p c n", p=128))
    P.dma(gg[:, :, :], g_d.rearrange("(c p) n -> p c n", p=128))
    bF = [P.ps([128, 512], F32, "kF%d" % g) for g in range(G)]
    tC = P.ps([128, 512], F32, "kC")
    tN = P.ps([128, 512], F32, "kN")
    pKV = [bF[g][:, 0:128].bc(BF16) for g in range(G)]
    pNA = pKV
    pGR = [bF[g][:, 128:256] for g in range(G)]
    pD = [bF[g][:, 256:512] for g in range(G)]
    pA = [bF[g][:, 0:256] for g in range(G)]
    pWS = tC[:, 0:128]
    pSU = tC[:, 128:256]
    pNrm = tN[:, 0:256]
    pOT = tN[:, 384:512]
    gflat = gg.rr("p c n -> p (c n)")
    gcflat = gc.rr("p c n -> p (c n)")
    glflat = glb.rr("p c n -> p (c n)")
    for c0 in range(0, W, 512):
        w_ = min(512, W - c0)
        P.mm(tC[:, 0:w_], C.triu_f[:, :], gflat[:, c0:c0 + w_], start=True, stop=True)
        P.copy(gcflat[:, c0:c0 + w_], tC[:, 0:w_], eng="dve")
    for c0 in range(0, W, 512):
        w_ = min(512, W - c0)
        P.mm(tC[:, 0:w_], C.sellast_f[:, :], gcflat[:, c0:c0 + w_], start=True, stop=True)
        P.copy(glflat[:, c0:c0 + w_], tC[:, 0:w_], eng="dve")
    P.act(c2[:, :, :], gc[:, :, :], AF.Exp)
    P.tt(c2[:, :, :], c2[:, :, :], beta[:, :, :], ALU.mult)
    P.ts(nbeta[:, :, :], beta[:, :, :], -1.0, ALU.mult)
    P.tt(dcol[:, :, :], glb[:, :, :], gc[:, :, :], ALU.subtract)
    P.act(dcol[:, :, :], dcol[:, :, :], AF.Exp)
    P.act(egl[:, :, :], glb[:, :, :], AF.Exp)

    kT = [P.sb([128, T], BF16, "kkT") for _ in range(2)]
    qT = [P.sb([128, T], BF16, "kqT") for _ in range(2)]
    vT = [P.sb([128, T], BF16, "kvT") for _ in range(2)]
    zb = [P.sb([128, T], BF16, "kzb") for _ in range(2)]
    oh = P.sb([128, T], F32, "koh")
    outb = [P.sb([128, T], BF16, "koutb") for _ in range(2)]
    S32 = P.sb([128, 128], F32, "kS32")
    Sbf = P.sb([128, 128], BF16, "kSbf")
    nbG = lambda shape, dt, nm: [P.sb(shape, dt, nm) for _ in range(G)]
    nb2G = lambda shape, dt, nm: [P.sb(shape, dt, nm) for _ in range(2 * G)]
    R = nbG([128, 256], BF16, "kR")
    Gb = nbG([128, 128], F32, "kGb")
    growS = nbG([128, 128], F32, "kgrowS")
    Dn = nbG([128, 128], F32, "kDn")
    E = nbG([128, 128], F32, "kE")
    Eg = nbG([128, 128], F32, "kEg")
    M0f = nbG([128, 128], F32, "kM0f")
    Atf = nbG([128, 128], F32, "kAtf")
    Atb = nbG([128, 128], BF16, "kAtb")
    MN = nbG([128, 256], BF16, "kMN")
    TTm = nbG([128, 256], BF16, "kTT")
    Xs = nbG([128, 256], BF16, "kXs")
    Kdec = nb2G([128, 128], BF16, "kKd")
    QgT = nb2G([128, 128], BF16, "kQg")
    AtT = nb2G([128, 128], BF16, "kAtT")
    UW = nb2G([128, 256], BF16, "kUW")
    wTb = nb2G([128, 128], BF16, "kwT")
    vnew = nb2G([128, 128], BF16, "kvn")
    sqb = [P.sb([128, TT], F32, "ksq") for _ in range(2)]
    lnb = P.sb([128, TT], F32, "kln")
    rsb = [P.sb([128, TT], F32, "krs") for _ in range(2)]
    tmpb = [P.sb([128, TT], F32, "ktmp") for _ in range(2)]

    def stages(hv, c, g, s2):
        hp = hv % 2
        sl = slice(c * 128, (c + 1) * 128)
        kt, qt, vt = kT[hp], qT[hp], vT[hp]
        kv, na, dd, aa, gr = pKV[g], pNA[g], pD[g], pA[g], pGR[g]
        L = []

        def st0():
            P.transpose(kv[:, 0:128], kt[:, sl], C.ident_b[:, :])
            P.transpose(kv[:, 128:256], vt[:, sl], C.ident_b[:, :])
            P.ts(Gb[g][:, :], C.ones_f[:, :], gg[:, c, hv:hv + 1], ALU.mult)
        L.append(st0)

        def st1():
            P.ts(R[g][:, 0:128], kv[:, 128:256], beta[:, c, hv:hv + 1], ALU.mult)
            P.ts(R[g][:, 128:256], kv[:, 0:128], c2[:, c, hv:hv + 1], ALU.mult)
            P.ts(Kdec[s2][:, :], kv[:, 0:128], dcol[:, c, hv:hv + 1], ALU.mult)
            P.mm(gr[:, 0:128], Gb[g][:, :], C.triu_f[:, :], start=True, stop=True)
            P.mm(dd[:, 0:128], kt[:, sl], kt[:, sl], start=True, stop=True)
            P.mm(dd[:, 128:256], qt[:, sl], kt[:, sl], start=True, stop=True)
        L.append(st1)

        def st2():
            P.copy(growS[g][:, :], gr[:, 0:128], eng=GE)
        L.append(st2)

        def st3():
            P.ts(Dn[g][:, :], growS[g][:, :], gc[:, c, hv:hv + 1], ALU.subtract, 0.0, ALU.max)
            P.act(Eg[g][:, :], growS[g][:, :], AF.Exp)
        L.append(st3)

        def st4():
            P.act(E[g][:, :], Dn[g][:, :], AF.Exp, scale=-1.0)
            P.tt(QgT[s2][:, :], qt[:, sl], Eg[g][:, :], ALU.mult, eng="pool")
        L.append(st4)

        def st5():
            P.stt(M0f[g][:, :], dd[:, 0:128], nbeta[:, c, hv:hv + 1], E[g][:, :], ALU.mult, ALU.mult)
            P.tt(Atf[g][:, :], dd[:, 128:256], E[g][:, :], ALU.mult)
        L.append(st5)

        def st6():
            P.tt(MN[g][:, 0:128], M0f[g][:, :], C.maskSL_f[:, :], ALU.mult, eng="pool")
            P.tt(Atb[g][:, :], Atf[g][:, :], C.maskLI_f[:, :], ALU.mult, eng="pool")
        L.append(st6)

        def st7():
            P.transpose(na[:, 0:128], MN[g][:, 0:128], C.ident_b[:, :])
            P.transpose(na[:, 128:256], Atb[g][:, :], C.ident_b[:, :])
        L.append(st7)

        def st8():
            P.copy(MN[g][:, 128:256], na[:, 0:128], eng=GE)
            P.copy(AtT[s2][:, :], na[:, 128:256], eng=GE)
        L.append(st8)
        for l in range(7):
            def sx(l=l):
                Tb = C.ident_b[:, :] if l == 0 else TTm[g][:, 0:128]
                Tbt = C.ident_b[:, :] if l == 0 else TTm[g][:, 128:256]
                P.mm(dd[:, 0:128], MN[g][:, 128:256], Tb, start=True, stop=True)
                P.mm(dd[:, 128:256], MN[g][:, 0:128], Tbt, start=True, stop=True)
            L.append(sx)

            def sm(l=l):
                P.tt(Xs[g][:, :], dd[:, 0:256], C.lvmask[:, l, :], ALU.mult)
            L.append(sm)

            def sy(l=l):
                Tb = C.ident_b[:, :] if l == 0 else TTm[g][:, 0:128]
                Tbt = C.ident_b[:, :] if l == 0 else TTm[g][:, 128:256]
                P.mm(aa[:, 0:128], Tbt, Xs[g][:, 0:128], start=True, stop=False)
                P.mm(aa[:, 0:128], C.ident_b[:, :], Tb, start=False, stop=True)
                P.mm(aa[:, 128:256], Tb, Xs[g][:, 128:256], start=True, stop=False)
                P.mm(aa[:, 128:256], C.ident_b[:, :], Tbt, start=False, stop=True)
            L.append(sy)

            def sc(l=l):
                P.copy(TTm[g][:, :], aa[:, 0:256], eng=GE)
            L.append(sc)

        def su():
            P.mm(dd[:, 0:256], TTm[g][:, 128:256], R[g][:, :], start=True, stop=True)
        L.append(su)

        def su2():
            P.copy(UW[s2][:, :], dd[:, 0:256], eng="dve")
        L.append(su2)

        def sw():
            P.transpose(na[:, 0:128], UW[s2][:, 128:256], C.ident_b[:, :])
        L.append(sw)

        def sw2():
            P.copy(wTb[s2][:, :], na[:, 0:128], eng=GE)
        L.append(sw2)
        return L

    def chain(hv, c, s2):
        sl = slice(c * 128, (c + 1) * 128)
        P.mm(pWS[:, 0:128], wTb[s2][:, :], Sbf[:, :], start=True, stop=True)
        P.tt(vnew[s2][:, :], UW[s2][:, 0:128], pWS[:, 0:128], ALU.subtract)
        P.mm(pOT[:, 0:128], Sbf[:, :], QgT[s2][:, :], start=True, stop=False)
        P.mm(pOT[:, 0:128], vnew[s2][:, :], AtT[s2][:, :], start=False, stop=True)
        P.copy(oh[:, sl], pOT[:, 0:128], eng=GE)
        P.mm(pSU[:, 0:128], Kdec[s2][:, :], vnew[s2][:, :], start=True, stop=True)
        P.stt(S32[:, :], S32[:, :], egl[:, c, hv:hv + 1], pSU[:, 0:128], ALU.mult, ALU.add)
        P.copy(Sbf[:, :], S32[:, :], eng="act")

    def load_head(hv):
        hp, hk = hv % 2, hv // 2
        P.dma(kT[hp][:, :], post[GK0 + hk * 128: GK0 + (hk + 1) * 128, :])
        P.dma(qT[hp][:, :], post[GQ0 + hk * 128: GQ0 + (hk + 1) * 128, :])
        P.dma(vT[hp][:, :], post[GV0 + hv * 128: GV0 + (hv + 1) * 128, :])

    def prep_group(gi):
        grp = groups[gi]
        lists = [stages(hv, c, g, (gi % 2) * G + g) for g, (hv, c) in enumerate(grp)]
        for si in range(len(lists[0])):
            for L in lists:
                L[si]()

    pairs = [(hv, c) for hv in range(32) for c in range(NCH)]
    groups = [pairs[i:i + G] for i in range(0, len(pairs), G)]
    n = 0
    for (nh, ncc) in groups[0]:
        if ncc == 0:
            load_head(nh)
    prep_group(0)
    for gi, grp in enumerate(groups):
        if gi + 1 < len(groups):
            for (nh, ncc) in groups[gi + 1]:
                if ncc == 0:
                    load_head(nh)
            prep_group(gi + 1)
        for g, (hv, c) in enumerate(grp):
            hp = hv % 2
            if c == 0:
                P.memset(S32[:, :], 0.0, eng="dve")
                P.memset(Sbf[:, :], 0.0, eng="dve")
                P.dma(zb[hp][:, :], zs[hv * 128:(hv + 1) * 128, :])
            chain(hv, c, (gi % 2) * G + g)
            if c == NCH - 1:
                ob_ = outb[hp]
                for ti in range(T // TT):
                    tsl = slice(ti * TT, (ti + 1) * TT)
                    sq, rs, tm = sqb[n % 2], rsb[n % 2], tmpb[n % 2]
                    n += 1
                    P.act(sq[:, :], oh[:, tsl], AF.Square)
                    P.mm(pNrm[:, 0:TT], C.ones_f[:, :], sq[:, :], start=True, stop=True)
                    P.act(lnb[:, :], pNrm[:, 0:TT], AF.Ln, bias=C.eps_col[:, :], scale=1.0 / 128)
                    P.act(rs[:, :], lnb[:, :], AF.Exp, scale=-0.5)
                    P.stt(tm[:, :], oh[:, tsl], gon[:, 0:1], rs[:, :], ALU.mult, ALU.mult)
                    P.tt(ob_[:, tsl], tm[:, :], zb[hp][:, tsl], ALU.mult, eng="pool")
                P.dma(oT[hv * 128:(hv + 1) * 128, :], ob_[:, :])
    P.end_phase()


W_NAMES = ["mix_norm", "ffn_norm", "ffn_w_gu", "ffn_w_down", "sb_w_in", "sb_q_norm", "sb_k_norm", "sb_w_out",
           "gdn_w_in", "gdn_conv_w", "gdn_a_log", "gdn_dt_bias", "gdn_o_norm", "gdn_w_out",
           "dsw_w_in", "dsw_q_norm", "dsw_k_norm", "dsw_w_out",
           "lru_w_in", "lru_conv_w", "lru_conv_b", "lru_w_a", "lru_b_a", "lru_w_x", "lru_b_x", "lru_lambda", "lru_w_out"]


def build(T, layers, shapes):
    nc = bass.Bass("TRN2", target_bir_lowering=False)
    es = ExitStack()
    es.__enter__()
    dr = {}
    dr["xT"] = nc.dram_tensor("xT", [D, T], F32, kind="ExternalInput").ap()
    dr["cf"] = nc.dram_tensor("cf", list(shapes["cf"]), F32, kind="ExternalInput").ap()
    for nm in W_NAMES:
        dr[nm] = nc.dram_tensor(nm, list(shapes[nm]), F32, kind="ExternalInput").ap()
    outT = nc.dram_tensor("outT", [D, T], F32, kind="ExternalOutput").ap()
    hT = nc.dram_tensor("hT", [D, T], BF16, kind="Internal").ap()
    dr["pos"] = nc.dram_tensor("pos", [1, T], I32, kind="ExternalInput").ap()
    s_q = nc.dram_tensor("s_q", [4096, T], BF16, kind="Internal").ap()
    s_k = nc.dram_tensor("s_k", [4096, T], BF16, kind="Internal").ap()
    s_v = nc.dram_tensor("s_v", [T, 4096], BF16, kind="Internal").ap()
    s_o = nc.dram_tensor("s_o", [4096, T], BF16, kind=("ExternalOutput" if DBG_OUT else "Internal")).ap()
    dk = "ExternalOutput" if DBG_OUT else "Internal"
    s_pre = nc.dram_tensor("s_pre", [8192, T], BF16, kind="Internal").ap()
    s_post = nc.dram_tensor("s_post", [8192, T], BF16, kind=dk).ap()
    s_bt = nc.dram_tensor("s_bt", [T, 32], F32, kind=dk).ap()
    s_g = nc.dram_tensor("s_g", [T, 32], F32, kind=dk).ap()
    s_ct = nc.dram_tensor("s_ct", [128, T], F32, kind="Internal").ap()
    s_st = nc.dram_tensor("s_st", [128, T], F32, kind="Internal").ap()
    s_x = nc.dram_tensor("s_x", [2048, T], F32, kind="Internal").ap()
    P = Prog(nc, es)
    C = setup_consts(P, nc, es, dr)
    P.begin_phase()
    cb = [P.sb([128, min(T, 2048)], F32, "cp") for _ in range(2)]
    TC = min(T, 2048)
    i = 0
    for c in range(KC):
        for t0 in range(0, T, TC):
            b = cb[i % 2]
            i += 1
            P.dma(b[:, :], dr["xT"][c * 128:(c + 1) * 128, t0:t0 + TC])
            P.dma(outT[c * 128:(c + 1) * 128, t0:t0 + TC], b[:, :])
    P.end_phase()
    x = outT
    for li in layers:
        kind = li % 4
        phase_norm(P, C, x, dr["mix_norm"][li], hT, T)
        if kind == 0:
            phase_sb_proj(P, C, hT, dr["sb_w_in"][0], dr["sb_q_norm"][0], dr["sb_k_norm"][0], s_q, s_k, s_v, T)
            phase_sb_attn(P, C, s_q, s_k, s_v, s_o, T)
            phase_outproj(P, C, x, s_o, 16, dr["sb_w_out"][0], T)
        elif kind == 1:
            phase_gdn_proj(P, C, hT, dr["gdn_w_in"][0], dr["gdn_a_log"][0], dr["gdn_dt_bias"][0], s_pre, s_q, s_bt, s_g, T)
            if DBG_STOP >= 2:
                phase_gdn_conv(P, C, s_pre, dr["gdn_conv_w"][0], s_post, T)
            if DBG_STOP >= 3:
                phase_gdn_core(P, C, s_post, s_q, s_bt, s_g, dr["gdn_o_norm"][0], s_o, T)
            if DBG_STOP >= 4:
                phase_outproj(P, C, x, s_o, 32, dr["gdn_w_out"][0], T)
        elif kind == 2:
            phase_rope_tables(P, C, dr["pos"], s_ct, s_st, T)
            phase_dsw_proj(P, C, hT, dr["dsw_w_in"][0], dr["dsw_q_norm"][0], dr["dsw_k_norm"][0], s_ct, s_st, s_q, s_k, s_v, T)
            phase_dsw_attn(P, C, s_q, s_k, s_v, s_o, T)
            phase_outproj(P, C, x, s_o, 18, dr["dsw_w_out"][0], T)
        elif kind == 3:
            phase_lru_proj(P, C, hT, dr["lru_w_in"][0], s_q, s_x, T)
            phase_lru_rec(P, C, s_q, s_x, s_o, dr["lru_conv_w"][0], dr["lru_conv_b"][0], dr["lru_w_a"][0], dr["lru_b_a"][0],
                          dr["lru_w_x"][0], dr["lru_b_x"][0], dr["lru_lambda"][0], T)
            phase_outproj(P, C, x, s_o, 16, dr["lru_w_out"][0], T)
        if not DBG_SKIP_FFN:
            phase_norm(P, C, x, dr["ffn_norm"][li], hT, T)
            phase_ffn(P, C, x, hT, dr["ffn_w_gu"][li], dr["ffn_w_down"][li], T)
    es.__exit__(None, None, None)
    return nc, P


def kernel(**inputs):
    T = inputs["x"].shape[1]
    B = inputs["x"].shape[0]
    consts = host_consts()
    shapes = {k: np.shape(inputs[k]) for k in W_NAMES}
    shapes["cf"] = consts["cf"].shape
    nc, P = build(T, list(range(4)), shapes)
    in_maps = []
    for b in range(B):
        m = {"xT": np.ascontiguousarray(inputs["x"][b].T), "cf": consts["cf"],
             "pos": np.ascontiguousarray(inputs["positions"][b:b + 1]).astype(np.int32)}
        for nm in W_NAMES:
            m[nm] = np.ascontiguousarray(inputs[nm], dtype=np.float32)
        in_maps.append(m)
    res = run_bass_kernel_spmd(nc, in_maps, core_ids=list(range(B)))
    out = np.stack([np.ascontiguousarray(r["outT"].T) for r in res.results], axis=0)
    return out.astype(np.float32)
```

```python
import numpy as np
from contextlib import ExitStack
import concourse.bass as bass
import concourse.mybir as mybir
from concourse.bass_utils import run_bass_kernel_spmd

F32 = mybir.dt.float32
BF16 = mybir.dt.bfloat16
I32 = mybir.dt.int32
AF = mybir.ActivationFunctionType
ALU = mybir.AluOpType

DBG_STOP = 9
DBG_CORE = 9
DBG_STEP = 9
SAME_ENG_RAW = False
DBG_GE = "act"
DBG_SKIP_FFN = False
DBG_OUT = False
DBG_K = 9
DBG_K2 = 0
D = 2048
KC = D // 128
FFN_H = 5632
EPS = 1e-6


class View:
    __slots__ = ("buf", "ap")

    def __init__(self, buf, ap):
        self.buf = buf
        self.ap = ap

    def __getitem__(self, idx):
        return View(self.buf, self.ap[idx])

    def rr(self, pat, **kw):
        return View(self.buf, self.ap.rearrange(pat, **kw))

    def bc(self, dt):
        return View(self.buf, self.ap.bitcast(dt))


class Buf:
    def __init__(self, t, name):
        self.t = t
        self.name = name
        self.writers = []
        self.readers = []
        self.sem = None
        self.semname = None
        self.psum = False

    def __getitem__(self, idx):
        return View(self, self.t[idx])

    def rr(self, pat, **kw):
        return View(self, self.t[:].rearrange(pat, **kw))


class Op:
    __slots__ = ("eng", "fn", "waits", "sem", "val", "inc", "is_dma")


def _ap(v):
    return v.ap if isinstance(v, View) else v


class Prog:
    CE = ("pe", "act", "dve", "pool")
    ALLE = ("pe", "act", "dve", "pool", "sp")

    def __init__(self, nc, es):
        self.nc = nc
        self.es = es
        self.esem = {e: es.enter_context(nc.semaphore("cs_" + e)) for e in self.CE}
        self.ecnt = {e: 0 for e in self.CE}
        self.semcnt = {}
        self.semobj = {}
        self.free_dma_sems = []
        self.n_dma_sems = 0
        self.known = {e: {} for e in self.ALLE}
        self.ops = {e: [] for e in self.ALLE}
        self.phase_sems = set()
        self.pes = None
        self.nops = 0
        self.uid = 0

    def begin_phase(self):
        self.pes = ExitStack()
        self.pes.__enter__()
        self.phase_bufs = []

    def sb(self, shape, dt, name=None):
        self.uid += 1
        name = (name or "b") + "_%d" % self.uid
        t = self.pes.enter_context(self.nc.sbuf_tensor(name, list(shape), dt))
        b = Buf(t, name)
        self.phase_bufs.append(b)
        return b

    def ps(self, shape, dt=F32, name=None):
        self.uid += 1
        name = (name or "p") + "_%d" % self.uid
        t = self.pes.enter_context(self.nc.psum_tensor(name, list(shape), dt))
        b = Buf(t, name)
        b.psum = True
        self.phase_bufs.append(b)
        return b

    def _dma_sem(self, buf):
        if buf.sem is None:
            if self.free_dma_sems:
                nm = self.free_dma_sems.pop()
            else:
                nm = "ds_%d" % self.n_dma_sems
                self.n_dma_sems += 1
                self.semobj[nm] = self.es.enter_context(self.nc.semaphore(nm))
                self.semcnt[nm] = 0
            buf.sem = self.semobj[nm]
            buf.semname = nm
        return buf.semname

    def _add(self, eng, fn, ins, outs, dma_sig=None):
        op = Op()
        op.eng = eng
        op.fn = fn
        op.is_dma = dma_sig is not None
        deps = []
        rb = []
        wb = []
        for v in ins:
            if isinstance(v, View) and v.buf not in rb:
                rb.append(v.buf)
        for v in outs:
            if isinstance(v, View) and v.buf not in wb:
                wb.append(v.buf)
        for b in rb:
            for w in b.writers:
                if w.eng == eng and not w.is_dma and not op.is_dma and (eng == "pe" or not SAME_ENG_RAW):
                    continue
                deps.append(w)
            if b.psum:
                for r in b.readers:
                    if r.eng != eng:
                        deps.append(r)
        for b in wb:
            for r in b.readers:
                if r.eng == eng and not r.is_dma and not op.is_dma:
                    continue
                deps.append(r)
            for w in b.writers:
                if (w.is_dma and op.is_dma) or (w.eng == eng and not w.is_dma and not op.is_dma):
                    continue
                deps.append(w)
        waits = {}
        for d in deps:
            k = d.sem
            if waits.get(k, (None, 0))[1] < d.val:
                waits[k] = (d.sem, d.val)
        kn = self.known[eng]
        op.waits = []
        for k, (s, v) in waits.items():
            if kn.get(k, 0) >= v:
                continue
            kn[k] = v
            op.waits.append((s, v))
        if op.is_dma:
            nm = self._dma_sem(dma_sig)
            self.semcnt[nm] += 1
            op.sem = nm
            op.val = 16 * self.semcnt[nm]
            op.inc = 16
            self.phase_sems.add(nm)
        else:
            self.ecnt[eng] += 1
            op.sem = "cs_" + eng
            op.val = self.ecnt[eng]
            op.inc = 1
        for b in rb:
            b.readers.append(op)
        for b in wb:
            if b.readers:
                b.readers = []
                b.writers = [op]
            else:
                if not op.is_dma:
                    b.writers = [w for w in b.writers if w.is_dma or w.eng != eng]
                b.writers.append(op)
        self.ops[eng].append(op)
        self.nops += 1
        return op

    def _sem(self, name):
        if name.startswith("cs_"):
            return self.esem[name[3:]]
        return self.semobj[name]

    def end_phase(self):
        finals = []
        for e in self.CE:
            if self.ecnt[e] > 0:
                finals.append(("cs_" + e, self.ecnt[e]))
        for nm in sorted(self.phase_sems):
            finals.append((nm, 16 * self.semcnt[nm]))
        ops = self.ops
        known = self.known
        P = self

        def emit(eng_name):
            def f(e):
                for op in ops[eng_name]:
                    for (s, v) in op.waits:
                        e.wait_ge(P._sem(s), v)
                    ins = op.fn(e)
                    ins.then_inc(P._sem(op.sem), op.inc)
                for (s, v) in finals:
                    if known[eng_name].get(s, 0) < v:
                        known[eng_name][s] = v
                        e.wait_ge(P._sem(s), v)
            return f

        with self.nc.Block() as blk:
            blk.tensor(emit("pe"))
            blk.scalar(emit("act"))
            blk.vector(emit("dve"))
            blk.gpsimd(emit("pool"))
            blk.sync(emit("sp"))
        for e in self.ALLE:
            self.ops[e] = []
        for b in self.phase_bufs:
            if b.semname is not None:
                self.free_dma_sems.append(b.semname)
        self.phase_sems = set()
        self.pes.__exit__(None, None, None)
        self.pes = None

    def dma(self, out, in_, q="sp", **kw):
        sig = out.buf if isinstance(out, View) else in_.buf
        o, i = _ap(out), _ap(in_)
        return self._add(q, lambda e: e.dma_start(out=o, in_=i, **kw), [in_], [out], dma_sig=sig)

    def mm(self, out, lhsT, rhs, start=True, stop=True, **kw):
        o, l, r = _ap(out), _ap(lhsT), _ap(rhs)
        return self._add("pe", lambda e: e.matmul(o, l, r, start=start, stop=stop, **kw), [lhsT, rhs], [out])

    def transpose(self, out, in_, ident):
        o, i, d = _ap(out), _ap(in_), _ap(ident)
        return self._add("pe", lambda e: e.transpose(o, i, d), [in_, ident], [out])

    def act(self, out, in_, func, bias=None, scale=1.0, accum_out=None, eng="act"):
        o, i = _ap(out), _ap(in_)
        kw = {}
        ins = [in_]
        outs = [out]
        if bias is not None:
            kw["bias"] = _ap(bias)
            ins.append(bias)
        if isinstance(scale, View):
            kw["scale"] = _ap(scale)
            ins.append(scale)
        else:
            kw["scale"] = scale
        if accum_out is not None:
            kw["accum_out"] = _ap(accum_out)
            outs.append(accum_out)
        return self._add("act", lambda e: e.activation(out=o, in_=i, func=func, **kw), ins, outs)

    def tt(self, out, in0, in1, op, eng="dve"):
        o, a, b = _ap(out), _ap(in0), _ap(in1)
        return self._add(eng, lambda e: e.tensor_tensor(out=o, in0=a, in1=b, op=op), [in0, in1], [out])

    def ts(self, out, in0, s1, op0, s2=None, op1=None, eng="dve"):
        o, a = _ap(out), _ap(in0)
        ins = [in0]
        if isinstance(s1, View):
            ins.append(s1)
        if isinstance(s2, View):
            ins.append(s2)
        x1, x2 = _ap(s1), _ap(s2)
        if op1 is None:
            return self._add(eng, lambda e: e.tensor_scalar(out=o, in0=a, scalar1=x1, scalar2=None, op0=op0), ins, [out])
        return self._add(eng, lambda e: e.tensor_scalar(out=o, in0=a, scalar1=x1, scalar2=x2, op0=op0, op1=op1), ins, [out])

    def stt(self, out, in0, scalar, in1, op0, op1):
        o, a, b = _ap(out), _ap(in0), _ap(in1)
        ins = [in0, in1]
        if isinstance(scalar, View):
            ins.append(scalar)
        s = _ap(scalar)
        return self._add("dve", lambda e: e.scalar_tensor_tensor(out=o, in0=a, scalar=s, in1=b, op0=op0, op1=op1), ins, [out])

    def copy(self, out, in_, eng="dve"):
        o, i = _ap(out), _ap(in_)
        if eng == "act":
            return self._add("act", lambda e: e.activation(out=o, in_=i, func=AF.Copy), [in_], [out])
        return self._add(eng, lambda e: e.tensor_copy(out=o, in_=i), [in_], [out])

    def memset(self, out, val, eng="pool"):
        o = _ap(out)
        return self._add(eng, lambda e: e.memset(o, val), [], [out])

    def recip(self, out, in_):
        o, i = _ap(out), _ap(in_)
        return self._add("dve", lambda e: e.reciprocal(out=o, in_=i), [in_], [out])

    def scan(self, out, d0, d1, initial, op0, op1):
        o, a, b = _ap(out), _ap(d0), _ap(d1)
        ins = [d0, d1]
        if isinstance(initial, View):
            ins.append(initial)
        ini = _ap(initial)
        return self._add("dve", lambda e: e.tensor_tensor_scan(out=o, data0=a, data1=b, initial=ini, op0=op0, op1=op1), ins, [out])


class Ctx:
    pass


def setup_consts(P, nc, es, cdram):
    C = Ctx()

    def sbg(name, shape, dt):
        t = es.enter_context(nc.sbuf_tensor(name, list(shape), dt))
        return Buf(t, name)

    C.ones_f = sbg("ones_f", [128, 128], F32)
    C.ones_b = sbg("ones_b", [128, 128], BF16)
    C.ident_b = sbg("ident_b", [128, 128], BF16)
    C.ident_f = sbg("ident_f", [128, 128], F32)
    C.negtri_b = sbg("negtri_b", [128, 128], BF16)
    C.sbmask = sbg("sbmask", [128, 4, 512], BF16)
    C.perm_b = sbg("perm_b", [128, 128], BF16)
    C.dmask = sbg("dmask", [128, 2, 512], BF16)
    C.invf = sbg("invf", [128, 2], F32)
    C.triu_f = sbg("triu_f", [128, 128], F32)
    C.sellast_f = sbg("sellast_f", [128, 128], F32)
    C.maskSL_f = sbg("maskSL_f", [128, 128], F32)
    C.maskLI_f = sbg("maskLI_f", [128, 128], F32)
    C.lvmask = sbg("lvmask", [128, 7, 256], BF16)
    C.eps_col = sbg("eps_col", [128, 1], F32)
    C.one_col = sbg("one_col", [128, 1], F32)
    P.begin_phase()
    stage = P.sb([128, CF_W], F32, "cstage")
    P.dma(stage[:, :], cdram["cf"][:, :])
    P.copy(C.ones_f[:, :], stage[:, 0:128])
    P.copy(C.ones_b[:, :], stage[:, 0:128])
    P.copy(C.ident_f[:, :], stage[:, 128:256])
    P.copy(C.ident_b[:, :], stage[:, 128:256])
    P.copy(C.negtri_b[:, :], stage[:, 256:384])
    P.copy(C.sbmask[:, :, :], stage[:, 640:640 + 2048].rr("p (j c) -> p j c", j=4))
    P.copy(C.perm_b[:, :], stage[:, 384:512])
    P.copy(C.dmask[:, :, :], stage[:, 2688:2688 + 1024].rr("p (j c) -> p j c", j=2))
    P.copy(C.invf[:, :], stage[:, 512:514])
    P.copy(C.triu_f[:, :], stage[:, 3712:3840])
    P.copy(C.sellast_f[:, :], stage[:, 3840:3968])
    P.copy(C.maskSL_f[:, :], stage[:, 3968:4096])
    P.copy(C.maskLI_f[:, :], stage[:, 4096:4224])
    P.copy(C.lvmask[:, :, :], stage[:, 4224:4224 + 1792].rr("p (l c) -> p l c", l=7))
    P.memset(C.eps_col[:, :], EPS)
    P.memset(C.one_col[:, :], 1.0)
    P.end_phase()
    return C


CF_W = 640 + 2048 + 1024 + 512 + 1792


def host_consts():
    cf = np.zeros((128, CF_W), np.float32)
    cf[:, 0:128] = 1.0
    cf[:, 128:256] = np.eye(128, dtype=np.float32)
    s = np.arange(128)[:, None]
    sp = np.arange(128)[None, :]
    cf[:, 256:384] = -(s >= sp).astype(np.float32)
    c = np.arange(512)[None, :]
    for j in range(4):
        cf[:, 640 + j * 512: 640 + (j + 1) * 512] = (c > j * 128 + s).astype(np.float32)
    for dd in range(16):
        cf[dd + 16, 384 + dd] = 1.0
        cf[dd, 384 + dd + 16] = 1.0
    invf = (np.float32(500000.0) ** (-np.arange(16, dtype=np.float32) / np.float32(16))).astype(np.float32)
    cf[0:16, 512] = invf
    cf[16:32, 512] = invf
    cf[0:16, 513] = -1.0
    cf[16:32, 513] = 1.0
    cm = np.arange(512)[None, :] % 128
    cf[:, 2688:2688 + 512] = (s <= cm).astype(np.float32)
    cf[:, 2688 + 512:2688 + 1024] = (s >= cm).astype(np.float32)
    cf[:, 3712:3840] = (s <= sp).astype(np.float32)
    cf[127, 3840:3968] = 1.0
    cf[:, 3968:4096] = (s > sp).astype(np.float32)
    cf[:, 4096:4224] = (s >= sp).astype(np.float32)
    for l in range(7):
        b = 1 << l
        mk = ((s // (2 * b) == sp // (2 * b)) & (s % (2 * b) >= b) & (sp % (2 * b) < b)).astype(np.float32)
        cf[:, 4224 + l * 256: 4224 + l * 256 + 128] = mk
        cf[:, 4224 + l * 256 + 128: 4224 + (l + 1) * 256] = mk.T
    return {"cf": cf}


def load_wpanel(P, wbuf, w_dram, kc_n, c0, ncols):
    src = w_dram[:, c0:c0 + ncols].rearrange("(c p) n -> p c n", p=128)
    half = kc_n // 2 if kc_n >= 2 else kc_n
    P.dma(wbuf[:, 0:half, 0:ncols], src[:, 0:half, :], q="pool")
    if half < kc_n:
        P.dma(wbuf[:, half:kc_n, 0:ncols], src[:, half:kc_n, :], q="pool")


def linear_fm(P, src, kc_n, ntok, w_dram, c0, ncols, wbufs, psums, epilogue, panel=512, state=None):
    if state is None:
        state = {"w": 0, "p": 0}
    npan = (ncols + panel - 1) // panel
    ci = 0
    for pi in range(npan):
        pc = min(panel, ncols - pi * panel)
        wb = wbufs[state["w"] % len(wbufs)]
        state["w"] += 1
        load_wpanel(P, wb, w_dram, kc_n, c0 + pi * panel, pc)
        for j in range(pc // 128):
            pb = psums[state["p"] % len(psums)]
            state["p"] += 1
            for kc in range(kc_n):
                P.mm(pb[:, 0:ntok], wb[:, kc, j * 128:(j + 1) * 128], src[:, kc, 0:ntok],
                     start=(kc == 0), stop=(kc == kc_n - 1))
            epilogue(ci, pb)
            ci += 1
    return state


def phase_norm(P, C, xT, g_dram, hT, T):
    P.begin_phase()
    TT = min(512, T)
    gcol = P.sb([128, KC], F32, "gcol")
    P.dma(gcol[:, :], g_dram.rearrange("(c p) -> p c", p=128), allow_slow_non_contiguous=True)
    xb = [P.sb([128, KC, TT], F32, "nx") for _ in range(2)]
    sq = P.sb([128, KC, TT], F32, "nsq")
    hb = [P.sb([128, KC, TT], BF16, "nh") for _ in range(2)]
    ssq = [P.ps([128, 512], F32, "nssq") for _ in range(2)]
    lnv = P.sb([128, TT], F32, "nln")
    rstd = [P.sb([128, TT], F32, "nrstd") for _ in range(2)]
    xr = xT.rearrange("(c p) t -> p c t", p=128)
    hr = hT.rearrange("(c p) t -> p c t", p=128)
    for ti in range(T // TT):
        x = xb[ti % 2]
        h = hb[ti % 2]
        ps = ssq[ti % 2]
        rs = rstd[ti % 2]
        P.dma(x[:, :, :], xr[:, :, ti * TT:(ti + 1) * TT])
        P.act(sq[:, :, :], x[:, :, :], AF.Square)
        for c in range(KC):
            P.mm(ps[:, 0:TT], C.ones_f[:, :], sq[:, c, :], start=(c == 0), stop=(c == KC - 1))
        P.act(lnv[:, :], ps[:, 0:TT], AF.Ln, bias=C.eps_col[:, :], scale=1.0 / D)
        P.act(rs[:, :], lnv[:, :], AF.Exp, scale=-0.5)
        for c in range(KC):
            P.stt(h[:, c, :], x[:, c, :], gcol[:, c:c + 1], rs[:, :], ALU.mult, ALU.mult)
        P.dma(hr[:, :, ti * TT:(ti + 1) * TT], h[:, :, :])
    P.end_phase()


def phase_ffn(P, C, xT, hT, w_gu, w_down, T):
    P.begin_phase()
    TB = min(512, T)
    HC = FFN_H // 128
    hb = [P.sb([128, KC, TB], BF16, "fh") for _ in range(2)]
    actT = P.sb([128, HC, TB], BF16, "fact")
    wg = [P.sb([128, KC, 512], BF16, "fwg") for _ in range(2)]
    wu = [P.sb([128, KC, 512], BF16, "fwu") for _ in range(2)]
    wd = [P.sb([128, HC, 256], BF16, "fwd") for _ in range(2)]
    sg = [P.sb([128, TB], F32, "fsg") for _ in range(2)]
    xt = [P.sb([128, TB], F32, "fx") for _ in range(3)]
    pg = [P.ps([128, 512], F32, "fpg") for _ in range(2)]
    pu = [P.ps([128, 512], F32, "fpu") for _ in range(2)]
    po = [P.ps([128, 512], F32, "fpo") for _ in range(2)]
    hr = hT.rearrange("(c p) t -> p c t", p=128)
    xr = xT.rearrange("(c p) t -> p c t", p=128)
    cnt = 0
    for bi in range(T // TB):
        h = hb[bi % 2]
        P.dma(h[:, :, :], hr[:, :, bi * TB:(bi + 1) * TB])
        for pi in range(FFN_H // 512):
            g_w = wg[pi % 2]
            u_w = wu[pi % 2]
            load_wpanel(P, g_w, w_gu, KC, pi * 512, 512)
            load_wpanel(P, u_w, w_gu, KC, FFN_H + pi * 512, 512)
            for j in range(4):
                fc = pi * 4 + j
                g_p = pg[cnt % 2]
                u_p = pu[cnt % 2]
                s_b = sg[cnt % 2]
                cnt += 1
                for kc in range(KC):
                    P.mm(g_p[:, 0:TB], g_w[:, kc, j * 128:(j + 1) * 128], h[:, kc, :], start=(kc == 0), stop=(kc == KC - 1))
                for kc in range(KC):
                    P.mm(u_p[:, 0:TB], u_w[:, kc, j * 128:(j + 1) * 128], h[:, kc, :], start=(kc == 0), stop=(kc == KC - 1))
                P.act(s_b[:, :], g_p[:, 0:TB], AF.Silu)
                P.tt(actT[:, fc, :], s_b[:, :], u_p[:, 0:TB], ALU.mult)
        for dp in range(D // 256):
            d_w = wd[dp % 2]
            load_wpanel(P, d_w, w_down, HC, dp * 256, 256)
            for j in range(2):
                dc = dp * 2 + j
                o_p = po[dc % 2]
                x_b = xt[dc % 3]
                P.dma(x_b[:, :], xr[:, dc, bi * TB:(bi + 1) * TB])
                for kc in range(HC):
                    P.mm(o_p[:, 0:TB], d_w[:, kc, j * 128:(j + 1) * 128], actT[:, kc, :], start=(kc == 0), stop=(kc == HC - 1))
                P.tt(x_b[:, :], x_b[:, :], o_p[:, 0:TB], ALU.add)
                P.dma(xr[:, dc, bi * TB:(bi + 1) * TB], x_b[:, :])
    P.end_phase()


def phase_outproj(P, C, xT, srcT, kc_n, w_out, T):
    P.begin_phase()
    TB = min(512, T)
    sb_ = [P.sb([128, kc_n, TB], BF16, "os") for _ in range(2)]
    wb = [P.sb([128, kc_n, 512], BF16, "ow") for _ in range(2)]
    xt = [P.sb([128, TB], F32, "ox") for _ in range(3)]
    po = [P.ps([128, 512], F32, "opo") for _ in range(2)]
    sr = srcT[0:kc_n * 128, :].rearrange("(c p) t -> p c t", p=128)
    xr = xT.rearrange("(c p) t -> p c t", p=128)
    st = {"w": 0, "p": 0}
    for bi in range(T // TB):
        s = sb_[bi % 2]
        P.dma(s[:, :, :], sr[:, :, bi * TB:(bi + 1) * TB])

        def epi(ci, pb, bi=bi):
            x_b = xt[ci % 3]
            P.dma(x_b[:, :], xr[:, ci, bi * TB:(bi + 1) * TB])
            P.tt(x_b[:, :], x_b[:, :], pb[:, 0:TB], ALU.add)
            P.dma(xr[:, ci, bi * TB:(bi + 1) * TB], x_b[:, :])
        linear_fm(P, s, kc_n, TB, w_out, 0, D, wb, po, epi, state=st)
    P.end_phase()


def headnorm_epilogue(P, C, pb, ntok, gcol_view, sqb, pss, lnb, rsb, outv):
    P.act(sqb[:, 0:ntok], pb[:, 0:ntok], AF.Square)
    P.mm(pss[:, 0:ntok], C.ones_f[:, :], sqb[:, 0:ntok], start=True, stop=True)
    P.act(lnb[:, 0:ntok], pss[:, 0:ntok], AF.Ln, bias=C.eps_col[:, :], scale=1.0 / 128)
    P.act(rsb[:, 0:ntok], lnb[:, 0:ntok], AF.Exp, scale=-0.5)
    P.stt(outv, pb[:, 0:ntok], gcol_view, rsb[:, 0:ntok], ALU.mult, ALU.mult)


def phase_sb_proj(P, C, hT, w_in, qn_d, kn_d, qT, kT, v, T):
    P.begin_phase()
    TB = min(512, T)
    H = 16
    hb = [P.sb([128, KC, TB], BF16, "sh") for _ in range(2)]
    wb = [P.sb([128, KC, 512], BF16, "sw") for _ in range(2)]
    gq = P.sb([128, 1], F32, "gq")
    gk = P.sb([128, 1], F32, "gk")
    P.dma(gq[:, :], qn_d.rearrange("(p o) -> p o", o=1))
    P.dma(gk[:, :], kn_d.rearrange("(p o) -> p o", o=1))
    P.ts(gq[:, :], gq[:, :], 128.0 ** -0.5, ALU.mult)
    pq = [P.ps([128, 512], F32, "spq") for _ in range(2)]
    pss = [P.ps([128, 512], F32, "spss") for _ in range(2)]
    pv = [P.ps([128, 512], F32, "spv") for _ in range(2)]
    sqb = [P.sb([128, TB], F32, "ssq") for _ in range(2)]
    lnb = P.sb([128, TB], F32, "sln")
    rsb = [P.sb([128, TB], F32, "srs") for _ in range(2)]
    ob = [P.sb([128, TB], BF16, "sob") for _ in range(3)]
    vb = [P.sb([128, 512], BF16, "svb") for _ in range(3)]
    hr = hT.rearrange("(c p) t -> p c t", p=128)
    st = {"w": 0, "p": 0}
    n = [0]
    for bi in range(T // TB):
        h = hb[bi % 2]
        P.dma(h[:, :, :], hr[:, :, bi * TB:(bi + 1) * TB])
        for which, (c0, gcol, dst) in enumerate(((0, gq, qT), (H * 128, gk, kT))):
            def epi(ci, pb, gcol=gcol, dst=dst, bi=bi):
                i = n[0]
                n[0] += 1
                o = ob[i % 3]
                headnorm_epilogue(P, C, pb, TB, gcol[:, 0:1], sqb[i % 2], pss[i % 2], lnb, rsb[i % 2], o[:, 0:TB])
                P.dma(dst[ci * 128:(ci + 1) * 128, bi * TB:(bi + 1) * TB], o[:, 0:TB])
            linear_fm(P, h, KC, TB, w_in, c0, H * 128, wb, pq, epi, state=st)
        for pi in range(H * 128 // 512):
            w = wb[st["w"] % 2]
            st["w"] += 1
            load_wpanel(P, w, w_in, KC, 2 * H * 128 + pi * 512, 512)
            for tt in range(TB // 128):
                i = n[0]
                n[0] += 1
                p = pv[i % 2]
                for kc in range(KC):
                    P.mm(p[:, :], h[:, kc, tt * 128:(tt + 1) * 128], w[:, kc, :], start=(kc == 0), stop=(kc == KC - 1))
                o = vb[i % 3]
                P.copy(o[:, :], p[:, :], eng="act")
                P.dma(v[bi * TB + tt * 128: bi * TB + (tt + 1) * 128, pi * 512:(pi + 1) * 512], o[:, :])
    P.end_phase()


def phase_sb_attn(P, C, qT, kT, v, oT, T):
    P.begin_phase()
    H = 16
    NS = 2
    NB = T // 128
    QT = min(512, T)

    def mkset(i):
        B = {}
        B["q"] = P.sb([128, T], BF16, "aq")
        B["k"] = P.sb([128, T], BF16, "ak")
        B["v"] = P.sb([128, NB, 128], BF16, "av")
        B["pz"] = [P.ps([128, 512], F32, "apz") for _ in range(2)]
        B["pcs"] = P.ps([128, 512], F32, "apcs")
        B["po"] = P.ps([128, 512], F32, "apo")
        B["e"] = [P.sb([128, QT], F32, "ae") for _ in range(2)]
        B["sp"] = [P.sb([128, QT], BF16, "asp") for _ in range(2)]
        B["d"] = [P.sb([128, QT], F32, "ad") for _ in range(2)]
        B["w"] = [P.sb([128, QT], BF16, "aw") for _ in range(2)]
        B["cs"] = [P.sb([128, QT], F32, "acs") for _ in range(2)]
        B["ob"] = [P.sb([128, QT], BF16, "aob") for _ in range(2)]
        return B
    sets = [mkset(i) for i in range(NS)]

    def stream(h, B):
        q, k, vv = B["q"], B["k"], B["v"]
        P.dma(q[:, :], qT[h * 128:(h + 1) * 128, :])
        P.dma(k[:, :], kT[h * 128:(h + 1) * 128, :])
        P.dma(vv[:, :, :], v[:, h * 128:(h + 1) * 128].rearrange("(nb s) d -> s nb d", s=128))
        yield
        u = 0
        for qi in range(T // QT):
            o_p = B["po"]
            cs = B["cs"][qi % 2]
            c_p = B["pcs"]
            nsub = QT // 128
            mlast = qi * nsub + nsub - 1
            for m in range(mlast, -1, -1):
                z = B["pz"][u % 2]
                e_b, s_b, d_b, w_b = B["e"][u % 2], B["sp"][u % 2], B["d"][u % 2], B["w"][u % 2]
                u += 1
                first = (m == mlast)
                diag = m >= qi * nsub
                j = m - qi * nsub
                qv = q[:, qi * QT:(qi + 1) * QT]
                P.mm(z[:, 0:QT], k[:, m * 128:(m + 1) * 128], qv, start=True, stop=False)
                yield
                P.act(e_b[:, :], z[:, 0:QT], AF.Exp)
                yield
                P.act(s_b[:, :], e_b[:, :], AF.Ln, bias=C.one_col[:, :])
                yield
                if diag:
                    P.tt(s_b[:, :], s_b[:, :], C.sbmask[:, j, 0:QT], ALU.mult, eng="pool")
                    yield
                P.mm(z[:, 0:QT], C.negtri_b[:, :], s_b[:, :], start=False, stop=True)
                if m > 0:
                    P.mm(c_p[:, 0:QT], C.ones_b[:, :], s_b[:, :], start=True, stop=True)
                yield
                if first:
                    P.act(w_b[:, :], z[:, 0:QT], AF.Exp)
                    if m > 0:
                        P.copy(cs[:, :], c_p[:, 0:QT], eng="dve")
                else:
                    P.tt(d_b[:, :], z[:, 0:QT], cs[:, :], ALU.subtract)
                    yield
                    P.act(w_b[:, :], d_b[:, :], AF.Exp)
                    if m > 0:
                        P.tt(cs[:, :], cs[:, :], c_p[:, 0:QT], ALU.add)
                yield
                if diag:
                    P.tt(w_b[:, :], w_b[:, :], C.sbmask[:, j, 0:QT], ALU.mult, eng="pool")
                    yield
                P.mm(o_p[:, 0:QT], vv[:, m, :], w_b[:, :], start=first, stop=(m == 0))
                yield
            o_b = B["ob"][qi % 2]
            P.copy(o_b[:, :], o_p[:, 0:QT], eng="act")
            P.dma(oT[h * 128:(h + 1) * 128, qi * QT:(qi + 1) * QT], o_b[:, :])
            yield

    for h0 in range(0, H, NS):
        gens = [stream(h0 + i, sets[i]) for i in range(NS)]
        active = list(gens)
        while active:
            nxt = []
            for gnr in active:
                try:
                    next(gnr)
                    nxt.append(gnr)
                except StopIteration:
                    pass
            active = nxt
    P.end_phase()


def phase_lru_proj(P, C, hT, w_in, gT, xrT, T):
    P.begin_phase()
    TB = min(512, T)
    hb = [P.sb([128, KC, TB], BF16, "lh") for _ in range(2)]
    wb = [P.sb([128, KC, 512], BF16, "lw") for _ in range(2)]
    pq = [P.ps([128, 512], F32, "lpq") for _ in range(2)]
    gb = [P.sb([128, TB], BF16, "lgb") for _ in range(3)]
    xb = [P.sb([128, TB], F32, "lxb") for _ in range(3)]
    hr = hT.rearrange("(c p) t -> p c t", p=128)
    st = {"w": 0, "p": 0}
    for bi in range(T // TB):
        h = hb[bi % 2]
        P.dma(h[:, :, :], hr[:, :, bi * TB:(bi + 1) * TB])

        def epi(ci, pb, bi=bi):
            if ci < 16:
                o = gb[ci % 3]
                P.act(o[:, :], pb[:, 0:TB], AF.Gelu_apprx_tanh)
                P.dma(gT[ci * 128:(ci + 1) * 128, bi * TB:(bi + 1) * TB], o[:, :])
            else:
                o = xb[ci % 3]
                P.copy(o[:, :], pb[:, 0:TB], eng="dve")
                P.dma(xrT[(ci - 16) * 128:(ci - 15) * 128, bi * TB:(bi + 1) * TB], o[:, :])
        linear_fm(P, h, KC, TB, w_in, 0, 2 * D, wb, pq, epi, state=st)
    P.end_phase()


def phase_lru_rec(P, C, gT, xrT, yT, conv_w, conv_b, w_a, b_a, w_x, b_x, lam, T):
    P.begin_phase()
    TT = min(512, T)
    cw = P.sb([128, 4, KC], F32, "rcw")
    cb = P.sb([128, KC], F32, "rcb")
    ba = P.sb([128, KC], F32, "rba")
    bx = P.sb([128, KC], F32, "rbx")
    lm = P.sb([128, KC], F32, "rlm")
    cc = P.sb([128, KC], F32, "rcc")
    for k in range(4):
        P.dma(cw[:, k, :], conv_w[k].rearrange("(c p) -> p c", p=128), allow_slow_non_contiguous=True)
    P.dma(cb[:, :], conv_b.rearrange("(c p) -> p c", p=128), allow_slow_non_contiguous=True)
    P.dma(ba[:, :], b_a.rearrange("n (c p) -> p (n c)", p=128), allow_slow_non_contiguous=True)
    P.dma(bx[:, :], b_x.rearrange("n (c p) -> p (n c)", p=128), allow_slow_non_contiguous=True)
    P.dma(lm[:, :], lam.rearrange("(c p) -> p c", p=128), allow_slow_non_contiguous=True)
    P.act(cc[:, :], lm[:, :], AF.Exp, scale=-1.0)
    P.act(cc[:, :], cc[:, :], AF.Ln, bias=C.one_col[:, :])
    P.ts(cc[:, :], cc[:, :], -8.0, ALU.mult)
    xp = P.sb([128, 2, T + 3], F32, "rxp")
    xc = P.sb([128, 2, T], F32, "rxc")
    xcb = P.sb([128, 2, T], BF16, "rxcb")
    wab = [P.sb([128, 2, 256], BF16, "rwa") for _ in range(2)]
    wxb = [P.sb([128, 2, 256], BF16, "rwx") for _ in range(2)]
    a_b = P.sb([128, T], F32, "ra")
    u_b = P.sb([128, T], F32, "ru")
    hs = P.sb([128, T], F32, "rhs")
    g_b = [P.sb([128, T], BF16, "rg") for _ in range(2)]
    y_b = [P.sb([128, T], BF16, "ry") for _ in range(2)]
    pr = [P.ps([128, 512], F32, "rpr") for _ in range(2)]
    pi_ = [P.ps([128, 512], F32, "rpi") for _ in range(2)]
    r_t = [P.sb([128, TT], F32, "rrt") for _ in range(2)]
    i_t = [P.sb([128, TT], F32, "rit") for _ in range(2)]
    a2 = [P.sb([128, TT], F32, "ra2") for _ in range(2)]
    s_t = [P.sb([128, TT], F32, "rst") for _ in range(2)]
    P.memset(xp[:, :, 0:3], 0.0)
    n = 0
    for blk in range(8):
        wa_, wx_ = wab[blk % 2], wxb[blk % 2]
        P.dma(wa_[:, :, :], w_a[blk].rearrange("(c p) n -> p c n", p=128), q="pool")
        P.dma(wx_[:, :, :], w_x[blk].rearrange("(c p) n -> p c n", p=128), q="pool")
        for c in range(2):
            ch = blk * 2 + c
            P.dma(xp[:, c, 3:T + 3], xrT[ch * 128:(ch + 1) * 128, :])
            P.ts(xc[:, c, :], xp[:, c, 0:T], cw[:, 0, ch:ch + 1], ALU.mult, cb[:, ch:ch + 1], ALU.add)
            for k in range(1, 4):
                P.stt(xc[:, c, :], xp[:, c, k:T + k], cw[:, k, ch:ch + 1], xc[:, c, :], ALU.mult, ALU.add)
            P.copy(xcb[:, c, :], xc[:, c, :], eng="act")
        for jo in range(2):
            ch = blk * 2 + jo
            g = g_b[ch % 2]
            P.dma(g[:, :], gT[ch * 128:(ch + 1) * 128, :])
            for tt in range(T // TT):
                sl = slice(tt * TT, (tt + 1) * TT)
                p_r, p_i = pr[n % 2], pi_[n % 2]
                rt, it, a2t, stt_ = r_t[n % 2], i_t[n % 2], a2[n % 2], s_t[n % 2]
                n += 1
                for ki in range(2):
                    P.mm(p_r[:, 0:TT], wa_[:, ki, jo * 128:(jo + 1) * 128], xcb[:, ki, sl], start=(ki == 0), stop=(ki == 1))
                for ki in range(2):
                    P.mm(p_i[:, 0:TT], wx_[:, ki, jo * 128:(jo + 1) * 128], xcb[:, ki, sl], start=(ki == 0), stop=(ki == 1))
                P.act(rt[:, :], p_r[:, 0:TT], AF.Sigmoid, bias=ba[:, ch:ch + 1])
                P.act(it[:, :], p_i[:, 0:TT], AF.Sigmoid, bias=bx[:, ch:ch + 1])
                P.act(a_b[:, sl], rt[:, :], AF.Exp, scale=cc[:, ch:ch + 1])
                P.ts(a_b[:, sl], a_b[:, sl], 1.0, ALU.min)
                P.tt(a2t[:, :], a_b[:, sl], a_b[:, sl], ALU.mult)
                P.act(stt_[:, :], a2t[:, :], AF.Sqrt, bias=C.one_col[:, :], scale=-1.0)
                P.tt(it[:, :], it[:, :], xc[:, jo, sl], ALU.mult, eng="pool")
                P.tt(u_b[:, sl], it[:, :], stt_[:, :], ALU.mult)
            P.scan(hs[:, :], a_b[:, :], u_b[:, :], 0.0, ALU.mult, ALU.add)
            y = y_b[ch % 2]
            P.tt(y[:, :], hs[:, :], g[:, :], ALU.mult, eng="pool")
            P.dma(yT[ch * 128:(ch + 1) * 128, :], y[:, :])
    P.end_phase()


TWO_PI = 6.283185307179586
CW1 = 6.28125
CW2 = TWO_PI - CW1
DSW_DIL = (1, 4, 16)


def phase_rope_tables(P, C, pos, ctab, stab, T):
    P.begin_phase()
    TT = min(512, T)
    posi = [P.sb([1, TT], I32, "tpi") for _ in range(2)]
    posf = [P.sb([1, TT], F32, "tpf") for _ in range(2)]
    pb = [P.ps([128, 512], F32, "tpb") for _ in range(2)]
    ang = P.sb([32, TT], F32, "tang")
    ki = P.sb([32, TT], I32, "tki")
    kf = P.sb([32, TT], F32, "tkf")
    r = P.sb([32, TT], F32, "tr")
    y = P.sb([32, TT], F32, "ty")
    m = P.sb([32, TT], F32, "tm")
    sn = P.sb([32, TT], F32, "tsn")
    ct = [P.sb([128, TT], F32, "tct") for _ in range(2)]
    st_ = [P.sb([128, TT], F32, "tst") for _ in range(2)]
    for i in range(2):
        P.memset(ct[i][:, :], 1.0)
        P.memset(st_[i][:, :], 0.0)
    for ti in range(T // TT):
        sl = slice(ti * TT, (ti + 1) * TT)
        pi_, pf, p_b, c_t, s_t = posi[ti % 2], posf[ti % 2], pb[ti % 2], ct[ti % 2], st_[ti % 2]
        P.dma(pi_[:, :], pos[0:1, sl])
        P.copy(pf[:, :], pi_[:, :], eng="dve")
        P.mm(p_b[0:32, 0:TT], C.ones_f[0:1, 0:32], pf[:, :], start=True, stop=True)
        P.ts(ang[:, :], p_b[0:32, 0:TT], C.invf[0:32, 0:1], ALU.mult)
        P.ts(ki[:, :], ang[:, :], 1.0 / TWO_PI, ALU.mult)
        P.copy(kf[:, :], ki[:, :], eng="dve")
        P.stt(r[:, :], kf[:, :], -CW1, ang[:, :], ALU.mult, ALU.add)
        P.stt(r[:, :], kf[:, :], -CW2, r[:, :], ALU.mult, ALU.add)
        for shift, dst, sgn in ((0.0, s_t, True), (TWO_PI / 4, c_t, False)):
            P.ts(y[:, :], r[:, :], shift, ALU.add)
            P.ts(m[:, :], y[:, :], TWO_PI / 2, ALU.is_gt, -TWO_PI, ALU.mult)
            P.tt(y[:, :], y[:, :], m[:, :], ALU.add)
            P.ts(m[:, :], y[:, :], -TWO_PI / 2, ALU.is_lt, TWO_PI, ALU.mult)
            P.tt(y[:, :], y[:, :], m[:, :], ALU.add)
            if sgn:
                P.act(sn[:, :], y[:, :], AF.Sin)
                P.ts(dst[0:32, :], sn[:, :], C.invf[0:32, 1:2], ALU.mult)
            else:
                P.act(dst[0:32, :], y[:, :], AF.Sin)
        P.dma(ctab[:, sl], c_t[:, :])
        P.dma(stab[:, sl], s_t[:, :])
    P.end_phase()


def phase_dsw_proj(P, C, hT, w_in, qn_d, kn_d, ctab, stab, qT, kT, v, T):
    P.begin_phase()
    TB = min(512, T)
    H = 18
    hb = [P.sb([128, KC, TB], BF16, "dh") for _ in range(2)]
    wb = [P.sb([128, KC, 512], BF16, "dw") for _ in range(2)]
    gq = P.sb([128, 1], F32, "dgq")
    gk = P.sb([128, 1], F32, "dgk")
    P.dma(gq[:, :], qn_d.rearrange("(p o) -> p o", o=1))
    P.dma(gk[:, :], kn_d.rearrange("(p o) -> p o", o=1))
    P.ts(gq[:, :], gq[:, :], 128.0 ** -0.5, ALU.mult)
    pq = [P.ps([128, 512], F32, "dpq") for _ in range(2)]
    pss = [P.ps([128, 512], F32, "dpss") for _ in range(2)]
    pr = [P.ps([128, 512], F32, "dpr") for _ in range(2)]
    pv = [P.ps([128, 512], F32, "dpv") for _ in range(2)]
    sqb = [P.sb([128, TB], F32, "dsq") for _ in range(2)]
    lnb = P.sb([128, TB], F32, "dln")
    rsb = [P.sb([128, TB], F32, "drs") for _ in range(2)]
    qnb = [P.sb([128, TB], BF16, "dqn") for _ in range(2)]
    t1 = [P.sb([128, TB], F32, "dt1") for _ in range(2)]
    t2 = [P.sb([128, TB], F32, "dt2") for _ in range(2)]
    ob = [P.sb([128, TB], BF16, "dob") for _ in range(3)]
    vb = [P.sb([128, 512], BF16, "dvb") for _ in range(3)]
    cb = [P.sb([128, TB], F32, "dcb") for _ in range(2)]
    sb_ = [P.sb([128, TB], F32, "dsb") for _ in range(2)]
    hr = hT.rearrange("(c p) t -> p c t", p=128)
    st = {"w": 0, "p": 0}
    n = [0]
    for bi in range(T // TB):
        h = hb[bi % 2]
        c_t, s_t = cb[bi % 2], sb_[bi % 2]
        tsl = slice(bi * TB, (bi + 1) * TB)
        P.dma(h[:, :, :], hr[:, :, tsl])
        P.dma(c_t[:, :], ctab[:, tsl])
        P.dma(s_t[:, :], stab[:, tsl])
        for which, (c0, gcol, dst) in enumerate(((0, gq, qT), (H * 128, gk, kT))):
            def epi(ci, pb, gcol=gcol, dst=dst, tsl=tsl, c_t=c_t, s_t=s_t):
                i = n[0]
                n[0] += 1
                qn = qnb[i % 2]
                headnorm_epilogue(P, C, pb, TB, gcol[:, 0:1], sqb[i % 2], pss[i % 2], lnb, rsb[i % 2], qn[:, 0:TB])
                p_r = pr[i % 2]
                P.mm(p_r[:, 0:TB], C.perm_b[:, :], qn[:, 0:TB], start=True, stop=True)
                P.tt(t1[i % 2][:, :], p_r[:, 0:TB], s_t[:, :], ALU.mult)
                P.tt(t2[i % 2][:, :], qn[:, 0:TB], c_t[:, :], ALU.mult, eng="pool")
                o = ob[i % 3]
                P.tt(o[:, 0:TB], t1[i % 2][:, :], t2[i % 2][:, :], ALU.add)
                P.dma(dst[ci * 128:(ci + 1) * 128, tsl], o[:, 0:TB])
            linear_fm(P, h, KC, TB, w_in, c0, H * 128, wb, pq, epi, state=st)
        ncol = H * 128
        for pi in range((ncol + 511) // 512):
            pc = min(512, ncol - pi * 512)
            w = wb[st["w"] % 2]
            st["w"] += 1
            load_wpanel(P, w, w_in, KC, 2 * H * 128 + pi * 512, pc)
            for tt in range(TB // 128):
                i = n[0]
                n[0] += 1
                p = pv[i % 2]
                for kc in range(KC):
                    P.mm(p[:, 0:pc], h[:, kc, tt * 128:(tt + 1) * 128], w[:, kc, 0:pc], start=(kc == 0), stop=(kc == KC - 1))
                o = vb[i % 3]
                P.copy(o[:, 0:pc], p[:, 0:pc], eng="act")
                P.dma(v[bi * TB + tt * 128: bi * TB + (tt + 1) * 128, pi * 512:pi * 512 + pc], o[:, 0:pc])
    P.end_phase()


def phase_dsw_attn(P, C, qT, kT, v, oT, T):
    P.begin_phase()
    qnat = P.sb([128, T], BF16, "wqn")
    knat = P.sb([128, T], BF16, "wkn")
    qd = P.sb([128, T], BF16, "wqd")
    kd = P.sb([128, T], BF16, "wkd")
    vd = [P.sb([128, T // 128, 128], BF16, "wvd") for _ in range(2)]
    Og = [P.sb([128, T], F32, "wO") for _ in range(3)]
    Dg = [P.sb([128, T], F32, "wD") for _ in range(3)]
    outb = [P.sb([128, T], BF16, "wout") for _ in range(2)]
    psc = [P.ps([128, 512], F32, "wpsc") for _ in range(2)]
    psp = [P.ps([128, 512], F32, "wpsp") for _ in range(2)]
    po = [P.ps([128, 512], F32, "wpo") for _ in range(2)]
    pd = [P.ps([128, 512], F32, "wpd") for _ in range(2)]
    pcb = [P.sb([128, 512], BF16, "wpc") for _ in range(2)]
    ppb = [P.sb([128, 512], BF16, "wpp") for _ in range(2)]
    u = 0
    hcount = 0
    for hg in range(6):
        for g in range(3):
            d = DSW_DIL[g]
            head = g * 6 + hg
            L = T // d
            NBs = L // 128
            vv = vd[hcount % 2]
            hcount += 1
            P.dma(qnat[:, :], qT[head * 128:(head + 1) * 128, :])
            P.dma(knat[:, :], kT[head * 128:(head + 1) * 128, :])
            vsrc = v[:, head * 128:(head + 1) * 128].rearrange("(nb j r) c -> r j nb c", j=128, r=d)
            for r in range(d):
                P.dma(vv[:, r * NBs:(r + 1) * NBs, :], vsrc[r])
            if d == 1:
                qdv, kdv = qnat, knat
            else:
                P.copy(qd.rr("p (r n) -> p r n", r=d), qnat.rr("p (n r) -> p r n", r=d), eng="pool")
                P.copy(kd.rr("p (r n) -> p r n", r=d), knat.rr("p (n r) -> p r n", r=d), eng="pool")
                qdv, kdv = qd, kd
            Onat = Og[g].rr("p (n r) -> p r n", r=d)
            Dnat = Dg[g].rr("p (n r) -> p r n", r=d)
            for r in range(d):
                for q0 in range(0, NBs, 4):
                    nq = min(4, NBs - q0)
                    W = nq * 128
                    s_c, s_p, o_p, d_p = psc[u % 2], psp[u % 2], po[u % 2], pd[u % 2]
                    p_c, p_p = pcb[u % 2], ppb[u % 2]
                    u += 1
                    base = r * L
                    for b in range(nq):
                        qb = q0 + b
                        qs = slice(base + qb * 128, base + (qb + 1) * 128)
                        P.mm(s_c[:, b * 128:(b + 1) * 128], kdv[:, qs], qdv[:, qs], start=True, stop=True)
                        if qb > 0:
                            ks = slice(base + (qb - 1) * 128, base + qb * 128)
                            P.mm(s_p[:, b * 128:(b + 1) * 128], kdv[:, ks], qdv[:, qs], start=True, stop=True)
                    P.act(p_c[:, 0:W], s_c[:, 0:W], AF.Exp)
                    P.tt(p_c[:, 0:W], p_c[:, 0:W], C.dmask[:, 0, 0:W], ALU.mult, eng="pool")
                    lo = 128 if q0 == 0 else 0
                    if lo:
                        P.memset(p_p[:, 0:128], 0.0)
                    if W > lo:
                        P.act(p_p[:, lo:W], s_p[:, lo:W], AF.Exp)
                        P.tt(p_p[:, lo:W], p_p[:, lo:W], C.dmask[:, 1, lo:W], ALU.mult, eng="pool")
                    for b in range(nq):
                        qb = q0 + b
                        cs = slice(b * 128, (b + 1) * 128)
                        P.mm(o_p[:, cs], vv[:, r * NBs + qb, :], p_c[:, cs], start=True, stop=(qb == 0))
                        if qb > 0:
                            P.mm(o_p[:, cs], vv[:, r * NBs + qb - 1, :], p_p[:, cs], start=False, stop=True)
                    P.mm(d_p[:, 0:W], C.ones_b[:, :], p_c[:, 0:W], start=True, stop=False)
                    P.mm(d_p[:, 0:W], C.ones_b[:, :], p_p[:, 0:W], start=False, stop=True)
                    ns = slice(q0 * 128, q0 * 128 + W)
                    P.copy(Onat[:, r, ns], o_p[:, 0:W], eng="act")
                    P.copy(Dnat[:, r, ns], d_p[:, 0:W], eng="dve")
        P.tt(Dg[0][:, :], Dg[0][:, :], Dg[1][:, :], ALU.add)
        P.tt(Dg[0][:, :], Dg[0][:, :], Dg[2][:, :], ALU.add)
        P.recip(Dg[0][:, :], Dg[0][:, :])
        for g in range(3):
            head = g * 6 + hg
            ob_ = outb[g % 2]
            P.tt(ob_[:, :], Og[g][:, :], Dg[0][:, :], ALU.mult, eng=("pool" if g == 1 else "dve"))
            P.dma(oT[head * 128:(head + 1) * 128, :], ob_[:, :])
    P.end_phase()


GQ0, GK0, GV0, GZ0, GB0 = 0, 2048, 4096, 8192, 12288


def phase_gdn_proj(P, C, hT, w_in, a_log, dt_bias, pre, zs, beta_d, g_d, T):
    P.begin_phase()
    TB = min(512, T)
    hb = [P.sb([128, KC, TB], BF16, "gh") for _ in range(2)]
    wb = [P.sb([128, KC, 512], BF16, "gw") for _ in range(2)]
    wba = P.sb([128, KC, 64], BF16, "gwba")
    pq = [P.ps([128, 512], F32, "gpq") for _ in range(2)]
    pb_ = [P.ps([128, 512], F32, "gpb") for _ in range(2)]
    ob = [P.sb([128, TB], BF16, "gob") for _ in range(3)]
    row = P.sb([1, 64], F32, "grow")
    negea = P.sb([128, 32], F32, "gnegea")
    dtb = P.sb([128, 32], F32, "gdtb")
    bt = [P.sb([128, 32], F32, "gbt") for _ in range(2)]
    tt_ = [P.sb([128, 32], F32, "gtt") for _ in range(2)]
    gt = [P.sb([128, 32], F32, "ggt") for _ in range(2)]
    P.dma(row[:, 0:32], a_log.rearrange("(o n) -> o n", o=1))
    P.dma(row[:, 32:64], dt_bias.rearrange("(o n) -> o n", o=1))
    P.mm(pb_[0][:, 0:64], C.ones_f[0:1, :], row[:, :], start=True, stop=True)
    P.act(negea[:, :], pb_[0][:, 0:32], AF.Exp)
    P.ts(negea[:, :], negea[:, :], -1.0, ALU.mult)
    P.copy(dtb[:, :], pb_[0][:, 32:64], eng="dve")
    load_wpanel(P, wba, w_in, KC, GB0, 64)
    hr = hT.rearrange("(c p) t -> p c t", p=128)
    st = {"w": 0, "p": 0}
    n = 0
    for bi in range(T // TB):
        h = hb[bi % 2]
        tsl = slice(bi * TB, (bi + 1) * TB)
        P.dma(h[:, :, :], hr[:, :, tsl])

        def epi(ci, pb, tsl=tsl):
            o = ob[ci % 3]
            if ci < 64:
                P.copy(o[:, :], pb[:, 0:TB], eng=("act" if ci % 2 else "dve"))
                P.dma(pre[ci * 128:(ci + 1) * 128, tsl], o[:, :])
            else:
                P.act(o[:, :], pb[:, 0:TB], AF.Silu)
                P.dma(zs[(ci - 64) * 128:(ci - 63) * 128, tsl], o[:, :])
        linear_fm(P, h, KC, TB, w_in, 0, GB0, wb, pq, epi, state=st)
        for t4 in range(TB // 128):
            p = pb_[n % 2]
            b_t, t_t, g_t = bt[n % 2], tt_[n % 2], gt[n % 2]
            n += 1
            for kc in range(KC):
                P.mm(p[:, 0:64], h[:, kc, t4 * 128:(t4 + 1) * 128], wba[:, kc, :], start=(kc == 0), stop=(kc == KC - 1))
            rows = slice(bi * TB + t4 * 128, bi * TB + (t4 + 1) * 128)
            P.act(b_t[:, :], p[:, 0:32], AF.Sigmoid)
            P.dma(beta_d[rows, :], b_t[:, :])
            P.tt(t_t[:, :], p[:, 32:64], dtb[:, :], ALU.add)
            P.act(t_t[:, :], t_t[:, :], AF.Exp)
            P.act(t_t[:, :], t_t[:, :], AF.Ln, bias=C.one_col[:, :])
            P.tt(g_t[:, :], t_t[:, :], negea[:, :], ALU.mult)
            P.dma(g_d[rows, :], g_t[:, :])
    P.end_phase()


def phase_gdn_conv(P, C, pre, conv_w, post, T):
    P.begin_phase()
    TT = min(512, T)
    NCHN = 64
    cw = P.sb([128, 4, NCHN], F32, "ccw")
    for k in range(4):
        P.dma(cw[:, k, :], conv_w[k].rearrange("(c p) -> p c", p=128), allow_slow_non_contiguous=True)
    xp = [P.sb([128, T + 3], BF16, "cxp") for _ in range(2)]
    acc = [P.sb([128, T], F32, "cacc") for _ in range(2)]
    sl_ = [P.sb([128, T], F32, "csl") for _ in range(2)]
    ob = [P.sb([128, T], BF16, "cob") for _ in range(2)]
    sqb = [P.sb([128, TT], F32, "csq") for _ in range(2)]
    lnb = P.sb([128, TT], F32, "cln")
    rsb = [P.sb([128, TT], F32, "crs") for _ in range(2)]
    pss = [P.ps([128, 512], F32, "cpss") for _ in range(2)]
    for i in range(2):
        P.memset(xp[i][:, 0:3], 0.0)
    n = 0
    for ch in range(NCHN):
        x, a, s_, o = xp[ch % 2], acc[ch % 2], sl_[ch % 2], ob[ch % 2]
        P.dma(x[:, 3:T + 3], pre[ch * 128:(ch + 1) * 128, :])
        P.ts(a[:, :], x[:, 0:T], cw[:, 0, ch:ch + 1], ALU.mult)
        for k in range(1, 4):
            P.stt(a[:, :], x[:, k:T + k], cw[:, k, ch:ch + 1], a[:, :], ALU.mult, ALU.add)
        if ch >= 32:
            P.act(o[:, :], a[:, :], AF.Silu)
        else:
            P.act(s_[:, :], a[:, :], AF.Silu)
            sc = (128.0 ** -0.5) if ch < 16 else 1.0
            for ti in range(T // TT):
                tsl = slice(ti * TT, (ti + 1) * TT)
                sq, ps, rs = sqb[n % 2], pss[n % 2], rsb[n % 2]
                n += 1
                P.tt(sq[:, :], s_[:, tsl], s_[:, tsl], ALU.mult, eng="pool")
                P.mm(ps[:, 0:TT], C.ones_f[:, :], sq[:, :], start=True, stop=True)
                P.act(lnb[:, :], ps[:, 0:TT], AF.Ln, bias=C.eps_col[:, :])
                P.act(rs[:, :], lnb[:, :], AF.Exp, scale=-0.5)
                P.stt(o[:, tsl], s_[:, tsl], sc, rs[:, :], ALU.mult, ALU.mult)
        P.dma(post[ch * 128:(ch + 1) * 128, :], o[:, :])
    P.end_phase()


def phase_gdn_core(P, C, post, zs, beta_d, g_d, onorm_d, oT, T):
    P.begin_phase()
    GE = DBG_GE
    NCH = T // 128
    TT = min(256, T)
    W = NCH * 32
    G = 6
    f = lambda nm: P.sb([128, NCH, 32], F32, nm)
    beta, gg, gc, c2, nbeta, dcol, egl, glb = f("kb"), f("kg"), f("kgc"), f("kc2"), f("knb"), f("kdc"), f("keg"), f("kgl")
    gon = P.sb([128, 1], F32, "kgon")
    P.dma(gon[:, :], onorm_d.rearrange("(p o) -> p o", o=1))
    P.dma(beta[:, :, :], beta_d.rearrange("(c p) n -> p c n", p=128))
    P.dma(gg[:, :, :], g_d.rearrange("(c p) n -> p c n", p=128))
    bF = [P.ps([128, 512], F32, "kF%d" % g) for g in range(G)]
    tC = P.ps([128, 512], F32, "kC")
    tN = P.ps([128, 512], F32, "kN")
    pKV = [bF[g][:, 0:128].bc(BF16) for g in range(G)]
    pNA = pKV
    pGR = [bF[g][:, 128:256] for g in range(G)]
    pD = [bF[g][:, 256:512] for g in range(G)]
    pA = [bF[g][:, 0:256] for g in range(G)]
    pWS = tC[:, 0:128]
    pSU = tC[:, 128:256]
    pNrm = tN[:, 0:256]
    pOT = tN[:, 384:512]
    gflat = gg.rr("p c n -> p (c n)")
    gcflat = gc.rr("p c n -> p (c n)")
    glflat = glb.rr("p c n -> p (c n)")
    for c0 in range(0, W, 512):
        w_ = min(512, W - c0)
        P.mm(tC[:, 0:w_], C.triu_f[:, :], gflat[:, c0:c0 + w_], start=True, stop=True)
        P.copy(gcflat[:, c0:c0 + w_], tC[:, 0:w_], eng="dve")
    for c0 in range(0, W, 512):
        w_ = min(512, W - c0)
        P.mm(tC[:, 0:w_], C.sellast_f[:, :], gcflat[:, c0:c0 + w_], start=True, stop=True)
        P.copy(glflat[:, c0:c0 + w_], tC[:, 0:w_], eng="dve")
    P.act(c2[:, :, :], gc[:, :, :], AF.Exp)
    P.tt(c2[:, :, :], c2[:, :, :], beta[:, :, :], ALU.mult)
    P.ts(nbeta[:, :, :], beta[:, :, :], -1.0, ALU.mult)
    P.tt(dcol[:, :, :], glb[:, :, :], gc[:, :, :], ALU.subtract)
    P.act(dcol[:, :, :], dcol[:, :, :], AF.Exp)
    P.act(egl[:, :, :], glb[:, :, :], AF.Exp)

    kT = [P.sb([128, T], BF16, "kkT") for _ in range(2)]
    qT = [P.sb([128, T], BF16, "kqT") for _ in range(2)]
    vT = [P.sb([128, T], BF16, "kvT") for _ in range(2)]
    zb = [P.sb([128, T], BF16, "kzb") for _ in range(2)]
    oh = P.sb([128, T], F32, "koh")
    outb = [P.sb([128, T], BF16, "koutb") for _ in range(2)]
    S32 = P.sb([128, 128], F32, "kS32")
    Sbf = P.sb([128, 128], BF16, "kSbf")
    nbG = lambda shape, dt, nm: [P.sb(shape, dt, nm) for _ in range(G)]
    nb2G = lambda shape, dt, nm: [P.sb(shape, dt, nm) for _ in range(2 * G)]
    R = nbG([128, 256], BF16, "kR")
    Gb = nbG([128, 128], F32, "kGb")
    growS = nbG([128, 128], F32, "kgrowS")
    Dn = nbG([128, 128], F32, "kDn")
    E = nbG([128, 128], F32, "kE")
    Eg = nbG([128, 128], F32, "kEg")
    M0f = nbG([128, 128], F32, "kM0f")
    Atf = nbG([128, 128], F32, "kAtf")
    Atb = nbG([128, 128], BF16, "kAtb")
    MN = nbG([128, 256], BF16, "kMN")
    TTm = nbG([128, 256], BF16, "kTT")
    Xs = nbG([128, 256], BF16, "kXs")
    Kdec = nb2G([128, 128], BF16, "kKd")
    QgT = nb2G([128, 128], BF16, "kQg")
    AtT = nb2G([128, 128], BF16, "kAtT")
    UW = nb2G([128, 256], BF16, "kUW")
    wTb = nb2G([128, 128], BF16, "kwT")
    vnew = nb2G([128, 128], BF16, "kvn")
    sqb = [P.sb([128, TT], F32, "ksq") for _ in range(2)]
    lnb = P.sb([128, TT], F32, "kln")
    rsb = [P.sb([128, TT], F32, "krs") for _ in range(2)]
    tmpb = [P.sb([128, TT], F32, "ktmp") for _ in range(2)]

    def stages(hv, c, g, s2):
        hp = hv % 2
        sl = slice(c * 128, (c + 1) * 128)
        kt, qt, vt = kT[hp], qT[hp], vT[hp]
        kv, na, dd, aa, gr = pKV[g], pNA[g], pD[g], pA[g], pGR[g]
        L = []

        def st0():
            P.transpose(kv[:, 0:128], kt[:, sl], C.ident_b[:, :])
            P.transpose(kv[:, 128:256], vt[:, sl], C.ident_b[:, :])
            P.ts(Gb[g][:, :], C.ones_f[:, :], gg[:, c, hv:hv + 1], ALU.mult)
        L.append(st0)

        def st1():
            P.ts(R[g][:, 0:128], kv[:, 128:256], beta[:, c, hv:hv + 1], ALU.mult)
            P.ts(R[g][:, 128:256], kv[:, 0:128], c2[:, c, hv:hv + 1], ALU.mult)
            P.ts(Kdec[s2][:, :], kv[:, 0:128], dcol[:, c, hv:hv + 1], ALU.mult)
            P.mm(gr[:, 0:128], Gb[g][:, :], C.triu_f[:, :], start=True, stop=True)
            P.mm(dd[:, 0:128], kt[:, sl], kt[:, sl], start=True, stop=True)
            P.mm(dd[:, 128:256], qt[:, sl], kt[:, sl], start=True, stop=True)
        L.append(st1)

        def st2():
            P.copy(growS[g][:, :], gr[:, 0:128], eng=GE)
        L.append(st2)

        def st3():
            P.ts(Dn[g][:, :], growS[g][:, :], gc[:, c, hv:hv + 1], ALU.subtract, 0.0, ALU.max)
            P.act(Eg[g][:, :], growS[g][:, :], AF.Exp)
        L.append(st3)

        def st4():
            P.act(E[g][:, :], Dn[g][:, :], AF.Exp, scale=-1.0)
            P.tt(QgT[s2][:, :], qt[:, sl], Eg[g][:, :], ALU.mult, eng="pool")
        L.append(st4)

        def st5():
            P.stt(M0f[g][:, :], dd[:, 0:128], nbeta[:, c, hv:hv + 1], E[g][:, :], ALU.mult, ALU.mult)
            P.tt(Atf[g][:, :], dd[:, 128:256], E[g][:, :], ALU.mult)
        L.append(st5)

        def st6():
            P.tt(MN[g][:, 0:128], M0f[g][:, :], C.maskSL_f[:, :], ALU.mult, eng="pool")
            P.tt(Atb[g][:, :], Atf[g][:, :], C.maskLI_f[:, :], ALU.mult, eng="pool")
        L.append(st6)

        def st7():
            P.transpose(na[:, 0:128], MN[g][:, 0:128], C.ident_b[:, :])
            P.transpose(na[:, 128:256], Atb[g][:, :], C.ident_b[:, :])
        L.append(st7)

        def st8():
            P.copy(MN[g][:, 128:256], na[:, 0:128], eng=GE)
            P.copy(AtT[s2][:, :], na[:, 128:256], eng=GE)
        L.append(st8)
        for l in range(7):
            def sx(l=l):
                Tb = C.ident_b[:, :] if l == 0 else TTm[g][:, 0:128]
                Tbt = C.ident_b[:, :] if l == 0 else TTm[g][:, 128:256]
                P.mm(dd[:, 0:128], MN[g][:, 128:256], Tb, start=True, stop=True)
                P.mm(dd[:, 128:256], MN[g][:, 0:128], Tbt, start=True, stop=True)
            L.append(sx)

            def sm(l=l):
                P.tt(Xs[g][:, :], dd[:, 0:256], C.lvmask[:, l, :], ALU.mult)
            L.append(sm)

            def sy(l=l):
                Tb = C.ident_b[:, :] if l == 0 else TTm[g][:, 0:128]
                Tbt = C.ident_b[:, :] if l == 0 else TTm[g][:, 128:256]
                P.mm(aa[:, 0:128], Tbt, Xs[g][:, 0:128], start=True, stop=False)
                P.mm(aa[:, 0:128], C.ident_b[:, :], Tb, start=False, stop=True)
                P.mm(aa[:, 128:256], Tb, Xs[g][:, 128:256], start=True, stop=False)
                P.mm(aa[:, 128:256], C.ident_b[:, :], Tbt, start=False, stop=True)
            L.append(sy)

            def sc(l=l):
                P.copy(TTm[g][:, :], aa[:, 0:256], eng=GE)
            L.append(sc)

        def su():
            P.mm(dd[:, 0:256], TTm[g][:, 128:256], R[g][:, :], start=True, stop=True)
        L.append(su)

        def su2():
            P.copy(UW[s2][:, :], dd[:, 0:256], eng="dve")
        L.append(su2)

        def sw():
            P.transpose(na[:, 0:128], UW[s2][:, 128:256], C.ident_b[:, :])
        L.append(sw)

        def sw2():
            P.copy(wTb[s2][:, :], na[:, 0:128], eng=GE)
        L.append(sw2)
        return L

    def chain(hv, c, s2):
        sl = slice(c * 128, (c + 1) * 128)
        P.mm(pWS[:, 0:128], wTb[s2][:, :], Sbf[:, :], start=True, stop=True)
        P.tt(vnew[s2][:, :], UW[s2][:, 0:128], pWS[:, 0:128], ALU.subtract)
        P.mm(pOT[:, 0:128], Sbf[:, :], QgT[s2][:, :], start=True, stop=False)
        P.mm(pOT[:, 0:128], vnew[s2][:, :], AtT[s2][:, :], start=False, stop=True)
        P.copy(oh[:, sl], pOT[:, 0:128], eng=GE)
        P.mm(pSU[:, 0:128], Kdec[s2][:, :], vnew[s2][:, :], start=True, stop=True)
        P.stt(S32[:, :], S32[:, :], egl[:, c, hv:hv + 1], pSU[:, 0:128], ALU.mult, ALU.add)
        P.copy(Sbf[:, :], S32[:, :], eng="act")

    def load_head(hv):
        hp, hk = hv % 2, hv // 2
        P.dma(kT[hp][:, :], post[GK0 + hk * 128: GK0 + (hk + 1) * 128, :])
        P.dma(qT[hp][:, :], post[GQ0 + hk * 128: GQ0 + (hk + 1) * 128, :])
        P.dma(vT[hp][:, :], post[GV0 + hv * 128: GV0 + (hv + 1) * 128, :])

    def prep_group(gi):
        grp = groups[gi]
        lists = [stages(hv, c, g, (gi % 2) * G + g) for g, (hv, c) in enumerate(grp)]
        for si in range(len(lists[0])):
            for L in lists:
                L[si]()

    pairs = [(hv, c) for hv in range(32) for c in range(NCH)]
    groups = [pairs[i:i + G] for i in range(0, len(pairs), G)]
    n = 0
    for (nh, ncc) in groups[0]:
        if ncc == 0:
            load_head(nh)
    prep_group(0)
    for gi, grp in enumerate(groups):
        if gi + 1 < len(groups):
            for (nh, ncc) in groups[gi + 1]:
                if ncc == 0:
                    load_head(nh)
            prep_group(gi + 1)
        for g, (hv, c) in enumerate(grp):
            hp = hv % 2
            if c == 0:
                P.memset(S32[:, :], 0.0, eng="dve")
                P.memset(Sbf[:, :], 0.0, eng="dve")
                P.dma(zb[hp][:, :], zs[hv * 128:(hv + 1) * 128, :])
            chain(hv, c, (gi % 2) * G + g)
            if c == NCH - 1:
                ob_ = outb[hp]
                for ti in range(T // TT):
                    tsl = slice(ti * TT, (ti + 1) * TT)
                    sq, rs, tm = sqb[n % 2], rsb[n % 2], tmpb[n % 2]
                    n += 1
                    P.act(sq[:, :], oh[:, tsl], AF.Square)
                    P.mm(pNrm[:, 0:TT], C.ones_f[:, :], sq[:, :], start=True, stop=True)
                    P.act(lnb[:, :], pNrm[:, 0:TT], AF.Ln, bias=C.eps_col[:, :], scale=1.0 / 128)
                    P.act(rs[:, :], lnb[:, :], AF.Exp, scale=-0.5)
                    P.stt(tm[:, :], oh[:, tsl], gon[:, 0:1], rs[:, :], ALU.mult, ALU.mult)
                    P.tt(ob_[:, tsl], tm[:, :], zb[hp][:, tsl], ALU.mult, eng="pool")
                P.dma(oT[hv * 128:(hv + 1) * 128, :], ob_[:, :])
    P.end_phase()


W_NAMES = ["mix_norm", "ffn_norm", "ffn_w_gu", "ffn_w_down", "sb_w_in", "sb_q_norm", "sb_k_norm", "sb_w_out",
           "gdn_w_in", "gdn_conv_w", "gdn_a_log", "gdn_dt_bias", "gdn_o_norm", "gdn_w_out",
           "dsw_w_in", "dsw_q_norm", "dsw_k_norm", "dsw_w_out",
           "lru_w_in", "lru_conv_w", "lru_conv_b", "lru_w_a", "lru_b_a", "lru_w_x", "lru_b_x", "lru_lambda", "lru_w_out"]


def build(T, layers, shapes):
    nc = bass.Bass("TRN2", target_bir_lowering=False)
    es = ExitStack()
    es.__enter__()
    dr = {}
    dr["xT"] = nc.dram_tensor("xT", [D, T], F32, kind="ExternalInput").ap()
    dr["cf"] = nc.dram_tensor("cf", list(shapes["cf"]), F32, kind="ExternalInput").ap()
    for nm in W_NAMES:
        dr[nm] = nc.dram_tensor(nm, list(shapes[nm]), F32, kind="ExternalInput").ap()
    outT = nc.dram_tensor("outT", [D, T], F32, kind="ExternalOutput").ap()
    hT = nc.dram_tensor("hT", [D, T], BF16, kind="Internal").ap()
    dr["pos"] = nc.dram_tensor("pos", [1, T], I32, kind="ExternalInput").ap()
    s_q = nc.dram_tensor("s_q", [4096, T], BF16, kind="Internal").ap()
    s_k = nc.dram_tensor("s_k", [4096, T], BF16, kind="Internal").ap()
    s_v = nc.dram_tensor("s_v", [T, 4096], BF16, kind="Internal").ap()
    s_o = nc.dram_tensor("s_o", [4096, T], BF16, kind=("ExternalOutput" if DBG_OUT else "Internal")).ap()
    dk = "ExternalOutput" if DBG_OUT else "Internal"
    s_pre = nc.dram_tensor("s_pre", [8192, T], BF16, kind="Internal").ap()
    s_post = nc.dram_tensor("s_post", [8192, T], BF16, kind=dk).ap()
    s_bt = nc.dram_tensor("s_bt", [T, 32], F32, kind=dk).ap()
    s_g = nc.dram_tensor("s_g", [T, 32], F32, kind=dk).ap()
    s_ct = nc.dram_tensor("s_ct", [128, T], F32, kind="Internal").ap()
    s_st = nc.dram_tensor("s_st", [128, T], F32, kind="Internal").ap()
    s_x = nc.dram_tensor("s_x", [2048, T], F32, kind="Internal").ap()
    P = Prog(nc, es)
    C = setup_consts(P, nc, es, dr)
    P.begin_phase()
    cb = [P.sb([128, min(T, 2048)], F32, "cp") for _ in range(2)]
    TC = min(T, 2048)
    i = 0
    for c in range(KC):
        for t0 in range(0, T, TC):
            b = cb[i % 2]
            i += 1
            P.dma(b[:, :], dr["xT"][c * 128:(c + 1) * 128, t0:t0 + TC])
            P.dma(outT[c * 128:(c + 1) * 128, t0:t0 + TC], b[:, :])
    P.end_phase()
    x = outT
    for li in layers:
        kind = li % 4
        phase_norm(P, C, x, dr["mix_norm"][li], hT, T)
        if kind == 0:
            phase_sb_proj(P, C, hT, dr["sb_w_in"][0], dr["sb_q_norm"][0], dr["sb_k_norm"][0], s_q, s_k, s_v, T)
            phase_sb_attn(P, C, s_q, s_k, s_v, s_o, T)
            phase_outproj(P, C, x, s_o, 16, dr["sb_w_out"][0], T)
        elif kind == 1:
            phase_gdn_proj(P, C, hT, dr["gdn_w_in"][0], dr["gdn_a_log"][0], dr["gdn_dt_bias"][0], s_pre, s_q, s_bt, s_g, T)
            if DBG_STOP >= 2:
                phase_gdn_conv(P, C, s_pre, dr["gdn_conv_w"][0], s_post, T)
            if DBG_STOP >= 3:
                phase_gdn_core(P, C, s_post, s_q, s_bt, s_g, dr["gdn_o_norm"][0], s_o, T)
            if DBG_STOP >= 4:
                phase_outproj(P, C, x, s_o, 32, dr["gdn_w_out"][0], T)
        elif kind == 2:
            phase_rope_tables(P, C, dr["pos"], s_ct, s_st, T)
            phase_dsw_proj(P, C, hT, dr["dsw_w_in"][0], dr["dsw_q_norm"][0], dr["dsw_k_norm"][0], s_ct, s_st, s_q, s_k, s_v, T)
            phase_dsw_attn(P, C, s_q, s_k, s_v, s_o, T)
            phase_outproj(P, C, x, s_o, 18, dr["dsw_w_out"][0], T)
        elif kind == 3:
            phase_lru_proj(P, C, hT, dr["lru_w_in"][0], s_q, s_x, T)
            phase_lru_rec(P, C, s_q, s_x, s_o, dr["lru_conv_w"][0], dr["lru_conv_b"][0], dr["lru_w_a"][0], dr["lru_b_a"][0],
                          dr["lru_w_x"][0], dr["lru_b_x"][0], dr["lru_lambda"][0], T)
            phase_outproj(P, C, x, s_o, 16, dr["lru_w_out"][0], T)
        if not DBG_SKIP_FFN:
            phase_norm(P, C, x, dr["ffn_norm"][li], hT, T)
            phase_ffn(P, C, x, hT, dr["ffn_w_gu"][li], dr["ffn_w_down"][li], T)
    es.__exit__(None, None, None)
    return nc, P


def kernel(**inputs):
    T = inputs["x"].shape[1]
    B = inputs["x"].shape[0]
    consts = host_consts()
    shapes = {k: np.shape(inputs[k]) for k in W_NAMES}
    shapes["cf"] = consts["cf"].shape
    nc, P = build(T, list(range(4)), shapes)
    in_maps = []
    for b in range(B):
        m = {"xT": np.ascontiguousarray(inputs["x"][b].T), "cf": consts["cf"],
             "pos": np.ascontiguousarray(inputs["positions"][b:b + 1]).astype(np.int32)}
        for nm in W_NAMES:
            m[nm] = np.ascontiguousarray(inputs[nm], dtype=np.float32)
        in_maps.append(m)
    res = run_bass_kernel_spmd(nc, in_maps, core_ids=list(range(B)))
    out = np.stack([np.ascontiguousarray(r["outT"].T) for r in res.results], axis=0)
    return out.astype(np.float32)
```

```python
import numpy as np
from contextlib import ExitStack
import concourse.bass as bass
import concourse.mybir as mybir
from concourse.bass_utils import run_bass_kernel_spmd

F32 = mybir.dt.float32
BF16 = mybir.dt.bfloat16
I32 = mybir.dt.int32
AF = mybir.ActivationFunctionType
ALU = mybir.AluOpType

DBG_STOP = 9
DBG_CORE = 9
DBG_STEP = 9
SAME_ENG_RAW = False
DBG_GE = "act"
DBG_SKIP_FFN = False
DBG_OUT = False
DBG_K = 9
DBG_K2 = 0
D = 2048
KC = D // 128
FFN_H = 5632
EPS = 1e-6


class View:
    __slots__ = ("buf", "ap")

    def __init__(self, buf, ap):
        self.buf = buf
        self.ap = ap

    def __getitem__(self, idx):
        return View(self.buf, self.ap[idx])

    def rr(self, pat, **kw):
        return View(self.buf, self.ap.rearrange(pat, **kw))

    def bc(self, dt):
        return View(self.buf, self.ap.bitcast(dt))


class Buf:
    def __init__(self, t, name):
        self.t = t
        self.name = name
        self.writers = []
        self.readers = []
        self.sem = None
        self.semname = None
        self.psum = False

    def __getitem__(self, idx):
        return View(self, self.t[idx])

    def rr(self, pat, **kw):
        return View(self, self.t[:].rearrange(pat, **kw))


class Op:
    __slots__ = ("eng", "fn", "waits", "sem", "val", "inc", "is_dma")


def _ap(v):
    return v.ap if isinstance(v, View) else v


class Prog:
    CE = ("pe", "act", "dve", "pool")
    ALLE = ("pe", "act", "dve", "pool", "sp")

    def __init__(self, nc, es):
        self.nc = nc
        self.es = es
        self.esem = {e: es.enter_context(nc.semaphore("cs_" + e)) for e in self.CE}
        self.ecnt = {e: 0 for e in self.CE}
        self.semcnt = {}
        self.semobj = {}
        self.free_dma_sems = []
        self.n_dma_sems = 0
        self.known = {e: {} for e in self.ALLE}
        self.ops = {e: [] for e in self.ALLE}
        self.phase_sems = set()
        self.pes = None
        self.nops = 0
        self.uid = 0

    def begin_phase(self):
        self.pes = ExitStack()
        self.pes.__enter__()
        self.phase_bufs = []

    def sb(self, shape, dt, name=None):
        self.uid += 1
        name = (name or "b") + "_%d" % self.uid
        t = self.pes.enter_context(self.nc.sbuf_tensor(name, list(shape), dt))
        b = Buf(t, name)
        self.phase_bufs.append(b)
        return b

    def ps(self, shape, dt=F32, name=None):
        self.uid += 1
        name = (name or "p") + "_%d" % self.uid
        t = self.pes.enter_context(self.nc.psum_tensor(name, list(shape), dt))
        b = Buf(t, name)
        b.psum = True
        self.phase_bufs.append(b)
        return b

    def _dma_sem(self, buf):
        if buf.sem is None:
            if self.free_dma_sems:
                nm = self.free_dma_sems.pop()
            else:
                nm = "ds_%d" % self.n_dma_sems
                self.n_dma_sems += 1
                self.semobj[nm] = self.es.enter_context(self.nc.semaphore(nm))
                self.semcnt[nm] = 0
            buf.sem = self.semobj[nm]
            buf.semname = nm
        return buf.semname

    def _add(self, eng, fn, ins, outs, dma_sig=None):
        op = Op()
        op.eng = eng
        op.fn = fn
        op.is_dma = dma_sig is not None
        deps = []
        rb = []
        wb = []
        for v in ins:
            if isinstance(v, View) and v.buf not in rb:
                rb.append(v.buf)
        for v in outs:
            if isinstance(v, View) and v.buf not in wb:
                wb.append(v.buf)
        for b in rb:
            for w in b.writers:
                if w.eng == eng and not w.is_dma and not op.is_dma and (eng == "pe" or not SAME_ENG_RAW):
                    continue
                deps.append(w)
            if b.psum:
                for r in b.readers:
                    if r.eng != eng:
                        deps.append(r)
        for b in wb:
            for r in b.readers:
                if r.eng == eng and not r.is_dma and not op.is_dma:
                    continue
                deps.append(r)
            for w in b.writers:
                if (w.is_dma and op.is_dma) or (w.eng == eng and not w.is_dma and not op.is_dma):
                    continue
                deps.append(w)
        waits = {}
        for d in deps:
            k = d.sem
            if waits.get(k, (None, 0))[1] < d.val:
                waits[k] = (d.sem, d.val)
        kn = self.known[eng]
        op.waits = []
        for k, (s, v) in waits.items():
            if kn.get(k, 0) >= v:
                continue
            kn[k] = v
            op.waits.append((s, v))
        if op.is_dma:
            nm = self._dma_sem(dma_sig)
            self.semcnt[nm] += 1
            op.sem = nm
            op.val = 16 * self.semcnt[nm]
            op.inc = 16
            self.phase_sems.add(nm)
        else:
            self.ecnt[eng] += 1
            op.sem = "cs_" + eng
            op.val = self.ecnt[eng]
            op.inc = 1
        for b in rb:
            b.readers.append(op)
        for b in wb:
            if b.readers:
                b.readers = []
                b.writers = [op]
            else:
                if not op.is_dma:
                    b.writers = [w for w in b.writers if w.is_dma or w.eng != eng]
                b.writers.append(op)
        self.ops[eng].append(op)
        self.nops += 1
        return op

    def _sem(self, name):
        if name.startswith("cs_"):
            return self.esem[name[3:]]
        return self.semobj[name]

    def end_phase(self):
        finals = []
        for e in self.CE:
            if self.ecnt[e] > 0:
                finals.append(("cs_" + e, self.ecnt[e]))
        for nm in sorted(self.phase_sems):
            finals.append((nm, 16 * self.semcnt[nm]))
        ops = self.ops
        known = self.known
        P = self

        def emit(eng_name):
            def f(e):
                for op in ops[eng_name]:
                    for (s, v) in op.waits:
                        e.wait_ge(P._sem(s), v)
                    ins = op.fn(e)
                    ins.then_inc(P._sem(op.sem), op.inc)
                for (s, v) in finals:
                    if known[eng_name].get(s, 0) < v:
                        known[eng_name][s] = v
                        e.wait_ge(P._sem(s), v)
            return f

        with self.nc.Block() as blk:
            blk.tensor(emit("pe"))
            blk.scalar(emit("act"))
            blk.vector(emit("dve"))
            blk.gpsimd(emit("pool"))
            blk.sync(emit("sp"))
        for e in self.ALLE:
            self.ops[e] = []
        for b in self.phase_bufs:
            if b.semname is not None:
                self.free_dma_sems.append(b.semname)
        self.phase_sems = set()
        self.pes.__exit__(None, None, None)
        self.pes = None

    def dma(self, out, in_, q="sp", **kw):
        sig = out.buf if isinstance(out, View) else in_.buf
        o, i = _ap(out), _ap(in_)
        return self._add(q, lambda e: e.dma_start(out=o, in_=i, **kw), [in_], [out], dma_sig=sig)

    def mm(self, out, lhsT, rhs, start=True, stop=True, **kw):
        o, l, r = _ap(out), _ap(lhsT), _ap(rhs)
        return self._add("pe", lambda e: e.matmul(o, l, r, start=start, stop=stop, **kw), [lhsT, rhs], [out])

    def transpose(self, out, in_, ident):
        o, i, d = _ap(out), _ap(in_), _ap(ident)
        return self._add("pe", lambda e: e.transpose(o, i, d), [in_, ident], [out])

    def act(self, out, in_, func, bias=None, scale=1.0, accum_out=None, eng="act"):
        o, i = _ap(out), _ap(in_)
        kw = {}
        ins = [in_]
        outs = [out]
        if bias is not None:
            kw["bias"] = _ap(bias)
            ins.append(bias)
        if isinstance(scale, View):
            kw["scale"] = _ap(scale)
            ins.append(scale)
        else:
            kw["scale"] = scale
        if accum_out is not None:
            kw["accum_out"] = _ap(accum_out)
            outs.append(accum_out)
        return self._add("act", lambda e: e.activation(out=o, in_=i, func=func, **kw), ins, outs)

    def tt(self, out, in0, in1, op, eng="dve"):
        o, a, b = _ap(out), _ap(in0), _ap(in1)
        return self._add(eng, lambda e: e.tensor_tensor(out=o, in0=a, in1=b, op=op), [in0, in1], [out])

    def ts(self, out, in0, s1, op0, s2=None, op1=None, eng="dve"):
        o, a = _ap(out), _ap(in0)
        ins = [in0]
        if isinstance(s1, View):
            ins.append(s1)
        if isinstance(s2, View):
            ins.append(s2)
        x1, x2 = _ap(s1), _ap(s2)
        if op1 is None:
            return self._add(eng, lambda e: e.tensor_scalar(out=o, in0=a, scalar1=x1, scalar2=None, op0=op0), ins, [out])
        return self._add(eng, lambda e: e.tensor_scalar(out=o, in0=a, scalar1=x1, scalar2=x2, op0=op0, op1=op1), ins, [out])

    def stt(self, out, in0, scalar, in1, op0, op1):
        o, a, b = _ap(out), _ap(in0), _ap(in1)
        ins = [in0, in1]
        if isinstance(scalar, View):
            ins.append(scalar)
        s = _ap(scalar)
        return self._add("dve", lambda e: e.scalar_tensor_tensor(out=o, in0=a, scalar=s, in1=b, op0=op0, op1=op1), ins, [out])

    def copy(self, out, in_, eng="dve"):
        o, i = _ap(out), _ap(in_)
        if eng == "act":
            return self._add("act", lambda e: e.activation(out=o, in_=i, func=AF.Copy), [in_], [out])
        return self._add(eng, lambda e: e.tensor_copy(out=o, in_=i), [in_], [out])

    def memset(self, out, val, eng="pool"):
        o = _ap(out)
        return self._add(eng, lambda e: e.memset(o, val), [], [out])

    def recip(self, out, in_):
        o, i = _ap(out), _ap(in_)
        return self._add("dve", lambda e: e.reciprocal(out=o, in_=i), [in_], [out])

    def scan(self, out, d0, d1, initial, op0, op1):
        o, a, b = _ap(out), _ap(d0), _ap(d1)
        ins = [d0, d1]
        if isinstance(initial, View):
            ins.append(initial)
        ini = _ap(initial)
        return self._add("dve", lambda e: e.tensor_tensor_scan(out=o, data0=a, data1=b, initial=ini, op0=op0, op1=op1), ins, [out])


class Ctx:
    pass


def setup_consts(P, nc, es, cdram):
    C = Ctx()

    def sbg(name, shape, dt):
        t = es.enter_context(nc.sbuf_tensor(name, list(shape), dt))
        return Buf(t, name)

    C.ones_f = sbg("ones_f", [128, 128], F32)
    C.ones_b = sbg("ones_b", [128, 128], BF16)
    C.ident_b = sbg("ident_b", [128, 128], BF16)
    C.ident_f = sbg("ident_f", [128, 128], F32)
    C.negtri_b = sbg("negtri_b", [128, 128], BF16)
    C.sbmask = sbg("sbmask", [128, 4, 512], BF16)
    C.perm_b = sbg("perm_b", [128, 128], BF16)
    C.dmask = sbg("dmask", [128, 2, 512], BF16)
    C.invf = sbg("invf", [128, 2], F32)
    C.triu_f = sbg("triu_f", [128, 128], F32)
    C.sellast_f = sbg("sellast_f", [128, 128], F32)
    C.maskSL_f = sbg("maskSL_f", [128, 128], F32)
    C.maskLI_f = sbg("maskLI_f", [128, 128], F32)
    C.lvmask = sbg("lvmask", [128, 7, 256], BF16)
    C.eps_col = sbg("eps_col", [128, 1], F32)
    C.one_col = sbg("one_col", [128, 1], F32)
    P.begin_phase()
    stage = P.sb([128, CF_W], F32, "cstage")
    P.dma(stage[:, :], cdram["cf"][:, :])
    P.copy(C.ones_f[:, :], stage[:, 0:128])
    P.copy(C.ones_b[:, :], stage[:, 0:128])
    P.copy(C.ident_f[:, :], stage[:, 128:256])
    P.copy(C.ident_b[:, :], stage[:, 128:256])
    P.copy(C.negtri_b[:, :], stage[:, 256:384])
    P.copy(C.sbmask[:, :, :], stage[:, 640:640 + 2048].rr("p (j c) -> p j c", j=4))
    P.copy(C.perm_b[:, :], stage[:, 384:512])
    P.copy(C.dmask[:, :, :], stage[:, 2688:2688 + 1024].rr("p (j c) -> p j c", j=2))
    P.copy(C.invf[:, :], stage[:, 512:514])
    P.copy(C.triu_f[:, :], stage[:, 3712:3840])
    P.copy(C.sellast_f[:, :], stage[:, 3840:3968])
    P.copy(C.maskSL_f[:, :], stage[:, 3968:4096])
    P.copy(C.maskLI_f[:, :], stage[:, 4096:4224])
    P.copy(C.lvmask[:, :, :], stage[:, 4224:4224 + 1792].rr("p (l c) -> p l c", l=7))
    P.memset(C.eps_col[:, :], EPS)
    P.memset(C.one_col[:, :], 1.0)
    P.end_phase()
    return C


CF_W = 640 + 2048 + 1024 + 512 + 1792


def host_consts():
    cf = np.zeros((128, CF_W), np.float32)
    cf[:, 0:128] = 1.0
    cf[:, 128:256] = np.eye(128, dtype=np.float32)
    s = np.arange(128)[:, None]
    sp = np.arange(128)[None, :]
    cf[:, 256:384] = -(s >= sp).astype(np.float32)
    c = np.arange(512)[None, :]
    for j in range(4):
        cf[:, 640 + j * 512: 640 + (j + 1) * 512] = (c > j * 128 + s).astype(np.float32)
    for dd in range(16):
        cf[dd + 16, 384 + dd] = 1.0
        cf[dd, 384 + dd + 16] = 1.0
    invf = (np.float32(500000.0) ** (-np.arange(16, dtype=np.float32) / np.float32(16))).astype(np.float32)
    cf[0:16, 512] = invf
    cf[16:32, 512] = invf
    cf[0:16, 513] = -1.0
    cf[16:32, 513] = 1.0
    cm = np.arange(512)[None, :] % 128
    cf[:, 2688:2688 + 512] = (s <= cm).astype(np.float32)
    cf[:, 2688 + 512:2688 + 1024] = (s >= cm).astype(np.float32)
    cf[:, 3712:3840] = (s <= sp).astype(np.float32)
    cf[127, 3840:3968] = 1.0
    cf[:, 3968:4096] = (s > sp).astype(np.float32)
    cf[:, 4096:4224] = (s >= sp).astype(np.float32)
    for l in range(7):
        b = 1 << l
        mk = ((s // (2 * b) == sp // (2 * b)) & (s % (2 * b) >= b) & (sp % (2 * b) < b)).astype(np.float32)
        cf[:, 4224 + l * 256: 4224 + l * 256 + 128] = mk
        cf[:, 4224 + l * 256 + 128: 4224 + (l + 1) * 256] = mk.T
    return {"cf": cf}


def load_wpanel(P, wbuf, w_dram, kc_n, c0, ncols):
    src = w_dram[:, c0:c0 + ncols].rearrange("(c p) n -> p c n", p=128)
    half = kc_n // 2 if kc_n >= 2 else kc_n
    P.dma(wbuf[:, 0:half, 0:ncols], src[:, 0:half, :], q="pool")
    if half < kc_n:
        P.dma(wbuf[:, half:kc_n, 0:ncols], src[:, half:kc_n, :], q="pool")


def linear_fm(P, src, kc_n, ntok, w_dram, c0, ncols, wbufs, psums, epilogue, panel=512, state=None):
    if state is None:
        state = {"w": 0, "p": 0}
    npan = (ncols + panel - 1) // panel
    ci = 0
    for pi in range(npan):
        pc = min(panel, ncols - pi * panel)
        wb = wbufs[state["w"] % len(wbufs)]
        state["w"] += 1
        load_wpanel(P, wb, w_dram, kc_n, c0 + pi * panel, pc)
        for j in range(pc // 128):
            pb = psums[state["p"] % len(psums)]
            state["p"] += 1
            for kc in range(kc_n):
                P.mm(pb[:, 0:ntok], wb[:, kc, j * 128:(j + 1) * 128], src[:, kc, 0:ntok],
                     start=(kc == 0), stop=(kc == kc_n - 1))
            epilogue(ci, pb)
            ci += 1
    return state


def phase_norm(P, C, xT, g_dram, hT, T):
    P.begin_phase()
    TT = min(512, T)
    gcol = P.sb([128, KC], F32, "gcol")
    P.dma(gcol[:, :], g_dram.rearrange("(c p) -> p c", p=128), allow_slow_non_contiguous=True)
    xb = [P.sb([128, KC, TT], F32, "nx") for _ in range(2)]
    sq = P.sb([128, KC, TT], F32, "nsq")
    hb = [P.sb([128, KC, TT], BF16, "nh") for _ in range(2)]
    ssq = [P.ps([128, 512], F32, "nssq") for _ in range(2)]
    lnv = P.sb([128, TT], F32, "nln")
    rstd = [P.sb([128, TT], F32, "nrstd") for _ in range(2)]
    xr = xT.rearrange("(c p) t -> p c t", p=128)
    hr = hT.rearrange("(c p) t -> p c t", p=128)
    for ti in range(T // TT):
        x = xb[ti % 2]
        h = hb[ti % 2]
        ps = ssq[ti % 2]
        rs = rstd[ti % 2]
        P.dma(x[:, :, :], xr[:, :, ti * TT:(ti + 1) * TT])
        P.act(sq[:, :, :], x[:, :, :], AF.Square)
        for c in range(KC):
            P.mm(ps[:, 0:TT], C.ones_f[:, :], sq[:, c, :], start=(c == 0), stop=(c == KC - 1))
        P.act(lnv[:, :], ps[:, 0:TT], AF.Ln, bias=C.eps_col[:, :], scale=1.0 / D)
        P.act(rs[:, :], lnv[:, :], AF.Exp, scale=-0.5)
        for c in range(KC):
            P.stt(h[:, c, :], x[:, c, :], gcol[:, c:c + 1], rs[:, :], ALU.mult, ALU.mult)
        P.dma(hr[:, :, ti * TT:(ti + 1) * TT], h[:, :, :])
    P.end_phase()


def phase_ffn(P, C, xT, hT, w_gu, w_down, T):
    P.begin_phase()
    TB = min(512, T)
    HC = FFN_H // 128
    hb = [P.sb([128, KC, TB], BF16, "fh") for _ in range(2)]
    actT = P.sb([128, HC, TB], BF16, "fact")
    wg = [P.sb([128, KC, 512], BF16, "fwg") for _ in range(2)]
    wu = [P.sb([128, KC, 512], BF16, "fwu") for _ in range(2)]
    wd = [P.sb([128, HC, 256], BF16, "fwd") for _ in range(2)]
    sg = [P.sb([128, TB], F32, "fsg") for _ in range(2)]
    xt = [P.sb([128, TB], F32, "fx") for _ in range(3)]
    pg = [P.ps([128, 512], F32, "fpg") for _ in range(2)]
    pu = [P.ps([128, 512], F32, "fpu") for _ in range(2)]
    po = [P.ps([128, 512], F32, "fpo") for _ in range(2)]
    hr = hT.rearrange("(c p) t -> p c t", p=128)
    xr = xT.rearrange("(c p) t -> p c t", p=128)
    cnt = 0
    for bi in range(T // TB):
        h = hb[bi % 2]
        P.dma(h[:, :, :], hr[:, :, bi * TB:(bi + 1) * TB])
        for pi in range(FFN_H // 512):
            g_w = wg[pi % 2]
            u_w = wu[pi % 2]
            load_wpanel(P, g_w, w_gu, KC, pi * 512, 512)
            load_wpanel(P, u_w, w_gu, KC, FFN_H + pi * 512, 512)
            for j in range(4):
                fc = pi * 4 + j
                g_p = pg[cnt % 2]
                u_p = pu[cnt % 2]
                s_b = sg[cnt % 2]
                cnt += 1
                for kc in range(KC):
                    P.mm(g_p[:, 0:TB], g_w[:, kc, j * 128:(j + 1) * 128], h[:, kc, :], start=(kc == 0), stop=(kc == KC - 1))
                for kc in range(KC):
                    P.mm(u_p[:, 0:TB], u_w[:, kc, j * 128:(j + 1) * 128], h[:, kc, :], start=(kc == 0), stop=(kc == KC - 1))
                P.act(s_b[:, :], g_p[:, 0:TB], AF.Silu)
                P.tt(actT[:, fc, :], s_b[:, :], u_p[:, 0:TB], ALU.mult)
        for dp in range(D // 256):
            d_w = wd[dp % 2]
            load_wpanel(P, d_w, w_down, HC, dp * 256, 256)
            for j in range(2):
                dc = dp * 2 + j
                o_p = po[dc % 2]
                x_b = xt[dc % 3]
                P.dma(x_b[:, :], xr[:, dc, bi * TB:(bi + 1) * TB])
                for kc in range(HC):
                    P.mm(o_p[:, 0:TB], d_w[:, kc, j * 128:(j + 1) * 128], actT[:, kc, :], start=(kc == 0), stop=(kc == HC - 1))
                P.tt(x_b[:, :], x_b[:, :], o_p[:, 0:TB], ALU.add)
                P.dma(xr[:, dc, bi * TB:(bi + 1) * TB], x_b[:, :])
    P.end_phase()


def phase_outproj(P, C, xT, srcT, kc_n, w_out, T):
    P.begin_phase()
    TB = min(512, T)
    sb_ = [P.sb([128, kc_n, TB], BF16, "os") for _ in range(2)]
    wb = [P.sb([128, kc_n, 512], BF16, "ow") for _ in range(2)]
    xt = [P.sb([128, TB], F32, "ox") for _ in range(3)]
    po = [P.ps([128, 512], F32, "opo") for _ in range(2)]
    sr = srcT[0:kc_n * 128, :].rearrange("(c p) t -> p c t", p=128)
    xr = xT.rearrange("(c p) t -> p c t", p=128)
    st = {"w": 0, "p": 0}
    for bi in range(T // TB):
        s = sb_[bi % 2]
        P.dma(s[:, :, :], sr[:, :, bi * TB:(bi + 1) * TB])

        def epi(ci, pb, bi=bi):
            x_b = xt[ci % 3]
            P.dma(x_b[:, :], xr[:, ci, bi * TB:(bi + 1) * TB])
            P.tt(x_b[:, :], x_b[:, :], pb[:, 0:TB], ALU.add)
            P.dma(xr[:, ci, bi * TB:(bi + 1) * TB], x_b[:, :])
        linear_fm(P, s, kc_n, TB, w_out, 0, D, wb, po, epi, state=st)
    P.end_phase()


def headnorm_epilogue(P, C, pb, ntok, gcol_view, sqb, pss, lnb, rsb, outv):
    P.act(sqb[:, 0:ntok], pb[:, 0:ntok], AF.Square)
    P.mm(pss[:, 0:ntok], C.ones_f[:, :], sqb[:, 0:ntok], start=True, stop=True)
    P.act(lnb[:, 0:ntok], pss[:, 0:ntok], AF.Ln, bias=C.eps_col[:, :], scale=1.0 / 128)
    P.act(rsb[:, 0:ntok], lnb[:, 0:ntok], AF.Exp, scale=-0.5)
    P.stt(outv, pb[:, 0:ntok], gcol_view, rsb[:, 0:ntok], ALU.mult, ALU.mult)


def phase_sb_proj(P, C, hT, w_in, qn_d, kn_d, qT, kT, v, T):
    P.begin_phase()
    TB = min(512, T)
    H = 16
    hb = [P.sb([128, KC, TB], BF16, "sh") for _ in range(2)]
    wb = [P.sb([128, KC, 512], BF16, "sw") for _ in range(2)]
    gq = P.sb([128, 1], F32, "gq")
    gk = P.sb([128, 1], F32, "gk")
    P.dma(gq[:, :], qn_d.rearrange("(p o) -> p o", o=1))
    P.dma(gk[:, :], kn_d.rearrange("(p o) -> p o", o=1))
    P.ts(gq[:, :], gq[:, :], 128.0 ** -0.5, ALU.mult)
    pq = [P.ps([128, 512], F32, "spq") for _ in range(2)]
    pss = [P.ps([128, 512], F32, "spss") for _ in range(2)]
    pv = [P.ps([128, 512], F32, "spv") for _ in range(2)]
    sqb = [P.sb([128, TB], F32, "ssq") for _ in range(2)]
    lnb = P.sb([128, TB], F32, "sln")
    rsb = [P.sb([128, TB], F32, "srs") for _ in range(2)]
    ob = [P.sb([128, TB], BF16, "sob") for _ in range(3)]
    vb = [P.sb([128, 512], BF16, "svb") for _ in range(3)]
    hr = hT.rearrange("(c p) t -> p c t", p=128)
    st = {"w": 0, "p": 0}
    n = [0]
    for bi in range(T // TB):
        h = hb[bi % 2]
        P.dma(h[:, :, :], hr[:, :, bi * TB:(bi + 1) * TB])
        for which, (c0, gcol, dst) in enumerate(((0, gq, qT), (H * 128, gk, kT))):
            def epi(ci, pb, gcol=gcol, dst=dst, bi=bi):
                i = n[0]
                n[0] += 1
                o = ob[i % 3]
                headnorm_epilogue(P, C, pb, TB, gcol[:, 0:1], sqb[i % 2], pss[i % 2], lnb, rsb[i % 2], o[:, 0:TB])
                P.dma(dst[ci * 128:(ci + 1) * 128, bi * TB:(bi + 1) * TB], o[:, 0:TB])
            linear_fm(P, h, KC, TB, w_in, c0, H * 128, wb, pq, epi, state=st)
        for pi in range(H * 128 // 512):
            w = wb[st["w"] % 2]
            st["w"] += 1
            load_wpanel(P, w, w_in, KC, 2 * H * 128 + pi * 512, 512)
            for tt in range(TB // 128):
                i = n[0]
                n[0] += 1
                p = pv[i % 2]
                for kc in range(KC):
                    P.mm(p[:, :], h[:, kc, tt * 128:(tt + 1) * 128], w[:, kc, :], start=(kc == 0), stop=(kc == KC - 1))
                o = vb[i % 3]
                P.copy(o[:, :], p[:, :], eng="act")
                P.dma(v[bi * TB + tt * 128: bi * TB + (tt + 1) * 128, pi * 512:(pi + 1) * 512], o[:, :])
    P.end_phase()


def phase_sb_attn(P, C, qT, kT, v, oT, T):
    P.begin_phase()
    H = 16
    NS = 2
    NB = T // 128
    QT = min(256, T)

    def mkset(i):
        B = {}
        B["q"] = P.sb([128, T], BF16, "aq")
        B["k"] = P.sb([128, T], BF16, "ak")
        B["v"] = P.sb([128, NB, 128], BF16, "av")
        B["pb"] = [P.ps([128, 512], F32, "apb") for _ in range(2)]
        B["po"] = P.ps([128, 512], F32, "apo")
        B["e"] = [P.sb([128, QT], F32, "ae") for _ in range(2)]
        B["sp"] = [P.sb([128, QT], BF16, "asp") for _ in range(2)]
        B["d"] = [P.sb([128, QT], F32, "ad") for _ in range(2)]
        B["w"] = [P.sb([128, QT], BF16, "aw") for _ in range(2)]
        B["cs"] = [P.sb([128, QT], F32, "acs") for _ in range(2)]
        B["ob"] = [P.sb([128, QT], BF16, "aob") for _ in range(2)]
        return B
    sets = [mkset(i) for i in range(NS)]

    def stream(h, B):
        q, k, vv = B["q"], B["k"], B["v"]
        P.dma(q[:, :], qT[h * 128:(h + 1) * 128, :])
        P.dma(k[:, :], kT[h * 128:(h + 1) * 128, :])
        P.dma(vv[:, :, :], v[:, h * 128:(h + 1) * 128].rearrange("(nb s) d -> s nb d", s=128))
        yield
        uc = [0]
        for qi in range(T // QT):
            o_p = B["po"]
            cs = B["cs"][qi % 2]
            nsub = QT // 128
            mlast = qi * nsub + nsub - 1
            qv = q[:, qi * QT:(qi + 1) * QT]
            units = []
            for m in range(mlast, -1, -1):
                u = uc[0]
                uc[0] += 1
                units.append(dict(m=m, z=B["pb"][u % 2][:, 0:QT], c=B["pb"][u % 2][:, 256:256 + QT],
                                  e=B["e"][u % 2], s=B["sp"][u % 2], d=B["d"][u % 2], w=B["w"][u % 2],
                                  first=(m == mlast), diag=(m >= qi * nsub), j=m - qi * nsub))

            def front(U):
                m = U["m"]
                P.mm(U["z"], k[:, m * 128:(m + 1) * 128], qv, start=True, stop=False)
                yield
                P.act(U["e"][:, :], U["z"], AF.Exp)
                yield
                P.act(U["s"][:, :], U["e"][:, :], AF.Ln, bias=C.one_col[:, :])
                yield
                if U["diag"]:
                    P.tt(U["s"][:, :], U["s"][:, :], C.sbmask[:, U["j"], 0:QT], ALU.mult, eng="pool")
                    yield
                P.mm(U["z"], C.negtri_b[:, :], U["s"][:, :], start=False, stop=True)
                if m > 0:
                    P.mm(U["c"], C.ones_b[:, :], U["s"][:, :], start=True, stop=True)
                yield

            def back(U):
                m = U["m"]
                if U["first"]:
                    P.act(U["w"][:, :], U["z"], AF.Exp)
                    if m > 0:
                        P.copy(cs[:, :], U["c"], eng="dve")
                else:
                    P.tt(U["d"][:, :], U["z"], cs[:, :], ALU.subtract)
                    if m > 0:
                        P.tt(cs[:, :], cs[:, :], U["c"], ALU.add)
                    yield
                    P.act(U["w"][:, :], U["d"][:, :], AF.Exp)
                yield
                if U["diag"]:
                    P.tt(U["w"][:, :], U["w"][:, :], C.sbmask[:, U["j"], 0:QT], ALU.mult, eng="pool")
                    yield
                P.mm(o_p[:, 0:QT], vv[:, m, :], U["w"][:, :], start=U["first"], stop=(m == 0))
                yield

            yield from front(units[0])
            for ki in range(len(units)):
                if ki + 1 < len(units):
                    yield from front(units[ki + 1])
                yield from back(units[ki])
            o_b = B["ob"][qi % 2]
            P.copy(o_b[:, :], o_p[:, 0:QT], eng="act")
            P.dma(oT[h * 128:(h + 1) * 128, qi * QT:(qi + 1) * QT], o_b[:, :])
            yield

    for h0 in range(0, H, NS):
        gens = [stream(h0 + i, sets[i]) for i in range(NS)]
        active = list(gens)
        while active:
            nxt = []
            for gnr in active:
                try:
                    next(gnr)
                    nxt.append(gnr)
                except StopIteration:
                    pass
            active = nxt
    P.end_phase()


def phase_lru_proj(P, C, hT, w_in, gT, xrT, T):
    P.begin_phase()
    TB = min(512, T)
    hb = [P.sb([128, KC, TB], BF16, "lh") for _ in range(2)]
    wb = [P.sb([128, KC, 512], BF16, "lw") for _ in range(2)]
    pq = [P.ps([128, 512], F32, "lpq") for _ in range(2)]
    gb = [P.sb([128, TB], BF16, "lgb") for _ in range(3)]
    xb = [P.sb([128, TB], F32, "lxb") for _ in range(3)]
    hr = hT.rearrange("(c p) t -> p c t", p=128)
    st = {"w": 0, "p": 0}
    for bi in range(T // TB):
        h = hb[bi % 2]
        P.dma(h[:, :, :], hr[:, :, bi * TB:(bi + 1) * TB])

        def epi(ci, pb, bi=bi):
            if ci < 16:
                o = gb[ci % 3]
                P.act(o[:, :], pb[:, 0:TB], AF.Gelu_apprx_tanh)
                P.dma(gT[ci * 128:(ci + 1) * 128, bi * TB:(bi + 1) * TB], o[:, :])
            else:
                o = xb[ci % 3]
                P.copy(o[:, :], pb[:, 0:TB], eng="dve")
                P.dma(xrT[(ci - 16) * 128:(ci - 15) * 128, bi * TB:(bi + 1) * TB], o[:, :])
        linear_fm(P, h, KC, TB, w_in, 0, 2 * D, wb, pq, epi, state=st)
    P.end_phase()


def phase_lru_rec(P, C, gT, xrT, yT, conv_w, conv_b, w_a, b_a, w_x, b_x, lam, T):
    P.begin_phase()
    TT = min(512, T)
    cw = P.sb([128, 4, KC], F32, "rcw")
    cb = P.sb([128, KC], F32, "rcb")
    ba = P.sb([128, KC], F32, "rba")
    bx = P.sb([128, KC], F32, "rbx")
    lm = P.sb([128, KC], F32, "rlm")
    cc = P.sb([128, KC], F32, "rcc")
    for k in range(4):
        P.dma(cw[:, k, :], conv_w[k].rearrange("(c p) -> p c", p=128), allow_slow_non_contiguous=True)
    P.dma(cb[:, :], conv_b.rearrange("(c p) -> p c", p=128), allow_slow_non_contiguous=True)
    P.dma(ba[:, :], b_a.rearrange("n (c p) -> p (n c)", p=128), allow_slow_non_contiguous=True)
    P.dma(bx[:, :], b_x.rearrange("n (c p) -> p (n c)", p=128), allow_slow_non_contiguous=True)
    P.dma(lm[:, :], lam.rearrange("(c p) -> p c", p=128), allow_slow_non_contiguous=True)
    P.act(cc[:, :], lm[:, :], AF.Exp, scale=-1.0)
    P.act(cc[:, :], cc[:, :], AF.Ln, bias=C.one_col[:, :])
    P.ts(cc[:, :], cc[:, :], -8.0, ALU.mult)
    xp = P.sb([128, 2, T + 3], F32, "rxp")
    xc = P.sb([128, 2, T], F32, "rxc")
    xcb = P.sb([128, 2, T], BF16, "rxcb")
    wab = [P.sb([128, 2, 256], BF16, "rwa") for _ in range(2)]
    wxb = [P.sb([128, 2, 256], BF16, "rwx") for _ in range(2)]
    a_b = P.sb([128, T], F32, "ra")
    u_b = P.sb([128, T], F32, "ru")
    hs = P.sb([128, T], F32, "rhs")
    g_b = [P.sb([128, T], BF16, "rg") for _ in range(2)]
    y_b = [P.sb([128, T], BF16, "ry") for _ in range(2)]
    pr = [P.ps([128, 512], F32, "rpr") for _ in range(2)]
    pi_ = [P.ps([128, 512], F32, "rpi") for _ in range(2)]
    r_t = [P.sb([128, TT], F32, "rrt") for _ in range(2)]
    i_t = [P.sb([128, TT], F32, "rit") for _ in range(2)]
    a2 = [P.sb([128, TT], F32, "ra2") for _ in range(2)]
    s_t = [P.sb([128, TT], F32, "rst") for _ in range(2)]
    P.memset(xp[:, :, 0:3], 0.0)
    n = 0
    for blk in range(8):
        wa_, wx_ = wab[blk % 2], wxb[blk % 2]
        P.dma(wa_[:, :, :], w_a[blk].rearrange("(c p) n -> p c n", p=128), q="pool")
        P.dma(wx_[:, :, :], w_x[blk].rearrange("(c p) n -> p c n", p=128), q="pool")
        for c in range(2):
            ch = blk * 2 + c
            P.dma(xp[:, c, 3:T + 3], xrT[ch * 128:(ch + 1) * 128, :])
            P.ts(xc[:, c, :], xp[:, c, 0:T], cw[:, 0, ch:ch + 1], ALU.mult, cb[:, ch:ch + 1], ALU.add)
            for k in range(1, 4):
                P.stt(xc[:, c, :], xp[:, c, k:T + k], cw[:, k, ch:ch + 1], xc[:, c, :], ALU.mult, ALU.add)
            P.copy(xcb[:, c, :], xc[:, c, :], eng="act")
        for jo in range(2):
            ch = blk * 2 + jo
            g = g_b[ch % 2]
            P.dma(g[:, :], gT[ch * 128:(ch + 1) * 128, :])
            for tt in range(T // TT):
                sl = slice(tt * TT, (tt + 1) * TT)
                p_r, p_i = pr[n % 2], pi_[n % 2]
                rt, it, a2t, stt_ = r_t[n % 2], i_t[n % 2], a2[n % 2], s_t[n % 2]
                n += 1
                for ki in range(2):
                    P.mm(p_r[:, 0:TT], wa_[:, ki, jo * 128:(jo + 1) * 128], xcb[:, ki, sl], start=(ki == 0), stop=(ki == 1))
                for ki in range(2):
                    P.mm(p_i[:, 0:TT], wx_[:, ki, jo * 128:(jo + 1) * 128], xcb[:, ki, sl], start=(ki == 0), stop=(ki == 1))
                P.act(rt[:, :], p_r[:, 0:TT], AF.Sigmoid, bias=ba[:, ch:ch + 1])
                P.act(it[:, :], p_i[:, 0:TT], AF.Sigmoid, bias=bx[:, ch:ch + 1])
                P.act(a_b[:, sl], rt[:, :], AF.Exp, scale=cc[:, ch:ch + 1])
                P.ts(a_b[:, sl], a_b[:, sl], 1.0, ALU.min)
                P.tt(a2t[:, :], a_b[:, sl], a_b[:, sl], ALU.mult)
                P.act(stt_[:, :], a2t[:, :], AF.Sqrt, bias=C.one_col[:, :], scale=-1.0)
                P.tt(it[:, :], it[:, :], xc[:, jo, sl], ALU.mult, eng="pool")
                P.tt(u_b[:, sl], it[:, :], stt_[:, :], ALU.mult)
            P.scan(hs[:, :], a_b[:, :], u_b[:, :], 0.0, ALU.mult, ALU.add)
            y = y_b[ch % 2]
            P.tt(y[:, :], hs[:, :], g[:, :], ALU.mult, eng="pool")
            P.dma(yT[ch * 128:(ch + 1) * 128, :], y[:, :])
    P.end_phase()


TWO_PI = 6.283185307179586
CW1 = 6.28125
CW2 = TWO_PI - CW1
DSW_DIL = (1, 4, 16)


def phase_rope_tables(P, C, pos, ctab, stab, T):
    P.begin_phase()
    TT = min(512, T)
    posi = [P.sb([1, TT], I32, "tpi") for _ in range(2)]
    posf = [P.sb([1, TT], F32, "tpf") for _ in range(2)]
    pb = [P.ps([128, 512], F32, "tpb") for _ in range(2)]
    ang = P.sb([32, TT], F32, "tang")
    ki = P.sb([32, TT], I32, "tki")
    kf = P.sb([32, TT], F32, "tkf")
    r = P.sb([32, TT], F32, "tr")
    y = P.sb([32, TT], F32, "ty")
    m = P.sb([32, TT], F32, "tm")
    sn = P.sb([32, TT], F32, "tsn")
    ct = [P.sb([128, TT], F32, "tct") for _ in range(2)]
    st_ = [P.sb([128, TT], F32, "tst") for _ in range(2)]
    for i in range(2):
        P.memset(ct[i][:, :], 1.0)
        P.memset(st_[i][:, :], 0.0)
    for ti in range(T // TT):
        sl = slice(ti * TT, (ti + 1) * TT)
        pi_, pf, p_b, c_t, s_t = posi[ti % 2], posf[ti % 2], pb[ti % 2], ct[ti % 2], st_[ti % 2]
        P.dma(pi_[:, :], pos[0:1, sl])
        P.copy(pf[:, :], pi_[:, :], eng="dve")
        P.mm(p_b[0:32, 0:TT], C.ones_f[0:1, 0:32], pf[:, :], start=True, stop=True)
        P.ts(ang[:, :], p_b[0:32, 0:TT], C.invf[0:32, 0:1], ALU.mult)
        P.ts(ki[:, :], ang[:, :], 1.0 / TWO_PI, ALU.mult)
        P.copy(kf[:, :], ki[:, :], eng="dve")
        P.stt(r[:, :], kf[:, :], -CW1, ang[:, :], ALU.mult, ALU.add)
        P.stt(r[:, :], kf[:, :], -CW2, r[:, :], ALU.mult, ALU.add)
        for shift, dst, sgn in ((0.0, s_t, True), (TWO_PI / 4, c_t, False)):
            P.ts(y[:, :], r[:, :], shift, ALU.add)
            P.ts(m[:, :], y[:, :], TWO_PI / 2, ALU.is_gt, -TWO_PI, ALU.mult)
            P.tt(y[:, :], y[:, :], m[:, :], ALU.add)
            P.ts(m[:, :], y[:, :], -TWO_PI / 2, ALU.is_lt, TWO_PI, ALU.mult)
            P.tt(y[:, :], y[:, :], m[:, :], ALU.add)
            if sgn:
                P.act(sn[:, :], y[:, :], AF.Sin)
                P.ts(dst[0:32, :], sn[:, :], C.invf[0:32, 1:2], ALU.mult)
            else:
                P.act(dst[0:32, :], y[:, :], AF.Sin)
        P.dma(ctab[:, sl], c_t[:, :])
        P.dma(stab[:, sl], s_t[:, :])
    P.end_phase()


def phase_dsw_proj(P, C, hT, w_in, qn_d, kn_d, ctab, stab, qT, kT, v, T):
    P.begin_phase()
    TB = min(512, T)
    H = 18
    hb = [P.sb([128, KC, TB], BF16, "dh") for _ in range(2)]
    wb = [P.sb([128, KC, 512], BF16, "dw") for _ in range(2)]
    gq = P.sb([128, 1], F32, "dgq")
    gk = P.sb([128, 1], F32, "dgk")
    P.dma(gq[:, :], qn_d.rearrange("(p o) -> p o", o=1))
    P.dma(gk[:, :], kn_d.rearrange("(p o) -> p o", o=1))
    P.ts(gq[:, :], gq[:, :], 128.0 ** -0.5, ALU.mult)
    pq = [P.ps([128, 512], F32, "dpq") for _ in range(2)]
    pss = [P.ps([128, 512], F32, "dpss") for _ in range(2)]
    pr = [P.ps([128, 512], F32, "dpr") for _ in range(2)]
    pv = [P.ps([128, 512], F32, "dpv") for _ in range(2)]
    sqb = [P.sb([128, TB], F32, "dsq") for _ in range(2)]
    lnb = P.sb([128, TB], F32, "dln")
    rsb = [P.sb([128, TB], F32, "drs") for _ in range(2)]
    qnb = [P.sb([128, TB], BF16, "dqn") for _ in range(2)]
    t1 = [P.sb([128, TB], F32, "dt1") for _ in range(2)]
    t2 = [P.sb([128, TB], F32, "dt2") for _ in range(2)]
    ob = [P.sb([128, TB], BF16, "dob") for _ in range(3)]
    vb = [P.sb([128, 512], BF16, "dvb") for _ in range(3)]
    cb = [P.sb([128, TB], F32, "dcb") for _ in range(2)]
    sb_ = [P.sb([128, TB], F32, "dsb") for _ in range(2)]
    hr = hT.rearrange("(c p) t -> p c t", p=128)
    st = {"w": 0, "p": 0}
    n = [0]
    for bi in range(T // TB):
        h = hb[bi % 2]
        c_t, s_t = cb[bi % 2], sb_[bi % 2]
        tsl = slice(bi * TB, (bi + 1) * TB)
        P.dma(h[:, :, :], hr[:, :, tsl])
        P.dma(c_t[:, :], ctab[:, tsl])
        P.dma(s_t[:, :], stab[:, tsl])
        for which, (c0, gcol, dst) in enumerate(((0, gq, qT), (H * 128, gk, kT))):
            def epi(ci, pb, gcol=gcol, dst=dst, tsl=tsl, c_t=c_t, s_t=s_t):
                i = n[0]
                n[0] += 1
                qn = qnb[i % 2]
                headnorm_epilogue(P, C, pb, TB, gcol[:, 0:1], sqb[i % 2], pss[i % 2], lnb, rsb[i % 2], qn[:, 0:TB])
                p_r = pr[i % 2]
                P.mm(p_r[:, 0:TB], C.perm_b[:, :], qn[:, 0:TB], start=True, stop=True)
                P.tt(t1[i % 2][:, :], p_r[:, 0:TB], s_t[:, :], ALU.mult)
                P.tt(t2[i % 2][:, :], qn[:, 0:TB], c_t[:, :], ALU.mult, eng="pool")
                o = ob[i % 3]
                P.tt(o[:, 0:TB], t1[i % 2][:, :], t2[i % 2][:, :], ALU.add)
                P.dma(dst[ci * 128:(ci + 1) * 128, tsl], o[:, 0:TB])
            linear_fm(P, h, KC, TB, w_in, c0, H * 128, wb, pq, epi, state=st)
        ncol = H * 128
        for pi in range((ncol + 511) // 512):
            pc = min(512, ncol - pi * 512)
            w = wb[st["w"] % 2]
            st["w"] += 1
            load_wpanel(P, w, w_in, KC, 2 * H * 128 + pi * 512, pc)
            for tt in range(TB // 128):
                i = n[0]
                n[0] += 1
                p = pv[i % 2]
                for kc in range(KC):
                    P.mm(p[:, 0:pc], h[:, kc, tt * 128:(tt + 1) * 128], w[:, kc, 0:pc], start=(kc == 0), stop=(kc == KC - 1))
                o = vb[i % 3]
                P.copy(o[:, 0:pc], p[:, 0:pc], eng="act")
                P.dma(v[bi * TB + tt * 128: bi * TB + (tt + 1) * 128, pi * 512:pi * 512 + pc], o[:, 0:pc])
    P.end_phase()


def phase_dsw_attn(P, C, qT, kT, v, oT, T):
    P.begin_phase()
    qnat = P.sb([128, T], BF16, "wqn")
    knat = P.sb([128, T], BF16, "wkn")
    qd = P.sb([128, T], BF16, "wqd")
    kd = P.sb([128, T], BF16, "wkd")
    vd = [P.sb([128, T // 128, 128], BF16, "wvd") for _ in range(2)]
    Og = [P.sb([128, T], F32, "wO") for _ in range(3)]
    Dg = [P.sb([128, T], F32, "wD") for _ in range(3)]
    outb = [P.sb([128, T], BF16, "wout") for _ in range(2)]
    psc = [P.ps([128, 512], F32, "wpsc") for _ in range(2)]
    psp = [P.ps([128, 512], F32, "wpsp") for _ in range(2)]
    po = [P.ps([128, 512], F32, "wpo") for _ in range(2)]
    pd = [P.ps([128, 512], F32, "wpd") for _ in range(2)]
    pcb = [P.sb([128, 512], BF16, "wpc") for _ in range(2)]
    ppb = [P.sb([128, 512], BF16, "wpp") for _ in range(2)]
    u = 0
    hcount = 0
    for hg in range(6):
        for g in range(3):
            d = DSW_DIL[g]
            head = g * 6 + hg
            L = T // d
            NBs = L // 128
            vv = vd[hcount % 2]
            hcount += 1
            P.dma(qnat[:, :], qT[head * 128:(head + 1) * 128, :])
            P.dma(knat[:, :], kT[head * 128:(head + 1) * 128, :])
            vsrc = v[:, head * 128:(head + 1) * 128].rearrange("(nb j r) c -> r j nb c", j=128, r=d)
            for r in range(d):
                P.dma(vv[:, r * NBs:(r + 1) * NBs, :], vsrc[r])
            if d == 1:
                qdv, kdv = qnat, knat
            else:
                P.copy(qd.rr("p (r n) -> p r n", r=d), qnat.rr("p (n r) -> p r n", r=d), eng="pool")
                P.copy(kd.rr("p (r n) -> p r n", r=d), knat.rr("p (n r) -> p r n", r=d), eng="pool")
                qdv, kdv = qd, kd
            Onat = Og[g].rr("p (n r) -> p r n", r=d)
            Dnat = Dg[g].rr("p (n r) -> p r n", r=d)
            for r in range(d):
                for q0 in range(0, NBs, 4):
                    nq = min(4, NBs - q0)
                    W = nq * 128
                    s_c, s_p, o_p, d_p = psc[u % 2], psp[u % 2], po[u % 2], pd[u % 2]
                    p_c, p_p = pcb[u % 2], ppb[u % 2]
                    u += 1
                    base = r * L
                    for b in range(nq):
                        qb = q0 + b
                        qs = slice(base + qb * 128, base + (qb + 1) * 128)
                        P.mm(s_c[:, b * 128:(b + 1) * 128], kdv[:, qs], qdv[:, qs], start=True, stop=True)
                        if qb > 0:
                            ks = slice(base + (qb - 1) * 128, base + qb * 128)
                            P.mm(s_p[:, b * 128:(b + 1) * 128], kdv[:, ks], qdv[:, qs], start=True, stop=True)
                    P.act(p_c[:, 0:W], s_c[:, 0:W], AF.Exp)
                    P.tt(p_c[:, 0:W], p_c[:, 0:W], C.dmask[:, 0, 0:W], ALU.mult, eng="pool")
                    lo = 128 if q0 == 0 else 0
                    if lo:
                        P.memset(p_p[:, 0:128], 0.0)
                    if W > lo:
                        P.act(p_p[:, lo:W], s_p[:, lo:W], AF.Exp)
                        P.tt(p_p[:, lo:W], p_p[:, lo:W], C.dmask[:, 1, lo:W], ALU.mult, eng="pool")
                    for b in range(nq):
                        qb = q0 + b
                        cs = slice(b * 128, (b + 1) * 128)
                        P.mm(o_p[:, cs], vv[:, r * NBs + qb, :], p_c[:, cs], start=True, stop=(qb == 0))
                        if qb > 0:
                            P.mm(o_p[:, cs], vv[:, r * NBs + qb - 1, :], p_p[:, cs], start=False, stop=True)
                    P.mm(d_p[:, 0:W], C.ones_b[:, :], p_c[:, 0:W], start=True, stop=False)
                    P.mm(d_p[:, 0:W], C.ones_b[:, :], p_p[:, 0:W], start=False, stop=True)
                    ns = slice(q0 * 128, q0 * 128 + W)
                    P.copy(Onat[:, r, ns], o_p[:, 0:W], eng="act")
                    P.copy(Dnat[:, r, ns], d_p[:, 0:W], eng="dve")
        P.tt(Dg[0][:, :], Dg[0][:, :], Dg[1][:, :], ALU.add)
        P.tt(Dg[0][:, :], Dg[0][:, :], Dg[2][:, :], ALU.add)
        P.recip(Dg[0][:, :], Dg[0][:, :])
        for g in range(3):
            head = g * 6 + hg
            ob_ = outb[g % 2]
            P.tt(ob_[:, :], Og[g][:, :], Dg[0][:, :], ALU.mult, eng=("pool" if g == 1 else "dve"))
            P.dma(oT[head * 128:(head + 1) * 128, :], ob_[:, :])
    P.end_phase()


GQ0, GK0, GV0, GZ0, GB0 = 0, 2048, 4096, 8192, 12288


def phase_gdn_proj(P, C, hT, w_in, a_log, dt_bias, pre, zs, beta_d, g_d, T):
    P.begin_phase()
    TB = min(512, T)
    hb = [P.sb([128, KC, TB], BF16, "gh") for _ in range(2)]
    wb = [P.sb([128, KC, 512], BF16, "gw") for _ in range(2)]
    wba = P.sb([128, KC, 64], BF16, "gwba")
    pq = [P.ps([128, 512], F32, "gpq") for _ in range(2)]
    pb_ = [P.ps([128, 512], F32, "gpb") for _ in range(2)]
    ob = [P.sb([128, TB], BF16, "gob") for _ in range(3)]
    row = P.sb([1, 64], F32, "grow")
    negea = P.sb([128, 32], F32, "gnegea")
    dtb = P.sb([128, 32], F32, "gdtb")
    bt = [P.sb([128, 32], F32, "gbt") for _ in range(2)]
    tt_ = [P.sb([128, 32], F32, "gtt") for _ in range(2)]
    gt = [P.sb([128, 32], F32, "ggt") for _ in range(2)]
    P.dma(row[:, 0:32], a_log.rearrange("(o n) -> o n", o=1))
    P.dma(row[:, 32:64], dt_bias.rearrange("(o n) -> o n", o=1))
    P.mm(pb_[0][:, 0:64], C.ones_f[0:1, :], row[:, :], start=True, stop=True)
    P.act(negea[:, :], pb_[0][:, 0:32], AF.Exp)
    P.ts(negea[:, :], negea[:, :], -1.0, ALU.mult)
    P.copy(dtb[:, :], pb_[0][:, 32:64], eng="dve")
    load_wpanel(P, wba, w_in, KC, GB0, 64)
    hr = hT.rearrange("(c p) t -> p c t", p=128)
    st = {"w": 0, "p": 0}
    n = 0
    for bi in range(T // TB):
        h = hb[bi % 2]
        tsl = slice(bi * TB, (bi + 1) * TB)
        P.dma(h[:, :, :], hr[:, :, tsl])

        def epi(ci, pb, tsl=tsl):
            o = ob[ci % 3]
            if ci < 64:
                P.copy(o[:, :], pb[:, 0:TB], eng=("act" if ci % 2 else "dve"))
                P.dma(pre[ci * 128:(ci + 1) * 128, tsl], o[:, :])
            else:
                P.act(o[:, :], pb[:, 0:TB], AF.Silu)
                P.dma(zs[(ci - 64) * 128:(ci - 63) * 128, tsl], o[:, :])
        linear_fm(P, h, KC, TB, w_in, 0, GB0, wb, pq, epi, state=st)
        for t4 in range(TB // 128):
            p = pb_[n % 2]
            b_t, t_t, g_t = bt[n % 2], tt_[n % 2], gt[n % 2]
            n += 1
            for kc in range(KC):
                P.mm(p[:, 0:64], h[:, kc, t4 * 128:(t4 + 1) * 128], wba[:, kc, :], start=(kc == 0), stop=(kc == KC - 1))
            rows = slice(bi * TB + t4 * 128, bi * TB + (t4 + 1) * 128)
            P.act(b_t[:, :], p[:, 0:32], AF.Sigmoid)
            P.dma(beta_d[rows, :], b_t[:, :])
            P.tt(t_t[:, :], p[:, 32:64], dtb[:, :], ALU.add)
            P.act(t_t[:, :], t_t[:, :], AF.Exp)
            P.act(t_t[:, :], t_t[:, :], AF.Ln, bias=C.one_col[:, :])
            P.tt(g_t[:, :], t_t[:, :], negea[:, :], ALU.mult)
            P.dma(g_d[rows, :], g_t[:, :])
    P.end_phase()


def phase_gdn_conv(P, C, pre, conv_w, post, T):
    P.begin_phase()
    TT = min(512, T)
    NCHN = 64
    cw = P.sb([128, 4, NCHN], F32, "ccw")
    for k in range(4):
        P.dma(cw[:, k, :], conv_w[k].rearrange("(c p) -> p c", p=128), allow_slow_non_contiguous=True)
    xp = [P.sb([128, T + 3], BF16, "cxp") for _ in range(2)]
    acc = [P.sb([128, T], F32, "cacc") for _ in range(2)]
    sl_ = [P.sb([128, T], F32, "csl") for _ in range(2)]
    ob = [P.sb([128, T], BF16, "cob") for _ in range(2)]
    sqb = [P.sb([128, TT], F32, "csq") for _ in range(2)]
    lnb = P.sb([128, TT], F32, "cln")
    rsb = [P.sb([128, TT], F32, "crs") for _ in range(2)]
    pss = [P.ps([128, 512], F32, "cpss") for _ in range(2)]
    for i in range(2):
        P.memset(xp[i][:, 0:3], 0.0)
    n = 0
    for ch in range(NCHN):
        x, a, s_, o = xp[ch % 2], acc[ch % 2], sl_[ch % 2], ob[ch % 2]
        P.dma(x[:, 3:T + 3], pre[ch * 128:(ch + 1) * 128, :])
        P.ts(a[:, :], x[:, 0:T], cw[:, 0, ch:ch + 1], ALU.mult)
        for k in range(1, 4):
            P.stt(a[:, :], x[:, k:T + k], cw[:, k, ch:ch + 1], a[:, :], ALU.mult, ALU.add)
        if ch >= 32:
            P.act(o[:, :], a[:, :], AF.Silu)
        else:
            P.act(s_[:, :], a[:, :], AF.Silu)
            sc = (128.0 ** -0.5) if ch < 16 else 1.0
            for ti in range(T // TT):
                tsl = slice(ti * TT, (ti + 1) * TT)
                sq, ps, rs = sqb[n % 2], pss[n % 2], rsb[n % 2]
                n += 1
                P.tt(sq[:, :], s_[:, tsl], s_[:, tsl], ALU.mult, eng="pool")
                P.mm(ps[:, 0:TT], C.ones_f[:, :], sq[:, :], start=True, stop=True)
                P.act(lnb[:, :], ps[:, 0:TT], AF.Ln, bias=C.eps_col[:, :])
                P.act(rs[:, :], lnb[:, :], AF.Exp, scale=-0.5)
                P.stt(o[:, tsl], s_[:, tsl], sc, rs[:, :], ALU.mult, ALU.mult)
        P.dma(post[ch * 128:(ch + 1) * 128, :], o[:, :])
    P.end_phase()


def phase_gdn_core(P, C, post, zs, beta_d, g_d, onorm_d, oT, T):
    P.begin_phase()
    GE = DBG_GE
    NCH = T // 128
    TT = min(256, T)
    W = NCH * 32
    G = 6
    f = lambda nm: P.sb([128, NCH, 32], F32, nm)
    beta, gg, gc, c2, nbeta, dcol, egl, glb = f("kb"), f("kg"), f("kgc"), f("kc2"), f("knb"), f("kdc"), f("keg"), f("kgl")
    gon = P.sb([128, 1], F32, "kgon")
    P.dma(gon[:, :], onorm_d.rearrange("(p o) -> p o", o=1))
    P.dma(beta[:, :, :], beta_d.rearrange("(c p) n -> p c n", p=128))
    P.dma(gg[:, :, :], g_d.rearrange("(c p) n -> p c n", p=128))
    bF = [P.ps([128, 512], F32, "kF%d" % g) for g in range(G)]
    tC = P.ps([128, 512], F32, "kC")
    tN = P.ps([128, 512], F32, "kN")
    pKV = [bF[g][:, 0:128].bc(BF16) for g in range(G)]
    pNA = pKV
    pGR = [bF[g][:, 128:256] for g in range(G)]
    pD = [bF[g][:, 256:512] for g in range(G)]
    pA = [bF[g][:, 0:256] for g in range(G)]
    pWS = tC[:, 0:128]
    pSU = tC[:, 128:256]
    pNrm = tN[:, 0:256]
    pOT = tN[:, 384:512]
    gflat = gg.rr("p c n -> p (c n)")
    gcflat = gc.rr("p c n -> p (c n)")
    glflat = glb.rr("p c n -> p (c n)")
    for c0 in range(0, W, 512):
        w_ = min(512, W - c0)
        P.mm(tC[:, 0:w_], C.triu_f[:, :], gflat[:, c0:c0 + w_], start=True, stop=True)
        P.copy(gcflat[:, c0:c0 + w_], tC[:, 0:w_], eng="dve")
    for c0 in range(0, W, 512):
        w_ = min(512, W - c0)
        P.mm(tC[:, 0:w_], C.sellast_f[:, :], gcflat[:, c0:c0 + w_], start=True, stop=True)
        P.copy(glflat[:, c0:c0 + w_], tC[:, 0:w_], eng="dve")
    P.act(c2[:, :, :], gc[:, :, :], AF.Exp)
    P.tt(c2[:, :, :], c2[:, :, :], beta[:, :, :], ALU.mult)
    P.ts(nbeta[:, :, :], beta[:, :, :], -1.0, ALU.mult)
    P.tt(dcol[:, :, :], glb[:, :, :], gc[:, :, :], ALU.subtract)
    P.act(dcol[:, :, :], dcol[:, :, :], AF.Exp)
    P.act(egl[:, :, :], glb[:, :, :], AF.Exp)

    kT = [P.sb([128, T], BF16, "kkT") for _ in range(2)]
    qT = [P.sb([128, T], BF16, "kqT") for _ in range(2)]
    vT = [P.sb([128, T], BF16, "kvT") for _ in range(2)]
    zb = [P.sb([128, T], BF16, "kzb") for _ in range(2)]
    oh = P.sb([128, T], F32, "koh")
    outb = [P.sb([128, T], BF16, "koutb") for _ in range(2)]
    S32 = P.sb([128, 128], F32, "kS32")
    Sbf = P.sb([128, 128], BF16, "kSbf")
    nbG = lambda shape, dt, nm: [P.sb(shape, dt, nm) for _ in range(G)]
    nb2G = lambda shape, dt, nm: [P.sb(shape, dt, nm) for _ in range(2 * G)]
    R = nbG([128, 256], BF16, "kR")
    Gb = nbG([128, 128], F32, "kGb")
    growS = nbG([128, 128], F32, "kgrowS")
    Dn = nbG([128, 128], F32, "kDn")
    E = nbG([128, 128], F32, "kE")
    Eg = nbG([128, 128], F32, "kEg")
    M0f = nbG([128, 128], F32, "kM0f")
    Atf = nbG([128, 128], F32, "kAtf")
    Atb = nbG([128, 128], BF16, "kAtb")
    MN = nbG([128, 256], BF16, "kMN")
    TTm = nbG([128, 256], BF16, "kTT")
    Xs = nbG([128, 256], BF16, "kXs")
    Kdec = nb2G([128, 128], BF16, "kKd")
    QgT = nb2G([128, 128], BF16, "kQg")
    AtT = nb2G([128, 128], BF16, "kAtT")
    UW = nb2G([128, 256], BF16, "kUW")
    wTb = nb2G([128, 128], BF16, "kwT")
    vnew = nb2G([128, 128], BF16, "kvn")
    sqb = [P.sb([128, TT], F32, "ksq") for _ in range(2)]
    lnb = P.sb([128, TT], F32, "kln")
    rsb = [P.sb([128, TT], F32, "krs") for _ in range(2)]
    tmpb = [P.sb([128, TT], F32, "ktmp") for _ in range(2)]

    def stages(hv, c, g, s2):
        hp = hv % 2
        sl = slice(c * 128, (c + 1) * 128)
        kt, qt, vt = kT[hp], qT[hp], vT[hp]
        kv, na, dd, aa, gr = pKV[g], pNA[g], pD[g], pA[g], pGR[g]
        L = []

        def st0():
            P.transpose(kv[:, 0:128], kt[:, sl], C.ident_b[:, :])
            P.transpose(kv[:, 128:256], vt[:, sl], C.ident_b[:, :])
            P.ts(Gb[g][:, :], C.ones_f[:, :], gg[:, c, hv:hv + 1], ALU.mult)
        L.append(st0)

        def st1():
            P.ts(R[g][:, 0:128], kv[:, 128:256], beta[:, c, hv:hv + 1], ALU.mult)
            P.ts(R[g][:, 128:256], kv[:, 0:128], c2[:, c, hv:hv + 1], ALU.mult)
            P.ts(Kdec[s2][:, :], kv[:, 0:128], dcol[:, c, hv:hv + 1], ALU.mult)
            P.mm(gr[:, 0:128], Gb[g][:, :], C.triu_f[:, :], start=True, stop=True)
            P.mm(dd[:, 0:128], kt[:, sl], kt[:, sl], start=True, stop=True)
            P.mm(dd[:, 128:256], qt[:, sl], kt[:, sl], start=True, stop=True)
        L.append(st1)

        def st2():
            P.copy(growS[g][:, :], gr[:, 0:128], eng=GE)
        L.append(st2)

        def st3():
            P.ts(Dn[g][:, :], growS[g][:, :], gc[:, c, hv:hv + 1], ALU.subtract, 0.0, ALU.max)
            P.act(Eg[g][:, :], growS[g][:, :], AF.Exp)
        L.append(st3)

        def st4():
            P.act(E[g][:, :], Dn[g][:, :], AF.Exp, scale=-1.0)
            P.tt(QgT[s2][:, :], qt[:, sl], Eg[g][:, :], ALU.mult, eng="pool")
        L.append(st4)

        def st5():
            P.stt(M0f[g][:, :], dd[:, 0:128], nbeta[:, c, hv:hv + 1], E[g][:, :], ALU.mult, ALU.mult)
            P.tt(Atf[g][:, :], dd[:, 128:256], E[g][:, :], ALU.mult)
        L.append(st5)

        def st6():
            P.tt(MN[g][:, 0:128], M0f[g][:, :], C.maskSL_f[:, :], ALU.mult, eng="pool")
            P.tt(Atb[g][:, :], Atf[g][:, :], C.maskLI_f[:, :], ALU.mult, eng="pool")
        L.append(st6)

        def st7():
            P.transpose(na[:, 0:128], MN[g][:, 0:128], C.ident_b[:, :])
            P.transpose(na[:, 128:256], Atb[g][:, :], C.ident_b[:, :])
        L.append(st7)

        def st8():
            P.copy(MN[g][:, 128:256], na[:, 0:128], eng=GE)
            P.copy(AtT[s2][:, :], na[:, 128:256], eng=GE)
        L.append(st8)
        for l in range(7):
            def sx(l=l):
                Tb = C.ident_b[:, :] if l == 0 else TTm[g][:, 0:128]
                Tbt = C.ident_b[:, :] if l == 0 else TTm[g][:, 128:256]
                P.mm(dd[:, 0:128], MN[g][:, 128:256], Tb, start=True, stop=True)
                P.mm(dd[:, 128:256], MN[g][:, 0:128], Tbt, start=True, stop=True)
            L.append(sx)

            def sm(l=l):
                P.tt(Xs[g][:, :], dd[:, 0:256], C.lvmask[:, l, :], ALU.mult)
            L.append(sm)

            def sy(l=l):
                Tb = C.ident_b[:, :] if l == 0 else TTm[g][:, 0:128]
                Tbt = C.ident_b[:, :] if l == 0 else TTm[g][:, 128:256]
                P.mm(aa[:, 0:128], Tbt, Xs[g][:, 0:128], start=True, stop=False)
                P.mm(aa[:, 0:128], C.ident_b[:, :], Tb, start=False, stop=True)
                P.mm(aa[:, 128:256], Tb, Xs[g][:, 128:256], start=True, stop=False)
                P.mm(aa[:, 128:256], C.ident_b[:, :], Tbt, start=False, stop=True)
            L.append(sy)

            def sc(l=l):
                P.copy(TTm[g][:, :], aa[:, 0:256], eng=GE)
            L.append(sc)

        def su():
            P.mm(dd[:, 0:256], TTm[g][:, 128:256], R[g][:, :], start=True, stop=True)
        L.append(su)

        def su2():
            P.copy(UW[s2][:, :], dd[:, 0:256], eng="dve")
        L.append(su2)

        def sw():
            P.transpose(na[:, 0:128], UW[s2][:, 128:256], C.ident_b[:, :])
        L.append(sw)

        def sw2():
            P.copy(wTb[s2][:, :], na[:, 0:128], eng=GE)
        L.append(sw2)
        return L

    def chain(hv, c, s2):
        sl = slice(c * 128, (c + 1) * 128)
        P.mm(pWS[:, 0:128], wTb[s2][:, :], Sbf[:, :], start=True, stop=True)
        P.tt(vnew[s2][:, :], UW[s2][:, 0:128], pWS[:, 0:128], ALU.subtract)
        P.mm(pOT[:, 0:128], Sbf[:, :], QgT[s2][:, :], start=True, stop=False)
        P.mm(pOT[:, 0:128], vnew[s2][:, :], AtT[s2][:, :], start=False, stop=True)
        P.copy(oh[:, sl], pOT[:, 0:128], eng=GE)
        P.mm(pSU[:, 0:128], Kdec[s2][:, :], vnew[s2][:, :], start=True, stop=True)
        P.stt(S32[:, :], S32[:, :], egl[:, c, hv:hv + 1], pSU[:, 0:128], ALU.mult, ALU.add)
        P.copy(Sbf[:, :], S32[:, :], eng="act")

    def load_head(hv):
        hp, hk = hv % 2, hv // 2
        P.dma(kT[hp][:, :], post[GK0 + hk * 128: GK0 + (hk + 1) * 128, :])
        P.dma(qT[hp][:, :], post[GQ0 + hk * 128: GQ0 + (hk + 1) * 128, :])
        P.dma(vT[hp][:, :], post[GV0 + hv * 128: GV0 + (hv + 1) * 128, :])

    def prep_group(gi):
        grp = groups[gi]
        lists = [stages(hv, c, g, (gi % 2) * G + g) for g, (hv, c) in enumerate(grp)]
        for si in range(len(lists[0])):
            for L in lists:
                L[si]()

    pairs = [(hv, c) for hv in range(32) for c in range(NCH)]
    groups = [pairs[i:i + G] for i in range(0, len(pairs), G)]
    n = 0
    for (nh, ncc) in groups[0]:
        if ncc == 0:
            load_head(nh)
    prep_group(0)
    for gi, grp in enumerate(groups):
        if gi + 1 < len(groups):
            for (nh, ncc) in groups[gi + 1]:
                if ncc == 0:
                    load_head(nh)
            prep_group(gi + 1)
        for g, (hv, c) in enumerate(grp):
            hp = hv % 2
            if c == 0:
                P.memset(S32[:, :], 0.0, eng="dve")
                P.memset(Sbf[:, :], 0.0, eng="dve")
                P.dma(zb[hp][:, :], zs[hv * 128:(hv + 1) * 128, :])
            chain(hv, c, (gi % 2) * G + g)
            if c == NCH - 1:
                ob_ = outb[hp]
                for ti in range(T // TT):
                    tsl = slice(ti * TT, (ti + 1) * TT)
                    sq, rs, tm = sqb[n % 2], rsb[n % 2], tmpb[n % 2]
                    n += 1
                    P.act(sq[:, :], oh[:, tsl], AF.Square)
                    P.mm(pNrm[:, 0:TT], C.ones_f[:, :], sq[:, :], start=True, stop=True)
                    P.act(lnb[:, :], pNrm[:, 0:TT], AF.Ln, bias=C.eps_col[:, :], scale=1.0 / 128)
                    P.act(rs[:, :], lnb[:, :], AF.Exp, scale=-0.5)
                    P.stt(tm[:, :], oh[:, tsl], gon[:, 0:1], rs[:, :], ALU.mult, ALU.mult)
                    P.tt(ob_[:, tsl], tm[:, :], zb[hp][:, tsl], ALU.mult, eng="pool")
                P.dma(oT[hv * 128:(hv + 1) * 128, :], ob_[:, :])
    P.end_phase()


W_NAMES = ["mix_norm", "ffn_norm", "ffn_w_gu", "ffn_w_down", "sb_w_in", "sb_q_norm", "sb_k_norm", "sb_w_out",
           "gdn_w_in", "gdn_conv_w", "gdn_a_log", "gdn_dt_bias", "gdn_o_norm", "gdn_w_out",
           "dsw_w_in", "dsw_q_norm", "dsw_k_norm", "dsw_w_out",
           "lru_w_in", "lru_conv_w", "lru_conv_b", "lru_w_a", "lru_b_a", "lru_w_x", "lru_b_x", "lru_lambda", "lru_w_out"]


def build(T, layers, shapes):
    nc = bass.Bass("TRN2", target_bir_lowering=False)
    es = ExitStack()
    es.__enter__()
    dr = {}
    dr["xT"] = nc.dram_tensor("xT", [D, T], F32, kind="ExternalInput").ap()
    dr["cf"] = nc.dram_tensor("cf", list(shapes["cf"]), F32, kind="ExternalInput").ap()
    for nm in W_NAMES:
        dr[nm] = nc.dram_tensor(nm, list(shapes[nm]), F32, kind="ExternalInput").ap()
    outT = nc.dram_tensor("outT", [D, T], F32, kind="ExternalOutput").ap()
    hT = nc.dram_tensor("hT", [D, T], BF16, kind="Internal").ap()
    dr["pos"] = nc.dram_tensor("pos", [1, T], I32, kind="ExternalInput").ap()
    s_q = nc.dram_tensor("s_q", [4096, T], BF16, kind="Internal").ap()
    s_k = nc.dram_tensor("s_k", [4096, T], BF16, kind="Internal").ap()
    s_v = nc.dram_tensor("s_v", [T, 4096], BF16, kind="Internal").ap()
    s_o = nc.dram_tensor("s_o", [4096, T], BF16, kind=("ExternalOutput" if DBG_OUT else "Internal")).ap()
    dk = "ExternalOutput" if DBG_OUT else "Internal"
    s_pre = nc.dram_tensor("s_pre", [8192, T], BF16, kind="Internal").ap()
    s_post = nc.dram_tensor("s_post", [8192, T], BF16, kind=dk).ap()
    s_bt = nc.dram_tensor("s_bt", [T, 32], F32, kind=dk).ap()
    s_g = nc.dram_tensor("s_g", [T, 32], F32, kind=dk).ap()
    s_ct = nc.dram_tensor("s_ct", [128, T], F32, kind="Internal").ap()
    s_st = nc.dram_tensor("s_st", [128, T], F32, kind="Internal").ap()
    s_x = nc.dram_tensor("s_x", [2048, T], F32, kind="Internal").ap()
    P = Prog(nc, es)
    C = setup_consts(P, nc, es, dr)
    P.begin_phase()
    cb = [P.sb([128, min(T, 2048)], F32, "cp") for _ in range(2)]
    TC = min(T, 2048)
    i = 0
    for c in range(KC):
        for t0 in range(0, T, TC):
            b = cb[i % 2]
            i += 1
            P.dma(b[:, :], dr["xT"][c * 128:(c + 1) * 128, t0:t0 + TC])
            P.dma(outT[c * 128:(c + 1) * 128, t0:t0 + TC], b[:, :])
    P.end_phase()
    x = outT
    for li in layers:
        kind = li % 4
        phase_norm(P, C, x, dr["mix_norm"][li], hT, T)
        if kind == 0:
            phase_sb_proj(P, C, hT, dr["sb_w_in"][0], dr["sb_q_norm"][0], dr["sb_k_norm"][0], s_q, s_k, s_v, T)
            phase_sb_attn(P, C, s_q, s_k, s_v, s_o, T)
            phase_outproj(P, C, x, s_o, 16, dr["sb_w_out"][0], T)
        elif kind == 1:
            phase_gdn_proj(P, C, hT, dr["gdn_w_in"][0], dr["gdn_a_log"][0], dr["gdn_dt_bias"][0], s_pre, s_q, s_bt, s_g, T)
            if DBG_STOP >= 2:
                phase_gdn_conv(P, C, s_pre, dr["gdn_conv_w"][0], s_post, T)
            if DBG_STOP >= 3:
                phase_gdn_core(P, C, s_post, s_q, s_bt, s_g, dr["gdn_o_norm"][0], s_o, T)
            if DBG_STOP >= 4:
                phase_outproj(P, C, x, s_o, 32, dr["gdn_w_out"][0], T)
        elif kind == 2:
            phase_rope_tables(P, C, dr["pos"], s_ct, s_st, T)
            phase_dsw_proj(P, C, hT, dr["dsw_w_in"][0], dr["dsw_q_norm"][0], dr["dsw_k_norm"][0], s_ct, s_st, s_q, s_k, s_v, T)
            phase_dsw_attn(P, C, s_q, s_k, s_v, s_o, T)
            phase_outproj(P, C, x, s_o, 18, dr["dsw_w_out"][0], T)
        elif kind == 3:
            phase_lru_proj(P, C, hT, dr["lru_w_in"][0], s_q, s_x, T)
            phase_lru_rec(P, C, s_q, s_x, s_o, dr["lru_conv_w"][0], dr["lru_conv_b"][0], dr["lru_w_a"][0], dr["lru_b_a"][0],
                          dr["lru_w_x"][0], dr["lru_b_x"][0], dr["lru_lambda"][0], T)
            phase_outproj(P, C, x, s_o, 16, dr["lru_w_out"][0], T)
        if not DBG_SKIP_FFN:
            phase_norm(P, C, x, dr["ffn_norm"][li], hT, T)
            phase_ffn(P, C, x, hT, dr["ffn_w_gu"][li], dr["ffn_w_down"][li], T)
    es.__exit__(None, None, None)
    return nc, P


def kernel(**inputs):
    T = inputs["x"].shape[1]
    B = inputs["x"].shape[0]
    consts = host_consts()
    shapes = {k: np.shape(inputs[k]) for k in W_NAMES}
    shapes["cf"] = consts["cf"].shape
    nc, P = build(T, list(range(4)), shapes)
    in_maps = []
    for b in range(B):
        m = {"xT": np.ascontiguousarray(inputs["x"][b].T), "cf": consts["cf"],
             "pos": np.ascontiguousarray(inputs["positions"][b:b + 1]).astype(np.int32)}
        for nm in W_NAMES:
            m[nm] = np.ascontiguousarray(inputs[nm], dtype=np.float32)
        in_maps.append(m)
    res = run_bass_kernel_spmd(nc, in_maps, core_ids=list(range(B)))
    out = np.stack([np.ascontiguousarray(r["outT"].T) for r in res.results], axis=0)
    return out.astype(np.float32)
```
